# Optimizing a Trainium2 kernel written in Bass

```python
import math
import jax, jax.numpy as jnp
from jax import lax
import numpy as np

D_MODEL = 1024
BATCH = 8
SEQ = 4096
DEPTH = 4

N_MIXERS = 3
N_A = len(range(0, DEPTH, N_MIXERS))
N_B = len(range(1, DEPTH, N_MIXERS))
N_C = len(range(2, DEPTH, N_MIXERS))
RMS_EPS = 1e-6
D_FF = 2816
HEAD_DIM = 64
ROPE_DIM = HEAD_DIM // 4
ROPE_THETA = 500000.0
ATTN_SCALE = HEAD_DIM ** -0.5
MASK_NEG = -1e30

SSM_D_INNER = 2 * D_MODEL
SSM_HEAD_DIM = 64
SSM_HEADS = SSM_D_INNER // SSM_HEAD_DIM
SSM_GROUPS = 8
SSM_STATE = 128
SSM_CONV = 4
SSM_CHUNK = 128
SSM_CONV_DIM = SSM_D_INNER + 2 * SSM_GROUPS * SSM_STATE
SSM_IN_DIM = SSM_D_INNER + SSM_CONV_DIM + SSM_HEADS

SWA_HEADS = D_MODEL // HEAD_DIM
SWA_KV_HEADS = 4
SWA_WINDOW = 128
SWA_BLOCK = 128
SWA_QKV_DIM = (SWA_HEADS + 2 * SWA_KV_HEADS) * HEAD_DIM

NSA_HEADS = D_MODEL // HEAD_DIM
NSA_KV_HEADS = 4
NSA_CMP_LEN = 32
NSA_CMP_STRIDE = 16
NSA_CMP_HIDDEN = 256
NSA_SEL_BLOCK = 64
NSA_TOP_N = 8
NSA_WINDOW = 512
NSA_Q_BLOCK = 64
NSA_IN_DIM = NSA_HEADS * HEAD_DIM + 6 * NSA_KV_HEADS * HEAD_DIM + 3 * NSA_HEADS

kernel_name = 'hybrid_ssd_swa_nsa_macaron'


def rms_norm(x, w):
    xf = x.astype(jnp.float32)
    y = xf * lax.rsqrt(jnp.mean(xf * xf, axis=-1, keepdims=True) + RMS_EPS)
    return (y * w.astype(jnp.float32)).astype(x.dtype)


def swiglu(x, w_in, w_out):
    gate, up = jnp.split(x @ w_in, 2, axis=-1)
    return (jax.nn.silu(gate) * up) @ w_out


def rope_tables(positions):
    inv_freq = ROPE_THETA ** (-jnp.arange(0, ROPE_DIM, 2, dtype=jnp.float32) / ROPE_DIM)
    ang = positions.astype(jnp.float32)[..., None] * inv_freq
    return jnp.cos(ang), jnp.sin(ang)


def apply_rope(x, cos, sin):
    half = ROPE_DIM // 2
    c = cos[:, :, None, :].astype(x.dtype)
    s = sin[:, :, None, :].astype(x.dtype)
    x1 = x[..., :half]
    x2 = x[..., half:ROPE_DIM]
    return jnp.concatenate([x1 * c - x2 * s, x2 * c + x1 * s, x[..., ROPE_DIM:]], axis=-1)


def masked_softmax(logits, mask):
    p = jax.nn.softmax(jnp.where(mask, logits, MASK_NEG), axis=-1)
    return jnp.where(mask, p, 0.0)


def ssd_scan(x, dt, a, bm, cm):
    b, s, h, p = x.shape
    g, n = bm.shape[2], bm.shape[3]
    r = h // g
    q = SSM_CHUNK
    c = s // q
    xd = (x.astype(jnp.float32) * dt[..., None]).reshape(b, c, q, g, r, p)
    ad = (dt * a).reshape(b, c, q, g, r).transpose(0, 3, 4, 1, 2)
    bc = bm.astype(jnp.float32).reshape(b, c, q, g, n)
    cc = cm.astype(jnp.float32).reshape(b, c, q, g, n)
    a_cum = jnp.cumsum(ad, axis=-1)
    causal = jnp.tril(jnp.ones((q, q), dtype=bool))
    diff = a_cum[..., :, None] - a_cum[..., None, :]
    decay = jnp.exp(jnp.where(causal, diff, -jnp.inf))
    cb = jnp.einsum('bclgn,bcsgn->bgcls', cc, bc)
    y_diag = jnp.einsum('bgcls,bgrcls,bcsgrp->bclgrp', cb, decay, xd)
    decay_states = jnp.exp(a_cum[..., -1:] - a_cum)
    states = jnp.einsum('bcsgn,bgrcs,bcsgrp->bcgrpn', bc, decay_states, xd)
    chunk_decay = jnp.exp(a_cum[..., -1])

    def step(carry, inp):
        dec, st = inp
        return carry * dec[..., None, None] + st, carry

    init = jnp.zeros((b, g, r, p, n), jnp.float32)
    _, prev = lax.scan(step, init, (jnp.moveaxis(chunk_decay, -1, 0), jnp.moveaxis(states, 1, 0)))
    prev = jnp.moveaxis(prev, 0, 1)
    y_off = jnp.einsum('bclgn,bcgrpn,bgrcl->bclgrp', cc, prev, jnp.exp(a_cum))
    return (y_diag + y_off).reshape(b, s, h, p)


def mamba2_mixer(u, w_in, conv_w, conv_b, dt_bias, a_log, d_skip, norm_w, w_out):
    b, s, _ = u.shape
    zxbcdt = u @ w_in
    z, xbc, dt = jnp.split(zxbcdt, [SSM_D_INNER, SSM_D_INNER + SSM_CONV_DIM], axis=-1)
    xbc = lax.conv_general_dilated(xbc, conv_w[:, None, :], window_strides=(1,),
                                   padding=[(SSM_CONV - 1, 0)],
                                   dimension_numbers=('NWC', 'WIO', 'NWC'),
                                   feature_group_count=SSM_CONV_DIM) + conv_b
    xbc = jax.nn.silu(xbc)
    xs, bm, cm = jnp.split(xbc, [SSM_D_INNER, SSM_D_INNER + SSM_GROUPS * SSM_STATE], axis=-1)
    xs = xs.reshape(b, s, SSM_HEADS, SSM_HEAD_DIM)
    bm = bm.reshape(b, s, SSM_GROUPS, SSM_STATE)
    cm = cm.reshape(b, s, SSM_GROUPS, SSM_STATE)
    dt = jax.nn.softplus(dt.astype(jnp.float32) + dt_bias.astype(jnp.float32))
    a = -jnp.exp(a_log.astype(jnp.float32))
    y = ssd_scan(xs, dt, a, bm, cm)
    y = y + d_skip.astype(jnp.float32)[:, None] * xs.astype(jnp.float32)
    y = y.reshape(b, s, SSM_D_INNER)
    gated = y * jax.nn.silu(z.astype(jnp.float32))
    gg = gated.reshape(b, s, SSM_GROUPS, SSM_D_INNER // SSM_GROUPS)
    gg = gg * lax.rsqrt(jnp.mean(gg * gg, axis=-1, keepdims=True) + RMS_EPS)
    out = (gg.reshape(b, s, SSM_D_INNER) * norm_w.astype(jnp.float32)).astype(u.dtype)
    return out @ w_out


def swa_sink_attention(u, w_qkv, b_qkv, sinks, w_o, b_o, cos, sin):
    b, s, _ = u.shape
    H, KV = SWA_HEADS, SWA_KV_HEADS
    R = H // KV
    qkv = u @ w_qkv + b_qkv
    q, k, v = jnp.split(qkv, [H * HEAD_DIM, (H + KV) * HEAD_DIM], axis=-1)
    q = apply_rope(q.reshape(b, s, H, HEAD_DIM), cos, sin).reshape(b, s, KV, R, HEAD_DIM)
    k = apply_rope(k.reshape(b, s, KV, HEAD_DIM), cos, sin)
    v = v.reshape(b, s, KV, HEAD_DIM)
    kp = jnp.pad(k, ((0, 0), (SWA_WINDOW, 0), (0, 0), (0, 0)))
    vp = jnp.pad(v, ((0, 0), (SWA_WINDOW, 0), (0, 0), (0, 0)))
    span = SWA_BLOCK + SWA_WINDOW
    sink = sinks.astype(jnp.float32).reshape(1, KV, R, 1, 1)

    def block(i):
        t0 = i * SWA_BLOCK
        qb = lax.dynamic_slice_in_dim(q, t0, SWA_BLOCK, axis=1)
        kb = lax.dynamic_slice_in_dim(kp, t0, span, axis=1)
        vb = lax.dynamic_slice_in_dim(vp, t0, span, axis=1)
        qpos = t0 + jnp.arange(SWA_BLOCK)
        kpos = t0 - SWA_WINDOW + jnp.arange(span)
        delta = qpos[:, None] - kpos[None, :]
        mask = (delta >= 0) & (delta < SWA_WINDOW) & (kpos[None, :] >= 0)
        logits = jnp.einsum('bqhrd,bkhd->bhrqk', qb, kb).astype(jnp.float32) * ATTN_SCALE
        logits = jnp.where(mask, logits, -jnp.inf)
        sink_col = jnp.broadcast_to(sink, logits.shape[:-1] + (1,))
        probs = jax.nn.softmax(jnp.concatenate([logits, sink_col], axis=-1), axis=-1)[..., :-1]
        return jnp.einsum('bhrqk,bkhd->bqhrd', probs.astype(vb.dtype), vb)

    outs = lax.map(block, jnp.arange(s // SWA_BLOCK))
    o = jnp.moveaxis(outs, 0, 1).reshape(b, s, H * HEAD_DIM)
    return o @ w_o + b_o


def compress_blocks(blk, pe, w1, w2):
    bb, nn, ll, kv, dd = blk.shape
    z = (blk + pe[:, None, :]).transpose(0, 1, 3, 2, 4).reshape(bb, nn, kv, ll * dd)
    return jax.nn.silu(z @ w1) @ w2


def nsa_attention(u, w_in, pe_k, k_w1, k_w2, pe_v, v_w1, v_w2, w_o, cos, sin):
    b, s, _ = u.shape
    H, KV = NSA_HEADS, NSA_KV_HEADS
    R = H // KV
    qd, kvd = H * HEAD_DIM, KV * HEAD_DIM
    splits = [int(t) for t in np.cumsum([qd] + [kvd] * 6)]
    q, kc, vc, ks, vs, kw, vw, g = jnp.split(u @ w_in, splits, axis=-1)
    q = apply_rope(q.reshape(b, s, H, HEAD_DIM), cos, sin).reshape(b, s, KV, R, HEAD_DIM)
    kc = apply_rope(kc.reshape(b, s, KV, HEAD_DIM), cos, sin)
    ks = apply_rope(ks.reshape(b, s, KV, HEAD_DIM), cos, sin)
    kw = apply_rope(kw.reshape(b, s, KV, HEAD_DIM), cos, sin)
    vc = vc.reshape(b, s, KV, HEAD_DIM)
    vs = vs.reshape(b, s, KV, HEAD_DIM)
    vw = vw.reshape(b, s, KV, HEAD_DIM)
    gates = jax.nn.sigmoid(g.astype(jnp.float32)).astype(u.dtype).reshape(b, s, KV, R, 3)

    n_cmp = (s - NSA_CMP_LEN) // NSA_CMP_STRIDE + 1
    idx = np.arange(n_cmp)[:, None] * NSA_CMP_STRIDE + np.arange(NSA_CMP_LEN)[None, :]
    k_cmp = compress_blocks(kc[:, idx], pe_k, k_w1, k_w2)
    v_cmp = compress_blocks(vc[:, idx], pe_v, v_w1, v_w2)
    cmp_end = jnp.asarray(idx[:, -1].astype(np.int32))

    n_sel = s // NSA_SEL_BLOCK
    top_n = min(NSA_TOP_N, n_sel)
    sel_lo = np.arange(n_sel)[:, None] * NSA_SEL_BLOCK
    cmp_lo = np.arange(n_cmp)[None, :] * NSA_CMP_STRIDE
    ov = np.clip(np.minimum(sel_lo + NSA_SEL_BLOCK, cmp_lo + NSA_CMP_LEN) - np.maximum(sel_lo, cmp_lo), 0, None)
    overlap = jnp.asarray((ov / NSA_CMP_LEN).astype(np.float32))
    ks_blk = ks.reshape(b, n_sel, NSA_SEL_BLOCK, KV, HEAD_DIM).transpose(0, 3, 1, 2, 4)
    vs_blk = vs.reshape(b, n_sel, NSA_SEL_BLOCK, KV, HEAD_DIM).transpose(0, 3, 1, 2, 4)
    sel_ids = jnp.arange(n_sel)
    bi = jnp.arange(b)[:, None, None, None]
    hi = jnp.arange(KV)[None, :, None, None]

    kwp = jnp.pad(kw, ((0, 0), (NSA_WINDOW, 0), (0, 0), (0, 0)))
    vwp = jnp.pad(vw, ((0, 0), (NSA_WINDOW, 0), (0, 0), (0, 0)))
    span = NSA_Q_BLOCK + NSA_WINDOW

    def block(i):
        t0 = i * NSA_Q_BLOCK
        qb = lax.dynamic_slice_in_dim(q, t0, NSA_Q_BLOCK, axis=1)
        qpos = t0 + jnp.arange(NSA_Q_BLOCK)
        mc = cmp_end[None, :] <= qpos[:, None]
        lc = jnp.einsum('bqhrd,bchd->bhrqc', qb, k_cmp).astype(jnp.float32) * ATTN_SCALE
        pc = masked_softmax(lc, mc)
        o_cmp = jnp.einsum('bhrqc,bchd->bqhrd', pc.astype(v_cmp.dtype), v_cmp)
        imp = jnp.einsum('bhrqc,jc->bhqj', pc, overlap)
        cur = qpos // NSA_SEL_BLOCK
        forced = (sel_ids[None, :] == cur[:, None]) | (sel_ids[None, :] == 0)
        imp = jnp.where(forced, jnp.inf, imp)
        imp = jnp.where(sel_ids[None, :] * NSA_SEL_BLOCK <= qpos[:, None], imp, -jnp.inf)
        _, top = lax.top_k(imp, top_n)
        kg = ks_blk[bi, hi, top]
        vg = vs_blk[bi, hi, top]
        kpos = top[..., None] * NSA_SEL_BLOCK + jnp.arange(NSA_SEL_BLOCK)
        ms = (kpos <= qpos[None, None, :, None, None])[:, :, None]
        ls = jnp.einsum('bqhrd,bhqnkd->bhrqnk', qb, kg).astype(jnp.float32) * ATTN_SCALE
        ls = jnp.where(ms, ls, -jnp.inf)
        shp = ls.shape
        ps = jax.nn.softmax(ls.reshape(shp[:4] + (shp[4] * shp[5],)), axis=-1).reshape(shp)
        o_sel = jnp.einsum('bhrqnk,bhqnkd->bqhrd', ps.astype(vg.dtype), vg)
        kb = lax.dynamic_slice_in_dim(kwp, t0, span, axis=1)
        vb = lax.dynamic_slice_in_dim(vwp, t0, span, axis=1)
        wpos = t0 - NSA_WINDOW + jnp.arange(span)
        delta = qpos[:, None] - wpos[None, :]
        mw = (delta >= 0) & (delta < NSA_WINDOW) & (wpos[None, :] >= 0)
        lw = jnp.einsum('bqhrd,bkhd->bhrqk', qb, kb).astype(jnp.float32) * ATTN_SCALE
        pw = masked_softmax(lw, mw)
        o_win = jnp.einsum('bhrqk,bkhd->bqhrd', pw.astype(vb.dtype), vb)
        gb = lax.dynamic_slice_in_dim(gates, t0, NSA_Q_BLOCK, axis=1)
        return gb[..., 0:1] * o_cmp + gb[..., 1:2] * o_sel + gb[..., 2:3] * o_win

    outs = lax.map(block, jnp.arange(s // NSA_Q_BLOCK))
    o = jnp.moveaxis(outs, 0, 1).reshape(b, s, H * HEAD_DIM)
    return o @ w_o


def setup_inputs(seed: int = 0) -> dict:
    key = jax.random.key(seed)
    keys = iter(jax.random.split(key, 48))
    f32 = jnp.float32

    def nrm(shape, scale):
        return jax.random.normal(next(keys), shape, f32) * scale

    x = nrm((BATCH, SEQ, D_MODEL), 1.0)
    positions = jnp.broadcast_to(jnp.arange(SEQ, dtype=jnp.int32), (BATCH, SEQ))
    ln_ffn1 = 1.0 + nrm((DEPTH, D_MODEL), 0.01)
    ffn1_w_in = nrm((DEPTH, D_MODEL, 2 * D_FF), D_MODEL ** -0.5)
    ffn1_w_out = nrm((DEPTH, D_FF, D_MODEL), D_FF ** -0.5)
    ln_mix = 1.0 + nrm((DEPTH, D_MODEL), 0.01)
    ln_ffn2 = 1.0 + nrm((DEPTH, D_MODEL), 0.01)
    ffn2_w_in = nrm((DEPTH, D_MODEL, 2 * D_FF), D_MODEL ** -0.5)
    ffn2_w_out = nrm((DEPTH, D_FF, D_MODEL), D_FF ** -0.5)
    ssm_w_in = nrm((N_A, D_MODEL, SSM_IN_DIM), D_MODEL ** -0.5)
    ssm_conv_w = nrm((N_A, SSM_CONV, SSM_CONV_DIM), SSM_CONV ** -0.5)
    ssm_conv_b = nrm((N_A, SSM_CONV_DIM), 0.01)
    dt0 = jnp.exp(jax.random.uniform(next(keys), (N_A, SSM_HEADS), f32, math.log(1e-3), math.log(1e-1)))
    ssm_dt_bias = dt0 + jnp.log(-jnp.expm1(-dt0))
    ssm_a_log = jnp.log(jax.random.uniform(next(keys), (N_A, SSM_HEADS), f32, 1.0, 16.0))
    ssm_d = 1.0 + nrm((N_A, SSM_HEADS), 0.1)
    ssm_norm_w = 1.0 + nrm((N_A, SSM_D_INNER), 0.01)
    ssm_w_out = nrm((N_A, SSM_D_INNER, D_MODEL), SSM_D_INNER ** -0.5)
    swa_w_qkv = nrm((N_B, D_MODEL, SWA_QKV_DIM), D_MODEL ** -0.5)
    swa_b_qkv = nrm((N_B, SWA_QKV_DIM), 0.01)
    swa_sinks = nrm((N_B, SWA_HEADS), 0.5)
    swa_w_o = nrm((N_B, SWA_HEADS * HEAD_DIM, D_MODEL), (SWA_HEADS * HEAD_DIM) ** -0.5)
    swa_b_o = nrm((N_B, D_MODEL), 0.01)
    nsa_w_in = nrm((N_C, D_MODEL, NSA_IN_DIM), D_MODEL ** -0.5)
    nsa_pe_k = nrm((N_C, NSA_CMP_LEN, HEAD_DIM), 0.02)
    nsa_k_w1 = nrm((N_C, NSA_CMP_LEN * HEAD_DIM, NSA_CMP_HIDDEN), (NSA_CMP_LEN * HEAD_DIM) ** -0.5)
    nsa_k_w2 = nrm((N_C, NSA_CMP_HIDDEN, HEAD_DIM), NSA_CMP_HIDDEN ** -0.5)
    nsa_pe_v = nrm((N_C, NSA_CMP_LEN, HEAD_DIM), 0.02)
    nsa_v_w1 = nrm((N_C, NSA_CMP_LEN * HEAD_DIM, NSA_CMP_HIDDEN), (NSA_CMP_LEN * HEAD_DIM) ** -0.5)
    nsa_v_w2 = nrm((N_C, NSA_CMP_HIDDEN, HEAD_DIM), NSA_CMP_HIDDEN ** -0.5)
    nsa_w_o = nrm((N_C, NSA_HEADS * HEAD_DIM, D_MODEL), (NSA_HEADS * HEAD_DIM) ** -0.5)
    final_norm = 1.0 + nrm((D_MODEL,), 0.01)
    return {'x': x, 'positions': positions,
            'ln_ffn1': ln_ffn1, 'ffn1_w_in': ffn1_w_in, 'ffn1_w_out': ffn1_w_out,
            'ln_mix': ln_mix, 'ln_ffn2': ln_ffn2, 'ffn2_w_in': ffn2_w_in, 'ffn2_w_out': ffn2_w_out,
            'ssm_w_in': ssm_w_in, 'ssm_conv_w': ssm_conv_w, 'ssm_conv_b': ssm_conv_b,
            'ssm_dt_bias': ssm_dt_bias, 'ssm_a_log': ssm_a_log, 'ssm_d': ssm_d,
            'ssm_norm_w': ssm_norm_w, 'ssm_w_out': ssm_w_out,
            'swa_w_qkv': swa_w_qkv, 'swa_b_qkv': swa_b_qkv, 'swa_sinks': swa_sinks,
            'swa_w_o': swa_w_o, 'swa_b_o': swa_b_o,
            'nsa_w_in': nsa_w_in, 'nsa_pe_k': nsa_pe_k, 'nsa_k_w1': nsa_k_w1, 'nsa_k_w2': nsa_k_w2,
            'nsa_pe_v': nsa_pe_v, 'nsa_v_w1': nsa_v_w1, 'nsa_v_w2': nsa_v_w2, 'nsa_w_o': nsa_w_o,
            'final_norm': final_norm}


def reference(x, positions, ln_ffn1, ffn1_w_in, ffn1_w_out, ln_mix, ln_ffn2, ffn2_w_in, ffn2_w_out,
              ssm_w_in, ssm_conv_w, ssm_conv_b, ssm_dt_bias, ssm_a_log, ssm_d, ssm_norm_w, ssm_w_out,
              swa_w_qkv, swa_b_qkv, swa_sinks, swa_w_o, swa_b_o,
              nsa_w_in, nsa_pe_k, nsa_k_w1, nsa_k_w2, nsa_pe_v, nsa_v_w1, nsa_v_w2, nsa_w_o,
              final_norm):
    cos, sin = rope_tables(positions)
    h = x
    for i in range(DEPTH):
        kind = i % N_MIXERS
        inst = i // N_MIXERS
        h = h + 0.5 * swiglu(rms_norm(h, ln_ffn1[i]), ffn1_w_in[i], ffn1_w_out[i])
        u = rms_norm(h, ln_mix[i])
        if kind == 0:
            m = mamba2_mixer(u, ssm_w_in[inst], ssm_conv_w[inst], ssm_conv_b[inst], ssm_dt_bias[inst],
                             ssm_a_log[inst], ssm_d[inst], ssm_norm_w[inst], ssm_w_out[inst])
        elif kind == 1:
            m = swa_sink_attention(u, swa_w_qkv[inst], swa_b_qkv[inst], swa_sinks[inst],
                                   swa_w_o[inst], swa_b_o[inst], cos, sin)
        else:
            m = nsa_attention(u, nsa_w_in[inst], nsa_pe_k[inst], nsa_k_w1[inst], nsa_k_w2[inst],
                              nsa_pe_v[inst], nsa_v_w1[inst], nsa_v_w2[inst], nsa_w_o[inst], cos, sin)
        h = h + m
        h = h + 0.5 * swiglu(rms_norm(h, ln_ffn2[i]), ffn2_w_in[i], ffn2_w_out[i])
    return rms_norm(h, final_norm)
```

```python
import numpy as np
from contextlib import ExitStack, contextmanager
import concourse.bass as bass
import concourse.mybir as mybir
from concourse.bass_utils import run_bass_kernel_spmd

F32 = mybir.dt.float32
BF16 = mybir.dt.bfloat16
I32 = mybir.dt.int32
AF = mybir.ActivationFunctionType
ALU = mybir.AluOpType
AX = mybir.AxisListType

SAME_ENG_RAW_SYNC = True


class Res:
    __slots__ = ("w", "r", "name")

    def __init__(self, name=""):
        self.w = {}
        self.r = {}
        self.name = name


class Tile:
    def __init__(self, t, name, nparts=0):
        self.t = t
        self.res = Res(name)
        self.parts = [Res(f"{name}.{i}") for i in range(nparts)]

    def __getitem__(self, idx):
        return self.t[idx]


def _res(x):
    if isinstance(x, Res):
        return [x]
    if isinstance(x, Tile):
        return [x.res] + x.parts
    raise TypeError(x)


class KB:
    ENGS = ("pe", "act", "dve", "pool", "sp")

    def __init__(self, nc, nslot=12):
        self.nc = nc
        self.stack = ExitStack()
        self.eng = {"pe": nc.tensor, "act": nc.scalar, "dve": nc.vector, "pool": nc.gpsimd, "sp": nc.sync}
        self.sem = {k: self.stack.enter_context(nc.semaphore("s_" + k)) for k in self.ENGS}
        self.cnt = {k: 0 for k in self.ENGS}
        self.seen = {k: {} for k in self.ENGS}
        self.nslot = nslot
        self.dq = ("sp", "pool", "act")
        self.dsem = {q: [self.stack.enter_context(nc.semaphore(f"d_{q}{i}")) for i in range(nslot)] for q in self.dq}
        self.dcnt = {q: [0] * nslot for q in self.dq}
        self.dnext = {q: 0 for q in self.dq}
        self.pstack = None
        self.uid = 0
        self.ninst = 0

    @contextmanager
    def phase(self, name=""):
        old = self.pstack
        self.pstack = ExitStack()
        self.nphase = getattr(self, "nphase", 0) + 1
        scope = self.nc.named_scope(f"ph{self.nphase:02d}_{name}")
        scope.__enter__()
        try:
            yield
        finally:
            self.barrier()
            scope.__exit__(None, None, None)
            self.pstack.close()
            self.pstack = old

    def _nm(self, name):
        self.uid += 1
        return f"{name}_{self.uid}"

    def sb(self, name, shape, dt, nparts=0, glob=False):
        nm = self._nm(name)
        st = self.stack if (glob or self.pstack is None) else self.pstack
        t = st.enter_context(self.nc.sbuf_tensor(nm, list(shape), dt))
        return Tile(t, nm, nparts)

    def ps(self, name, shape, dt=F32, nparts=0, glob=False):
        nm = self._nm(name)
        st = self.stack if (glob or self.pstack is None) else self.pstack
        t = st.enter_context(self.nc.psum_tensor(nm, list(shape), dt))
        return Tile(t, nm, nparts)

    def _semof(self, key):
        return self.sem[key] if isinstance(key, str) else self.dsem[key[1]][key[2]]

    def _wait(self, e, deps):
        need = {}
        for key, c in deps:
            if c <= 0:
                continue
            if c <= self.seen[e].get(key, 0):
                continue
            if need.get(key, 0) < c:
                need[key] = c
        for key, c in need.items():
            self.eng[e].wait_ge(self._semof(key), c)
            self.seen[e][key] = c
            self.ninst += 1

    def _deps(self, e, reads, writes, strict=False):
        deps = []
        for x in reads:
            for r in _res(x):
                for key, c in r.w.items():
                    if key == e and not strict:
                        if e == "pe" or not SAME_ENG_RAW_SYNC:
                            continue
                    deps.append((key, c))
        for x in writes:
            for r in _res(x):
                for key, c in list(r.w.items()) + list(r.r.items()):
                    if key == e and not strict:
                        if e == "pe" or not SAME_ENG_RAW_SYNC:
                            continue
                    deps.append((key, c))
        return deps

    def _mark(self, key, c, reads, writes):
        for x in reads:
            for r in _res(x):
                r.r[key] = c
        for x in writes:
            for r in _res(x):
                r.w[key] = c

    def op(self, e, fn, reads=(), writes=(), sig=True):
        self._wait(e, self._deps(e, reads, writes))
        ins = fn(self.eng[e])
        self.ninst += 1
        if sig:
            self.cnt[e] += 1
            ins.then_inc(self.sem[e], 1)
            c = self.cnt[e]
        else:
            c = self.cnt[e] + 1
        self._mark(e, c, reads, writes)
        return ins

    def mm(self, out, lhsT, rhs, start, stop, reads=(), writes=(), sig=None, **kw):
        if sig is None:
            sig = stop
        return self.op("pe", lambda pe: pe.matmul(out, lhsT=lhsT, rhs=rhs, start=start, stop=stop, **kw),
                       reads, writes, sig=sig)

    def tr(self, out, in_, ident, reads=(), writes=(), sig=True):
        return self.op("pe", lambda pe: pe.transpose(out, in_, ident), reads, writes, sig=sig)

    def dma(self, q, out, in_, reads=(), writes=(), **kw):
        i = self.dnext[q]
        self.dnext[q] = (i + 1) % self.nslot
        key = ("d", q, i)
        deps = self._deps(q, reads, writes, strict=True) + [(key, self.dcnt[q][i])]
        self._wait(q, deps)
        ins = self.eng[q].dma_start(out=out, in_=in_, **kw)
        self.ninst += 1
        self.dcnt[q][i] += 16
        ins.then_inc(self.dsem[q][i], 16)
        self._mark(key, self.dcnt[q][i], reads, writes)
        return ins

    def barrier(self, engines=None):
        allk = [(k, self.cnt[k]) for k in self.ENGS]
        for q in self.dq:
            for i in range(self.nslot):
                allk.append((("d", q, i), self.dcnt[q][i]))
        for e in (engines or self.ENGS):
            self._wait(e, [(k, c) for k, c in allk if k != e])

    def close(self):
        self.stack.close()
S = 4096
D = 1024
DFF = 2816
EPS = 1e-6
NCH = S // 128


def hview(hT, t0, n):
    return hT[:, t0:t0 + n].rearrange("(k p) t -> p k t", p=128)


def setup_consts(kb, cin):
    C = {}
    C["ident_f"] = kb.sb("ident_f", [128, 128], F32, glob=True)
    kb.dma("sp", C["ident_f"][:, :], cin["c_ident"][:, :], writes=[C["ident_f"]])
    C["ident_b"] = kb.sb("ident_b", [128, 128], BF16, glob=True)
    kb.op("dve", lambda v: v.tensor_copy(out=C["ident_b"][:, :], in_=C["ident_f"][:, :]),
          reads=[C["ident_f"]], writes=[C["ident_b"]])
    C["ones_b"] = kb.sb("ones_b", [128, 128], BF16, glob=True)
    kb.op("dve", lambda v: v.memset(C["ones_b"][:, :], 1.0), writes=[C["ones_b"]])
    C["ones_f"] = kb.sb("ones_f", [128, 128], F32, glob=True)
    kb.op("dve", lambda v: v.memset(C["ones_f"][:, :], 1.0), writes=[C["ones_f"]])
    C["eps"] = kb.sb("eps", [128, 1], F32, glob=True)
    kb.op("dve", lambda v: v.memset(C["eps"][:, :], EPS), writes=[C["eps"]])
    return C


def load_vec_pk(kb, name, ap1d, nk):
    t = kb.sb(name, [128, nk], F32)
    kb.dma("sp", t[:, :], ap1d.rearrange("(k p) -> p k", p=128), writes=[t], allow_slow_non_contiguous=True)
    return t


class NormBufs:
    def __init__(self, kb, n=512, ss=None, off=0):
        self.sq = [kb.sb("sq", [128, n], BF16) for _ in range(2)]
        self.ss_t = ss if ss is not None else kb.ps("ss", [128, n], F32)
        self.off = off
        self.rstd = kb.sb("rstd", [128, n], F32)
        self.i = 0


def emit_norm(kb, C, nb, h, lnw, xn, n, hoff=0, xoff=0):
    hs = slice(hoff, hoff + n)
    xs = slice(xoff, xoff + n)
    for k in range(8):
        sq = nb.sq[nb.i % 2]
        nb.i += 1
        kb.op("act", lambda a: a.activation(out=sq[:, :n], in_=h[:, k, hs], func=AF.Square), reads=[h], writes=[sq])
        kb.mm(nb.ss_t[:, nb.off:nb.off + n], C["ones_b"][:, :], sq[:, :n], start=(k == 0), stop=(k == 7),
              reads=[sq, C["ones_b"]], writes=[nb.ss_t], sig=True)
    kb.op("act", lambda a: a.activation(out=nb.rstd[:, :n], in_=nb.ss_t[:, nb.off:nb.off + n], func=AF.Sqrt,
                                        scale=1.0 / D, bias=C["eps"][:, :]),
          reads=[nb.ss_t, C["eps"]], writes=[nb.rstd])
    kb.op("dve", lambda v: v.reciprocal(out=nb.rstd[:, :n], in_=nb.rstd[:, :n]), reads=[nb.rstd], writes=[nb.rstd])
    for k in range(8):
        kb.op("dve", lambda v: v.scalar_tensor_tensor(out=xn[:, k, xs], in0=h[:, k, hs], scalar=lnw[:, k:k + 1],
                                                      in1=nb.rstd[:, :n], op0=ALU.mult, op1=ALU.mult),
              reads=[h, lnw, nb.rstd], writes=[xn])


def ph_in_transpose(kb, C, x_ap, hT):
    with kb.phase("in_t"):
        xb = [kb.sb("xin", [128, D], F32) for _ in range(2)]
        tps = [kb.ps("tps", [128, 512], F32) for _ in range(2)]
        st = [kb.sb("xst", [128, 8, 128], F32) for _ in range(2)]
        for c in range(NCH):
            xt = xb[c % 2]
            kb.dma("sp", xt[:, :], x_ap[c * 128:(c + 1) * 128, :], writes=[xt])
            s = st[c % 2]
            for g4 in range(2):
                ps = tps[g4]
                for k in range(4):
                    kb.tr(ps[:, k * 128:(k + 1) * 128], xt[:, (g4 * 4 + k) * 128:(g4 * 4 + k + 1) * 128],
                          C["ident_f"][:, :], reads=[xt, C["ident_f"]], writes=[ps], sig=(k == 3))
                e = "act" if g4 == 0 else "dve"
                if e == "act":
                    kb.op(e, lambda a: a.copy(out=s[:, g4 * 4:(g4 + 1) * 4, :],
                                              in_=ps[:, :].rearrange("p (k t) -> p k t", k=4)),
                          reads=[ps], writes=[s])
                else:
                    kb.op(e, lambda v: v.tensor_copy(out=s[:, g4 * 4:(g4 + 1) * 4, :],
                                                     in_=ps[:, :].rearrange("p (k t) -> p k t", k=4)),
                          reads=[ps], writes=[s])
            kb.dma("pool", hview(hT, c * 128, 128), s[:, :, :], reads=[s])


def ph_ffn(kb, C, hT, lnw_ap, w_in_ap, w_out_ap):
    T = 1024
    NT = S // T
    NH = T // 512
    NJ = DFF // 128
    with kb.phase("ffn"):
        lnw = load_vec_pk(kb, "lnw", lnw_ap, 8)
        hbuf = [kb.sb("h", [128, 8, T], F32) for _ in range(2)]
        xn = kb.sb("xn", [128, 8, T], BF16)
        act = kb.sb("act", [128, NJ, T], BF16)
        nb = NormBufs(kb)
        NW = 3
        wbuf = [kb.sb("win", [128, 8, 2, 128], BF16) for _ in range(NW)]
        NO = 3
        wobuf = [kb.sb("wo", [128, NJ, 128], BF16) for _ in range(NO)]
        gps = [kb.ps("g", [128, 512], F32) for _ in range(2)]
        ups = [kb.ps("u", [128, 512], F32) for _ in range(2)]
        ops_ = [kb.ps("o", [128, 512], F32) for _ in range(2)]
        sgb = [kb.sb("sg", [128, 512], F32) for _ in range(2)]
        hob = [kb.sb("ho", [128, 512], F32) for _ in range(2)]

        def load_h(tt):
            kb.dma("sp", hbuf[tt % 2][:, :, :], hview(hT, tt * T, T), writes=[hbuf[tt % 2]])

        win_jobs = [(tt, j) for tt in range(NT) for j in range(NJ)]
        wo_jobs = [(tt, dc) for tt in range(NT) for dc in range(8)]
        st = {"wi": 0, "wo": 0}

        def issue_win(upto):
            while st["wi"] <= upto and st["wi"] < len(win_jobs):
                _, j = win_jobs[st["wi"]]
                w = wbuf[st["wi"] % NW]
                for gu in range(2):
                    c0 = gu * DFF + j * 128
                    kb.dma("pool", w[:, :, gu, :], w_in_ap[:, c0:c0 + 128].rearrange("(k p) c -> p k c", p=128),
                           writes=[w])
                st["wi"] += 1

        def issue_wo(upto):
            while st["wo"] <= upto and st["wo"] < len(wo_jobs):
                _, dc = wo_jobs[st["wo"]]
                w = wobuf[st["wo"] % NO]
                kb.dma("pool", w[:, :, :], w_out_ap[:, dc * 128:(dc + 1) * 128].rearrange("(j p) c -> p j c", p=128),
                       writes=[w])
                st["wo"] += 1

        load_h(0)
        issue_win(1)
        cnt = 0
        for tt in range(NT):
            if tt + 1 < NT:
                load_h(tt + 1)
            h = hbuf[tt % 2]
            for hf in range(NH):
                emit_norm(kb, C, nb, h, lnw, xn, 512, hoff=hf * 512, xoff=hf * 512)
            for j in range(NJ):
                ji = tt * NJ + j
                issue_win(ji + 2)
                if j == 8:
                    issue_wo(tt * 8)
                if j == 16:
                    issue_wo(tt * 8 + 1)
                w = wbuf[ji % NW]
                for hf in range(NH):
                    sl = slice(hf * 512, (hf + 1) * 512)
                    g_ps = gps[cnt % 2]
                    u_ps = ups[cnt % 2]
                    sg = sgb[cnt % 2]
                    cnt += 1
                    for k in range(8):
                        kb.mm(g_ps[:, :], w[:, k, 0, :], xn[:, k, sl], start=(k == 0), stop=(k == 7),
                              reads=[w, xn], writes=[g_ps])
                    for k in range(8):
                        kb.mm(u_ps[:, :], w[:, k, 1, :], xn[:, k, sl], start=(k == 0), stop=(k == 7),
                              reads=[w, xn], writes=[u_ps])
                    kb.op("act", lambda a: a.activation(out=sg[:, :], in_=g_ps[:, :], func=AF.Silu),
                          reads=[g_ps], writes=[sg])
                    kb.op("dve", lambda v: v.tensor_tensor(out=act[:, j, sl], in0=sg[:, :], in1=u_ps[:, :],
                                                           op=ALU.mult),
                          reads=[sg, u_ps], writes=[act])
            for dc in range(8):
                oi = tt * 8 + dc
                issue_wo(oi + 2)
                if dc >= 6:
                    issue_win((tt + 1) * NJ + (dc - 6))
                wo = wobuf[oi % NO]
                for hf in range(NH):
                    sl = slice(hf * 512, (hf + 1) * 512)
                    o_ps = ops_[cnt % 2]
                    ho = hob[cnt % 2]
                    cnt += 1
                    for j in range(NJ):
                        kb.mm(o_ps[:, :], wo[:, j, :], act[:, j, sl], start=(j == 0), stop=(j == NJ - 1),
                              reads=[wo, act], writes=[o_ps])
                    kb.op("dve", lambda v: v.scalar_tensor_tensor(out=ho[:, :], in0=o_ps[:, :], scalar=0.5,
                                                                  in1=h[:, dc, sl], op0=ALU.mult, op1=ALU.add),
                          reads=[o_ps, h], writes=[ho])
                    t0 = tt * T + hf * 512
                    kb.dma("sp", hT[dc * 128:(dc + 1) * 128, t0:t0 + 512], ho[:, :], reads=[ho])


def ph_final(kb, C, hT, fnw_ap, out_ap):
    with kb.phase("final"):
        lnw = load_vec_pk(kb, "fnw", fnw_ap, 8)
        hbuf = [kb.sb("h", [128, 8, 512], F32) for _ in range(2)]
        xn = [kb.sb("xnf", [128, 8, 512], F32) for _ in range(2)]
        nb = NormBufs(kb)
        tps = [kb.ps("tpo", [128, 512], F32) for _ in range(2)]
        ob = [kb.sb("ob", [128, D], F32) for _ in range(2)]
        n = 0
        for tt in range(S // 512):
            h = hbuf[tt % 2]
            kb.dma("sp", h[:, :, :], hview(hT, tt * 512, 512), writes=[h])
            x = xn[tt % 2]
            emit_norm(kb, C, nb, h, lnw, x, 512)
            for c in range(4):
                o = ob[n % 2]
                n += 1
                for g4 in range(2):
                    ps = tps[g4]
                    for k in range(4):
                        kk = g4 * 4 + k
                        kb.tr(ps[:, k * 128:(k + 1) * 128], x[:, kk, c * 128:(c + 1) * 128], C["ident_f"][:, :],
                              reads=[x, C["ident_f"]], writes=[ps], sig=(k == 3))
                    if g4 == 0:
                        kb.op("act", lambda a: a.copy(out=o[:, 0:512], in_=ps[:, :]), reads=[ps], writes=[o])
                    else:
                        kb.op("dve", lambda v: v.tensor_copy(out=o[:, 512:1024], in_=ps[:, :]), reads=[ps], writes=[o])
                t0 = tt * 512 + c * 128
                kb.dma("pool", out_ap[t0:t0 + 128, :], o[:, :], reads=[o])
NEG = -30000.0
BIG = 10000.0
SCALE = 0.125
TWO_PI = 6.283185307179586
CW1 = 6.28125
CW2 = TWO_PI - CW1
MAGIC = 12582912.0


def attn_host_consts(c):
    k = np.arange(128)[:, None]
    q = np.arange(128)[None, :]
    cur = np.where(k <= q, 0.0, NEG).astype(np.float32)
    prev = np.where(k > q, 0.0, NEG).astype(np.float32)
    c["c_mask_cur"] = np.tile(cur, (1, 4))
    c["c_mask_prev"] = np.tile(prev, (1, 4))
    invf = (500000.0 ** (-np.arange(0, 16, 2, dtype=np.float32) / 16.0)).astype(np.float32)
    c["c_invf"] = np.tile(invf[None, :], (128, 1)).astype(np.float32)
    L = np.zeros((32, 128), np.float32)
    for j in range(8):
        L[j] = np.where(16 * (j - 1) + 31 <= np.arange(128), 0.0, NEG)
    c["c_lneg"] = np.tile(L, (1, 4))
    E = np.zeros((32, 384), np.float32)
    for j in range(8):
        E[j, j + 127] = 1.0
    c["c_eshift"] = E
    X = np.zeros((64, 4096), np.float32)
    for kk in range(4096):
        X[kk // 64, kk] = 1.0
    c["c_expand"] = X
    T = np.zeros((128, 128), np.float32)
    for ql in range(128):
        cb = 1 if ql >= 64 else 0
        for y in range(128):
            jj = y - 64
            T[ql, y] = BIG if jj == cb else (-BIG if jj > cb else 0.0)
    c["c_ftab"] = T
    n_cmp = 255
    sel_lo = np.arange(64)[:, None] * 64
    cmp_lo = np.arange(n_cmp)[None, :] * 16
    ov = np.clip(np.minimum(sel_lo + 64, cmp_lo + 32) - np.maximum(sel_lo, cmp_lo), 0, None) / 32.0
    ovT = np.zeros((256, 64), np.float32)
    ovT[:n_cmp] = ov.T
    c["c_ovT"] = ovT


def ensure_rope(kb, C, P):
    if "cos" in C:
        return
    cos = kb.sb("cos", [128, 32, 8], F32, glob=True)
    sin = kb.sb("sin", [128, 32, 8], F32, glob=True)
    with kb.phase("rope"):
        pi_ = kb.sb("posi", [32, 128], I32)
        kb.dma("sp", pi_[:, :], P.inp("positions").rearrange("(c p) o -> c (p o)", p=128), writes=[pi_])
        pf = kb.sb("posf", [32, 128], F32)
        kb.op("dve", lambda v: v.tensor_copy(out=pf[:, :], in_=pi_[:, :]), reads=[pi_], writes=[pf])
        ps = kb.ps("pps", [128, 32], F32)
        kb.tr(ps[:, :], pf[:, :], C["ident_f"][:32, :32], reads=[pf, C["ident_f"]], writes=[ps])
        pt = kb.sb("post", [128, 32], F32)
        kb.op("dve", lambda v: v.tensor_copy(out=pt[:, :], in_=ps[:, :]), reads=[ps], writes=[pt])
        invf = kb.sb("invf", [128, 8], F32)
        kb.dma("sp", invf[:, :], P.inp("c_invf")[:, :], writes=[invf])
        ang = kb.sb("ang", [128, 32, 8], F32)
        kb.op("dve", lambda v: v.tensor_tensor(out=ang[:, :, :], in0=pt[:, :].unsqueeze(2).broadcast_to([128, 32, 8]),
                                               in1=invf[:, :].unsqueeze(1).broadcast_to([128, 32, 8]), op=ALU.mult),
              reads=[pt, invf], writes=[ang])
        a2 = ang[:, :, :].rearrange("p c f -> p (c f)")
        kf = kb.sb("kf", [128, 256], F32)
        r = kb.sb("rr", [128, 256], F32)
        m = kb.sb("mm", [128, 256], F32)
        kb.op("dve", lambda v: v.tensor_scalar(out=kf[:, :], in0=a2, scalar1=1.0 / TWO_PI, scalar2=MAGIC,
                                               op0=ALU.mult, op1=ALU.add), reads=[ang], writes=[kf])
        kb.op("dve", lambda v: v.tensor_scalar(out=kf[:, :], in0=kf[:, :], scalar1=-MAGIC, scalar2=None,
                                               op0=ALU.add), reads=[kf], writes=[kf])
        kb.op("dve", lambda v: v.scalar_tensor_tensor(out=r[:, :], in0=kf[:, :], scalar=-CW1, in1=a2,
                                                      op0=ALU.mult, op1=ALU.add), reads=[kf, ang], writes=[r])
        kb.op("dve", lambda v: v.scalar_tensor_tensor(out=r[:, :], in0=kf[:, :], scalar=-CW2, in1=r[:, :],
                                                      op0=ALU.mult, op1=ALU.add), reads=[kf, r], writes=[r])
        PI_C = 3.1415925
        kb.op("dve", lambda v: v.tensor_scalar(out=r[:, :], in0=r[:, :], scalar1=PI_C, scalar2=-PI_C,
                                               op0=ALU.min, op1=ALU.max), reads=[r], writes=[r])
        kb.op("act", lambda a: a.activation(out=sin[:, :, :].rearrange("p c f -> p (c f)"), in_=r[:, :], func=AF.Sin),
              reads=[r], writes=[sin])
        kb.op("dve", lambda v: v.tensor_scalar(out=m[:, :], in0=r[:, :], scalar1=np.pi / 2, scalar2=-TWO_PI,
                                               op0=ALU.is_gt, op1=ALU.mult), reads=[r], writes=[m])
        kb.op("dve", lambda v: v.scalar_tensor_tensor(out=m[:, :], in0=r[:, :], scalar=np.pi / 2, in1=m[:, :],
                                                      op0=ALU.add, op1=ALU.add), reads=[r, m], writes=[m])
        kb.op("dve", lambda v: v.tensor_scalar(out=m[:, :], in0=m[:, :], scalar1=PI_C, scalar2=-PI_C,
                                               op0=ALU.min, op1=ALU.max), reads=[m], writes=[m])
        kb.op("act", lambda a: a.activation(out=cos[:, :, :].rearrange("p c f -> p (c f)"), in_=m[:, :], func=AF.Sin),
              reads=[m], writes=[cos])
    C["cos"] = cos
    C["sin"] = sin


def load_w_bf16(kb, name, w_ap, nk, F, q="pool"):
    t = kb.sb(name, [128, nk, F], BF16)
    for k in range(nk):
        kb.dma(q, t[:, k, :], w_ap[k * 128:(k + 1) * 128, :], writes=[t])
    return t


def load_const_bf16(kb, name, ap, shape):
    t = kb.sb(name, shape, BF16)
    kb.dma("pool", t[:, :], ap[:, :], writes=[t])
    return t


def emit_rope(kb, C, x3, nshape, i, tmp, reads_writes):
    cosb = C["cos"][:, i, :]
    sinb = C["sin"][:, i, :]
    for _ in nshape:
        cosb = cosb.unsqueeze(1)
        sinb = sinb.unsqueeze(1)
    shp = [128] + list(nshape) + [8]
    cosb = cosb.broadcast_to(shp)
    sinb = sinb.broadcast_to(shp)
    pre = (slice(None),) * (1 + len(nshape))
    x1 = x3[pre + (slice(0, 8),)]
    x2 = x3[pre + (slice(8, 16),)]
    t1, t2, t3, t4 = tmp
    rw = reads_writes
    kb.op("dve", lambda v: v.tensor_tensor(out=t1, in0=x1, in1=cosb, op=ALU.mult), reads=[rw, C["cos"]], writes=[rw])
    kb.op("dve", lambda v: v.tensor_tensor(out=t2, in0=x2, in1=sinb, op=ALU.mult), reads=[rw, C["sin"]], writes=[rw])
    kb.op("dve", lambda v: v.tensor_tensor(out=t3, in0=x2, in1=cosb, op=ALU.mult), reads=[rw, C["cos"]], writes=[rw])
    kb.op("dve", lambda v: v.tensor_tensor(out=t4, in0=x1, in1=sinb, op=ALU.mult), reads=[rw, C["sin"]], writes=[rw])
    kb.op("dve", lambda v: v.tensor_tensor(out=x1, in0=t1, in1=t2, op=ALU.subtract), reads=[rw], writes=[rw])
    kb.op("dve", lambda v: v.tensor_tensor(out=x2, in0=t3, in1=t4, op=ALU.add), reads=[rw], writes=[rw])


def ph_nsa_pre(kb, C, P, hT, layer, inst):
    ensure_rope(kb, C, P)
    kcmpT = kb.sb("kcmpT", [64, 4, 256], BF16, glob=True)
    vcmp = kb.sb("vcmp", [128, 2, 4, 128], BF16, glob=True)
    w_in = P.inp("nsa_w_in")[inst]
    with kb.phase("nsa_pre"):
        lnw = load_vec_pk(kb, "lnw", P.inp("ln_mix")[layer], 8)
        wkv = kb.sb("wkv", [128, 8, 512], BF16)
        for k in range(8):
            kb.dma("pool", wkv[:, k, :], w_in[k * 128:(k + 1) * 128, 1024:1536], writes=[wkv])
        w1 = {}
        for nm in ("k", "v"):
            w1[nm] = kb.sb("w1" + nm, [64, 32, 256], BF16)
            kb.dma("pool", w1[nm][:, :, :], P.inp(f"nsa_{nm}_w1")[inst].rearrange("(l d) h -> d l h", d=64),
                   writes=[w1[nm]])
        w2 = {}
        for nm in ("k", "v"):
            w2[nm] = kb.sb("w2" + nm, [128, 2, 64], BF16)
            kb.dma("pool", w2[nm][:, :, :], P.inp(f"nsa_{nm}_w2")[inst].rearrange("(c p) d -> p c d", p=128),
                   writes=[w2[nm]])
        peT = {}
        aux = kb.ps("aux", [128, 512], F32)
        for nm in ("k", "v"):
            pe32 = kb.sb("pe32" + nm, [32, 64], F32)
            kb.dma("sp", pe32[:, :], P.inp(f"nsa_pe_{nm}")[inst][:, :], writes=[pe32])
            pps = aux
            kb.tr(pps[0:64, 0:32], pe32[:, :], C["ident_f"][:32, :32], reads=[pe32, C["ident_f"]], writes=[pps])
            peT[nm] = kb.sb("peT" + nm, [64, 32], BF16)
            kb.op("dve", lambda v: v.tensor_copy(out=peT[nm][:, :], in_=pps[0:64, 0:32]), reads=[pps], writes=[peT[nm]])
        kT = kb.sb("kcT", [64, 4, S], BF16)
        vT = kb.sb("vcT", [64, 4, S], BF16)
        hb = [kb.sb("h", [128, 8, 128], F32) for _ in range(2)]
        xn = kb.sb("xn", [128, 8, 128], BF16)
        nb = NormBufs(kb, 128)
        pj = [kb.ps("pj", [128, 512], F32) for _ in range(2)]
        kv = kb.sb("kv", [128, 512], F32)
        kvb = kb.sb("kvb", [128, 512], BF16)
        tmp = kb.sb("rt", [128, 4, 4, 8], F32)
        tp = [kb.ps("tp", [64, 512], BF16) for _ in range(2)]

        def load_h(i):
            kb.dma("sp", hb[i % 2][:, :, :], hview(hT, i * 128, 128), writes=[hb[i % 2]])
        load_h(0)
        for i in range(NCH):
            if i + 1 < NCH:
                load_h(i + 1)
            h = hb[i % 2]
            emit_norm(kb, C, nb, h, lnw, xn, 128)
            ps = pj[i % 2]
            for k in range(8):
                kb.mm(ps[:, :], xn[:, k, :], wkv[:, k, :], start=(k == 0), stop=(k == 7), reads=[xn, wkv], writes=[ps])
            kb.op("act", lambda a: a.copy(out=kv[:, :], in_=ps[:, :]), reads=[ps], writes=[kv])
            x3 = kv[:, 0:256].rearrange("p (h d) -> p h d", d=64)
            emit_rope(kb, C, x3, [4], i, [tmp[:, j, :, :] for j in range(4)], kv)
            kb.op("dve", lambda v: v.tensor_copy(out=kvb[:, :], in_=kv[:, :]), reads=[kv], writes=[kvb])
            for half, dst in ((0, kT), (1, vT)):
                t = tp[half]
                for g in range(4):
                    c0 = half * 256 + g * 64
                    kb.tr(t[:, g * 128:(g + 1) * 128], kvb[:, c0:c0 + 64], C["ident_b"][:, :],
                          reads=[kvb, C["ident_b"]], writes=[t], sig=(g == 3))
                eng = "act" if half == 0 else "pool"
                if half == 0:
                    kb.op("act", lambda a: a.copy(out=dst[:, :, i * 128:(i + 1) * 128],
                                                  in_=t[:, :].rearrange("p (g t) -> p g t", g=4)),
                          reads=[t], writes=[dst])
                else:
                    kb.op("dve", lambda v: v.tensor_copy(out=dst[:, :, i * 128:(i + 1) * 128],
                                                         in_=t[:, :].rearrange("p (g t) -> p g t", g=4)),
                          reads=[t], writes=[dst])
        hps = pj
        bps = aux
        ops_ = aux
        kb.op("dve", lambda v: v.memset(vcmp[:, :, :, :], 1.0), writes=[vcmp])
        kb.op("dve", lambda v: v.memset(kcmpT[:, :, :], 0.0), writes=[kcmpT])
        ov32 = kb.sb("ov32", [128, 2, 64], F32)
        kb.dma("sp", ov32[:, :, :], P.inp("c_ovT").rearrange("(c p) j -> p c j", p=128), writes=[ov32])
        for g in range(4):
            kb.op("dve", lambda v: v.tensor_copy(out=vcmp[:, :, g, 65:128], in_=ov32[:, :, 1:64]), reads=[ov32], writes=[vcmp])
        n = 0
        for nm, src in (("k", kT), ("v", vT)):
            bias = kb.sb("cb" + nm, [128, 2], F32)
            for hc in range(2):
                for l in range(32):
                    kb.mm(bps[:, 32 + hc:33 + hc], w1[nm][:, l, hc * 128:(hc + 1) * 128], peT[nm][:, l:l + 1],
                          start=(l == 0), stop=(l == 31), reads=[w1[nm], peT[nm]], writes=[bps])
            kb.op("dve", lambda v: v.tensor_copy(out=bias[:, :], in_=bps[:, 32:34]), reads=[bps], writes=[bias])
            for g in range(4):
                hid = kb.sb("hid", [128, 2, 256], BF16)
                for hc in range(2):
                    hp = hps[n % 2]
                    n += 1
                    for l in range(32):
                        rhs = src[:, g, :].rearrange("p (c s) -> p c s", s=16)[:, l // 16:l // 16 + 255, l % 16]
                        kb.mm(hp[:, 0:255], w1[nm][:, l, hc * 128:(hc + 1) * 128], rhs, start=(l == 0), stop=(l == 31),
                              reads=[w1[nm], src], writes=[hp])
                    kb.op("act", lambda a: a.activation(out=hid[:, hc, 0:255], in_=hp[:, 0:255], func=AF.Silu,
                                                        bias=bias[:, hc:hc + 1]),
                          reads=[hp, bias], writes=[hid])
                if nm == "k":
                    for hc in range(2):
                        kb.mm(ops_[0:64, 256:511], w2[nm][:, hc, :], hid[:, hc, 0:255], start=(hc == 0), stop=(hc == 1),
                              reads=[w2[nm], hid], writes=[ops_])
                    kb.op("dve", lambda v: v.tensor_copy(out=kcmpT[:, g, 0:255], in_=ops_[0:64, 256:511]),
                          reads=[ops_], writes=[kcmpT])
                else:
                    for cc in range(2):
                        M = 128 if cc == 0 else 127
                        for hc in range(2):
                            kb.mm(ops_[0:M, 256 + cc * 64:256 + (cc + 1) * 64], hid[:, hc, cc * 128:cc * 128 + M], w2[nm][:, hc, :],
                                  start=(hc == 0), stop=(hc == 1), reads=[w2[nm], hid], writes=[ops_])
                        kb.op("dve", lambda v: v.tensor_copy(out=vcmp[0:M, cc, g, 0:64], in_=ops_[0:M, 256 + cc * 64:256 + (cc + 1) * 64]),
                              reads=[ops_], writes=[vcmp])
    C["kcmpT"] = kcmpT
    C["vcmp"] = vcmp


def ph_attn(kb, C, P, hT, layer, inst, kind):
    ensure_rope(kb, C, P)
    nsa = kind == "nsa"
    if nsa:
        ph_nsa_pre(kb, C, P, hT, layer, inst)
        w_in = P.inp("nsa_w_in")[inst]
        F = 2608
        w_o = P.inp("nsa_w_o")[inst]
    else:
        w_in = P.inp("swa_w_qkv")[inst]
        F = 1536
        w_o = P.inp("swa_w_o")[inst]
    NCT = (F + 511) // 512
    with kb.phase("attn"):
        lnw = load_vec_pk(kb, "lnw", P.inp("ln_mix")[layer], 8)
        wq = load_w_bf16(kb, "wq", w_in, 8, F)
        wo = load_w_bf16(kb, "wo", w_o, 8, D)
        mcur = load_const_bf16(kb, "mcur", P.inp("c_mask_cur"), [128, 512])
        mprev = load_const_bf16(kb, "mprev", P.inp("c_mask_prev"), [128, 512])
        if nsa:
            lneg = load_const_bf16(kb, "lneg", P.inp("c_lneg"), [32, 512])
            esh = load_const_bf16(kb, "esh", P.inp("c_eshift"), [32, 384])
            ftab = kb.sb("ftab", [128, 128], F32)
            kb.dma("sp", ftab[:, :], P.inp("c_ftab")[:, :], writes=[ftab])
            kcmpT, vcmp = C["kcmpT"], C["vcmp"]
            branches = ["sel", "win"]
            kcol = {"sel": 1536, "win": 2048}
            vcol = {"sel": 1792, "win": 2304}
            nback = {"sel": 10 ** 6, "win": 4}
        else:
            bias = kb.sb("bqkv", [128, F], F32)
            kb.dma("sp", bias[:, :], P.inp("swa_b_qkv")[inst].partition_broadcast(128), writes=[bias])
            bo = load_vec_pk(kb, "bo", P.inp("swa_b_o")[inst], 8)
            esink = kb.sb("esink", [128, 16], F32)
            kb.dma("sp", esink[:, :], P.inp("swa_sinks")[inst].partition_broadcast(128), writes=[esink])
            kb.op("act", lambda a: a.activation(out=esink[:, :], in_=esink[:, :], func=AF.Exp), reads=[esink], writes=[esink])
            branches = ["win"]
            kcol = {"win": 1024}
            vcol = {"win": 1280}
            nback = {"win": 1}
        ring = {b: (8 if b == "win" else NCH) for b in branches}
        KT = {b: kb.sb("KT" + b, [128 if b == "sel" else 64, 4, ring[b] * 128], BF16) for b in branches}
        if nsa:
            for g in range(4):
                kb.dma("pool", KT["sel"][64:128, g, :], P.inp("c_expand")[:, :], writes=[KT["sel"]])
            qsel_res = [Res("qsel%d" % g) for g in range(4)]
        VC = {b: kb.sb("VC" + b, [128, ring[b], 4, 65], BF16) for b in branches}
        for b in branches:
            kb.op("pool", lambda g_: g_.memset(VC[b][:, :, :, :], 1.0), writes=[VC[b]])
        hb = [kb.sb("h", [128, 8, 128], F32) for _ in range(2)]
        xn = kb.sb("xn", [128, 8, 128], BF16)
        nb = NormBufs(kb, 128)
        pj = [kb.ps("pj", [128, 512], F32) for _ in range(2)]
        tpp = kb.ps("tp", [128, 1024], BF16)
        stp = [kb.ps("st", [128, 512], F32) for _ in range(2)]
        opp = [kb.ps("op", [128, 512], F32) for _ in range(2)]
        qkv = kb.sb("qkv", [128, F], F32)
        qkb = kb.sb("qkb", [128, F], BF16)
        QT = kb.sb("QT", [128, 16, 128], BF16)
        rtq = kb.sb("rtq", [128, 4, 16, 8], F32)
        rtk = kb.sb("rtk", [128, 4, 3, 4, 8], F32)
        pT = [kb.sb("pT", [128, 512], BF16) for _ in range(4)]
        osb = kb.sb("osb", [128, 16, 64], F32)
        osbb = kb.sb("osbb", [128, D], BF16)
        oT = kb.sb("oT", [128, 8, 128], BF16)
        otmp = kb.sb("otmp", [128, 4, 64], F32)
        den = kb.sb("den", [128, 4], F32)
        coef = kb.sb("coef", [128, 4], F32)
        hout = [kb.sb("hout", [128, 8, 128], F32) for _ in range(2)]
        if nsa:
            gates = kb.sb("gates", [128, 48], F32)
            imp = kb.sb("imp", [128, 64], F32)
            top8 = kb.sb("top8", [128, 8], F32)
            nselb = kb.sb("nselb", [128, 64], BF16)
        st = {"p": 0, "s": 0, "o": 0}

        def load_h(i):
            kb.dma("sp", hb[i % 2][:, :, :], hview(hT, i * 128, 128), writes=[hb[i % 2]])

        def make_unit(items, lhsT_k, M, rhs_q, masks, v_rhs, ncol, o_ref, kreads, vreads, first_in_group, post):
            slot = {}

            def A():
                sps = stp[st["s"] % 2]
                st["s"] += 1
                pt = pT[st["p"] % 4]
                st["p"] += 1
                slot["pt"] = pt
                kb.mm(sps[0:M, :], lhsT_k, rhs_q, start=True, stop=(len(masks) == 0), reads=kreads + [QT], writes=[sps])
                for mi, (ml, mr, mrd) in enumerate(masks):
                    kb.mm(sps[0:M, :], ml, mr, start=False, stop=(mi == len(masks) - 1), reads=mrd, writes=[sps])
                kb.op("act", lambda a: a.activation(out=pt[0:M, :], in_=sps[0:M, :], func=AF.Exp, scale=SCALE),
                      reads=[sps], writes=[pt])

            def B():
                o_ps = o_ref["t"]
                pt = slot["pt"]
                if first_in_group:
                    kb.op("dve", lambda v: v.memset(o_ps[:, :], 0.0), writes=[o_ps])
                for r in range(4):
                    kb.mm(o_ps[:, r * 128:r * 128 + ncol], pt[0:M, r * 128:(r + 1) * 128], v_rhs, start=False, stop=False,
                          reads=[pt] + vreads, writes=[o_ps], sig=(r == 3), skip_group_check=True)
                if post is not None:
                    post()
            items.append((A, B))

        def run_items(items):
            LA = 1
            for n, (A, B) in enumerate(items):
                A()
                if n >= LA:
                    items[n - LA][1]()
            for n in range(max(0, len(items) - LA), len(items)):
                items[n][1]()

        def new_ref():
            ref = {"t": opp[st["o"] % 2]}
            st["o"] += 1
            return ref

        def evac(o_ps, g, ncol, first, gate_col, extra_den=None):
            o3 = o_ps[:, :].rearrange("p (r c) -> p r c", c=128)
            kb.op("dve", lambda v: v.tensor_scalar(out=den[:, :].unsqueeze(2), in0=o3[:, :, 64:65], scalar1=1e-30, scalar2=None,
                                                   op0=ALU.max), reads=[o_ps], writes=[den])
            if extra_den is not None:
                kb.op("dve", lambda v: v.tensor_tensor(out=den[:, :], in0=den[:, :], in1=extra_den, op=ALU.add),
                      reads=[den, esink], writes=[den])
            kb.op("dve", lambda v: v.reciprocal(out=den[:, :], in_=den[:, :]), reads=[den], writes=[den])
            if gate_col is not None:
                gv = gates[:, :].rearrange("p (h b) -> p h b", b=3)[:, 4 * g:4 * g + 4, gate_col]
                kb.op("dve", lambda v: v.tensor_tensor(out=coef[:, :], in0=den[:, :], in1=gv, op=ALU.mult),
                      reads=[den, gates], writes=[coef])
                cf = coef
            else:
                cf = den
            cb_ = cf[:, :].unsqueeze(2).broadcast_to([128, 4, 64])
            if first:
                kb.op("dve", lambda v: v.tensor_tensor(out=osb[:, 4 * g:4 * g + 4, :], in0=o3[:, :, 0:64], in1=cb_,
                                                       op=ALU.mult), reads=[o_ps, cf], writes=[osb])
            else:
                kb.op("dve", lambda v: v.tensor_tensor(out=otmp[:, :, :], in0=o3[:, :, 0:64], in1=cb_, op=ALU.mult),
                      reads=[o_ps, cf], writes=[otmp])
                kb.op("dve", lambda v: v.tensor_tensor(out=osb[:, 4 * g:4 * g + 4, :], in0=osb[:, 4 * g:4 * g + 4, :],
                                                       in1=otmp[:, :, :], op=ALU.add), reads=[otmp, osb], writes=[osb])

        load_h(0)
        for i in range(NCH):
            if i + 1 < NCH:
                load_h(i + 1)
            h = hb[i % 2]
            emit_norm(kb, C, nb, h, lnw, xn, 128)
            for ct in range(NCT):
                c0 = ct * 512
                c1 = min(F, c0 + 512)
                ps = pj[ct % 2]
                for k in range(8):
                    kb.mm(ps[:, 0:c1 - c0], xn[:, k, :], wq[:, k, c0:c1], start=(k == 0), stop=(k == 7),
                          reads=[xn, wq], writes=[ps])
                if nsa:
                    kb.op("act", lambda a: a.copy(out=qkv[:, c0:c1], in_=ps[:, 0:c1 - c0]), reads=[ps], writes=[qkv])
                else:
                    kb.op("dve", lambda v: v.tensor_tensor(out=qkv[:, c0:c1], in0=ps[:, 0:c1 - c0], in1=bias[:, c0:c1],
                                                           op=ALU.add), reads=[ps, bias], writes=[qkv])
            q3 = qkv[:, 0:1024].rearrange("p (h d) -> p h d", d=64)
            emit_rope(kb, C, q3, [16], i, [rtq[:, j, :, :] for j in range(4)], qkv)
            if nsa:
                k4 = qkv[:, 1024:2560].rearrange("p (b h d) -> p b h d", b=3, d=64)[:, :, 0:4, :]
                emit_rope(kb, C, k4, [3, 4], i, [rtk[:, j, :, :, :] for j in range(4)], qkv)
                kb.op("act", lambda a: a.activation(out=gates[:, :], in_=qkv[:, 2560:2608], func=AF.Sigmoid),
                      reads=[qkv], writes=[gates])
            else:
                k3 = qkv[:, 1024:1280].rearrange("p (h d) -> p h d", d=64)
                emit_rope(kb, C, k3, [4], i, [rtk[:, j, 0, :, :] for j in range(4)], qkv)
            kb.op("pool", lambda g_: g_.tensor_copy(out=qkb[:, 0:F], in_=qkv[:, 0:F]), reads=[qkv], writes=[qkb])
            for hh in range(2):
                for j in range(8):
                    hd = hh * 8 + j
                    kb.tr(tpp[0:64, j * 128:(j + 1) * 128], qkb[:, hd * 64:(hd + 1) * 64], C["ident_b"][:, :],
                          reads=[qkb, C["ident_b"]], writes=[tpp], sig=(j == 7))
                kb.op("act", lambda a: a.copy(out=QT[0:64, hh * 8:(hh + 1) * 8, :],
                                              in_=tpp[0:64, :].rearrange("p (j t) -> p j t", j=8)),
                      reads=[tpp], writes=[QT])
            for bi, b in enumerate(branches):
                for g in range(4):
                    c0 = kcol[b] + g * 64
                    kb.tr(tpp[0:64, (bi * 4 + g) * 128:(bi * 4 + g + 1) * 128], qkb[:, c0:c0 + 64], C["ident_b"][:, :],
                          reads=[qkb, C["ident_b"]], writes=[tpp], sig=(g == 3))
                isl = i % ring[b]
                kb.op("act", lambda a: a.copy(out=KT[b][0:64, :, isl * 128:(isl + 1) * 128],
                                              in_=tpp[0:64, bi * 512:(bi + 1) * 512].rearrange("p (g t) -> p g t", g=4)),
                      reads=[tpp], writes=[KT[b]])
                kb.op("pool", lambda g_: g_.tensor_copy(out=VC[b][:, isl, :, 0:64],
                                                        in_=qkv[:, vcol[b]:vcol[b] + 256].rearrange("p (g d) -> p g d", d=64)),
                      reads=[qkv], writes=[VC[b]])
            items = []

            def selection(g, o_ps):
                o3 = o_ps[:, :].rearrange("p (r c) -> p r c", c=128)
                nst = QT[64:128, 4 * g:4 * g + 4, :]
                kb.op("dve", lambda v: v.memset(imp[:, 0:1], 0.0), writes=[imp])
                for r in range(4):
                    if r == 0:
                        kb.op("dve", lambda v: v.tensor_scalar(out=imp[:, 1:64], in0=o3[:, 0, 65:128], scalar1=den[:, 0:1],
                                                               scalar2=None, op0=ALU.mult),
                              reads=[o_ps, den], writes=[imp])
                    else:
                        kb.op("dve", lambda v: v.scalar_tensor_tensor(out=imp[:, 1:64], in0=o3[:, r, 65:128],
                                                                      scalar=den[:, r:r + 1], in1=imp[:, 1:64],
                                                                      op0=ALU.mult, op1=ALU.add),
                              reads=[o_ps, den, imp], writes=[imp])
                kb.op("dve", lambda v: v.tensor_tensor(out=imp[:, :], in0=imp[:, :], in1=ftab[:, 64 - 2 * i:128 - 2 * i],
                                                       op=ALU.add), reads=[imp, ftab], writes=[imp])
                kb.op("dve", lambda v: v.tensor_scalar(out=imp[:, 0:1], in0=imp[:, 0:1], scalar1=BIG, scalar2=None,
                                                       op0=ALU.add), reads=[imp], writes=[imp])
                kb.op("dve", lambda v: v.max(out=top8[:, :], in_=imp[:, :]), reads=[imp], writes=[top8])
                kb.op("dve", lambda v: v.tensor_scalar(out=imp[:, :], in0=imp[:, :], scalar1=top8[:, 7:8], scalar2=-1.0,
                                                       op0=ALU.is_ge, op1=ALU.add), reads=[imp, top8], writes=[imp])
                kb.op("dve", lambda v: v.tensor_scalar(out=nselb[:, :], in0=imp[:, :], scalar1=-NEG, scalar2=None,
                                                       op0=ALU.mult), reads=[imp], writes=[nselb])
                kb.tr(tpp[0:64, 0:128], nselb[:, :], C["ident_b"][:, :], reads=[nselb, C["ident_b"]], writes=[tpp])
                kb.op("dve", lambda v: v.tensor_copy(out=nst,
                                                     in_=tpp[0:64, 0:128].unsqueeze(1).broadcast_to([64, 4, 128])),
                      reads=[tpp], writes=[qsel_res[g]])

            if nsa:
                for g in range(4):
                    rhs_q = QT[0:64, 4 * g:4 * g + 4, :]
                    ref = new_ref()
                    ncv = min(255, 8 * i + 7)
                    ccs = [cc for cc in range(2) if min(128, ncv - 128 * cc) > 0]
                    for ci, cc in enumerate(ccs):
                        M = min(128, ncv - 128 * cc)
                        o = 8 * i - 128 * cc
                        masks = []
                        if 0 <= o <= 128:
                            masks.append((esh[:, 128 - o:128 - o + M], lneg[:, :], [esh, lneg]))
                        post = None
                        if ci == len(ccs) - 1:
                            def post(g=g, ref=ref):
                                evac(ref["t"], g, 128, True, 0)
                                selection(g, ref["t"])
                        make_unit(items, kcmpT[:, g, cc * 128:cc * 128 + M], M, rhs_q, masks, vcmp[0:M, cc, g, :], 128, ref,
                                  [kcmpT], [vcmp], ci == 0, post)
            for g in range(4):
                for bi_, b in enumerate(branches):
                    ref = new_ref()
                    lo = max(0, i - nback[b])
                    kcs = list(range(lo, i + 1))
                    for ki, kc in enumerate(kcs):
                        masks = []
                        if kc == i:
                            masks.append((C["ident_b"][:, :], mcur[:, :], [C["ident_b"], mcur]))
                        elif b == "win" and kc == i - nback[b]:
                            masks.append((C["ident_b"][:, :], mprev[:, :], [C["ident_b"], mprev]))
                        ks_ = kc % ring[b]
                        post = None
                        if ki == len(kcs) - 1:
                            if nsa:
                                def post(g=g, ref=ref, b=b):
                                    evac(ref["t"], g, 65, False, 1 if b == "sel" else 2)
                            else:
                                def post(g=g, ref=ref):
                                    evac(ref["t"], g, 65, True, None, extra_den=esink[:, 4 * g:4 * g + 4])
                        if b == "sel":
                            lhs = KT[b][:, g, ks_ * 128:(ks_ + 1) * 128]
                            rq = QT[:, 4 * g:4 * g + 4, :]
                            kr = [KT[b], qsel_res[g]]
                        else:
                            lhs = KT[b][0:64, g, ks_ * 128:(ks_ + 1) * 128]
                            rq = QT[0:64, 4 * g:4 * g + 4, :]
                            kr = [KT[b]]
                        make_unit(items, lhs, 128, rq, masks, VC[b][:, ks_, g, :], 65, ref, kr, [VC[b]], ki == 0, post)
            run_items(items)
            kb.op("pool", lambda g_: g_.tensor_copy(out=osbb[:, :], in_=osb[:, :, :].rearrange("p h d -> p (h d)")),
                  reads=[osb], writes=[osbb])
            for k in range(8):
                kb.tr(tpp[:, k * 128:(k + 1) * 128], osbb[:, k * 128:(k + 1) * 128], C["ident_b"][:, :],
                      reads=[osbb, C["ident_b"]], writes=[tpp], sig=(k == 7))
            kb.op("act", lambda a: a.copy(out=oT[:, :, :], in_=tpp[:, :].rearrange("p (k t) -> p k t", k=8)),
                  reads=[tpp], writes=[oT])
            ho = hout[i % 2]
            for half in range(2):
                ps = pj[half]
                for dcl in range(4):
                    dc = half * 4 + dcl
                    for k in range(8):
                        kb.mm(ps[:, dcl * 128:(dcl + 1) * 128], wo[:, k, dc * 128:(dc + 1) * 128], oT[:, k, :],
                              start=(k == 0), stop=(k == 7), reads=[wo, oT], writes=[ps], sig=(k == 7 and dcl == 3))
                hv = h[:, half * 4:(half + 1) * 4, :]
                kb.op("dve", lambda v: v.tensor_tensor(out=ho[:, half * 4:(half + 1) * 4, :],
                                                       in0=ps[:, :].rearrange("p (c t) -> p c t", c=4), in1=hv, op=ALU.add),
                      reads=[ps, h], writes=[ho])
            if not nsa:
                kb.op("dve", lambda v: v.tensor_tensor(out=ho[:, :, :], in0=ho[:, :, :],
                                                       in1=bo[:, :].unsqueeze(2).broadcast_to([128, 8, 128]), op=ALU.add),
                      reads=[ho, bo], writes=[ho])
            kb.dma("sp", hview(hT, i * 128, 128), ho[:, :, :], reads=[ho])


def ph_swa(kb, C, P, hT, layer, inst):
    ph_attn(kb, C, P, hT, layer, inst, "swa")


def ph_nsa(kb, C, P, hT, layer, inst):
    ph_attn(kb, C, P, hT, layer, inst, "nsa")
def mamba_host_consts(c):
    s1 = np.arange(128)[:, None]
    s2 = np.arange(128)[None, :]
    c["c_triU"] = (s1 <= s2).astype(np.float32)
    c["c_strictL"] = (s1 > s2).astype(np.float32)


def load_bcast(kb, name, ap1d, n, q="sp"):
    t = kb.sb(name, [128, n], F32)
    kb.dma(q, t[:, :], ap1d.partition_broadcast(128), writes=[t])
    return t


def ph_mamba(kb, C, P, hT, layer, inst):
    nc = kb.nc
    if "ydram" not in C:
        C["ydram"] = nc.dram_tensor("ydram", [S, 2048], F32, kind="Internal").ap()
    ydram = C["ydram"]
    w_in = P.inp("ssm_w_in")[inst]
    with kb.phase("mamba_a"):
        lnw = load_vec_pk(kb, "lnw", P.inp("ln_mix")[layer], 8)
        wx = kb.sb("wx", [128, 8, 4096], BF16)
        wdt = kb.sb("wdt", [128, 8, 32], BF16)
        for k in range(8):
            kb.dma("pool", wx[:, k, :], w_in[k * 128:(k + 1) * 128, 2048:6144], writes=[wx])
            kb.dma("pool", wdt[:, k, :], w_in[k * 128:(k + 1) * 128, 6144:6176], writes=[wdt])
        triU = kb.sb("triU", [128, 128], F32)
        kb.dma("sp", triU[:, :], P.inp("c_triU")[:, :], writes=[triU])
        strictL = kb.sb("strictL", [128, 128], F32)
        kb.dma("sp", strictL[:, :], P.inp("c_strictL")[:, :], writes=[strictL])
        pj = [kb.ps("pj", [128, 512], F32) for _ in range(2)]
        tp = kb.ps("tp", [128, 1024], BF16)
        sm = kb.ps("sm", [128, 512], F32)
        cbp = kb.ps("cbp", [128, 512], F32)
        stp_ = kb.ps("stp", [128, 512], F32)
        df = kb.ps("df", [128, 512], F32)
        yy = kb.ps("yy", [128, 512], F32)
        nb = NormBufs(kb, 128, ss=sm, off=128)
        cw_raw = kb.sb("cw_raw", [128, 128], F32)
        kb.dma("sp", cw_raw[:, :], P.inp("ssm_conv_w")[inst].rearrange("k (cc p) -> (k cc) p", p=128), writes=[cw_raw])
        kb.tr(pj[0][:, 0:128], cw_raw[:, :], C["ident_f"][:, :], reads=[cw_raw, C["ident_f"]], writes=[pj[0]])
        cwT = kb.sb("cwT", [128, 128], F32)
        kb.op("dve", lambda v: v.tensor_copy(out=cwT[:, :], in_=pj[0][:, 0:128]), reads=[pj[0]], writes=[cwT])
        cb_raw = kb.sb("cb_raw", [32, 128], F32)
        kb.dma("sp", cb_raw[:, :], P.inp("ssm_conv_b")[inst].rearrange("(cc p) -> cc p", p=128), writes=[cb_raw])
        kb.tr(pj[1][:, 0:32], cb_raw[:, :], C["ident_f"][:32, :32], reads=[cb_raw, C["ident_f"]], writes=[pj[1]])
        cbT = kb.sb("cbT", [128, 32], F32)
        kb.op("dve", lambda v: v.tensor_copy(out=cbT[:, :], in_=pj[1][:, 0:32]), reads=[pj[1]], writes=[cbT])
        diag = kb.sb("diag", [128, 128, 128], BF16)
        for idx in range(128):
            e = "dve" if idx % 2 == 0 else "pool"
            kb.op(e, lambda v: v.tensor_scalar(out=diag[:, idx, :], in0=C["ident_f"][:, :], scalar1=cwT[:, idx:idx + 1],
                                               scalar2=None, op0=ALU.mult), reads=[C["ident_f"], cwT], writes=[diag])
        dtb = load_bcast(kb, "dtb", P.inp("ssm_dt_bias")[inst], 32)
        aneg = load_bcast(kb, "aneg", P.inp("ssm_a_log")[inst], 32)
        kb.op("act", lambda a: a.activation(out=aneg[:, :], in_=aneg[:, :], func=AF.Exp), reads=[aneg], writes=[aneg])
        kb.op("dve", lambda v: v.tensor_scalar(out=aneg[:, :], in0=aneg[:, :], scalar1=-1.0, scalar2=None, op0=ALU.mult),
              reads=[aneg], writes=[aneg])
        dsk = load_bcast(kb, "dsk", P.inp("ssm_d")[inst], 32)
        one = kb.sb("one", [128, 1], F32)
        kb.op("dve", lambda v: v.memset(one[:, :], 1.0), writes=[one])

        hb = [kb.sb("h", [128, 8, 128], F32) for _ in range(2)]
        xn = kb.sb("xn", [128, 8, 128], BF16)
        raw = kb.sb("raw", [128, 32, 131], BF16)
        kb.op("dve", lambda v: v.memset(raw[:, :, :], 0.0), writes=[raw])
        xbcT = kb.sb("xbcT", [128, 32, 128], BF16)
        xs_tm = kb.sb("xs_tm", [128, 32, 64], F32)
        B_tm = kb.sb("B_tm", [128, 8, 128], BF16)
        dtt = kb.sb("dtt", [128, 32], F32)
        ad = kb.sb("ad", [128, 32], F32)
        acum = kb.sb("acum", [128, 32], F32)
        eacum = kb.sb("eacum", [128, 32], F32)
        dst = kb.sb("dst", [128, 32], F32)
        cdec = kb.sb("cdec", [128, 32], F32)
        xd = kb.sb("xd", [128, 32, 64], BF16)
        xdd = kb.sb("xdd", [128, 32, 64], BF16)
        cbm2 = [kb.sb("cbm", [128, 128], BF16) for _ in range(2)]
        Rm = kb.sb("Rm", [128, 4, 128], F32)
        decT2 = [kb.sb("decT", [128, 4, 128], BF16) for _ in range(2)]
        MT2 = [kb.sb("MT", [128, 4, 128], BF16) for _ in range(2)]
        t1 = kb.sb("t1", [128, 4, 64], F32)
        t2 = kb.sb("t2", [128, 4, 64], F32)
        ypre = [kb.sb("ypre", [128, 32, 64], F32) for _ in range(2)]
        prev = kb.sb("prev", [128, 32, 64], F32)
        prevb = kb.sb("prevb", [128, 32, 64], BF16)
        kb.op("dve", lambda v: v.memset(prev[:, :, :], 0.0), writes=[prev])
        kb.op("dve", lambda v: v.memset(prevb[:, :, :], 0.0), writes=[prevb])

        def load_h(i):
            kb.dma("sp", hb[i % 2][:, :, :], hview(hT, i * 128, 128), writes=[hb[i % 2]])
        def proj_slice(cc4):
            ps = pj[cc4 % 2]
            for c in range(4):
                cc = cc4 * 4 + c
                for k in range(8):
                    kb.mm(ps[:, c * 128:(c + 1) * 128], wx[:, k, cc * 128:(cc + 1) * 128], xn[:, k, :],
                          start=(k == 0), stop=(k == 7), reads=[wx, xn], writes=[ps], sig=(k == 7 and c == 3))
            kb.op("act", lambda a: a.copy(out=raw[:, cc4 * 4:(cc4 + 1) * 4, 3:131],
                                          in_=ps[:, :].rearrange("p (c t) -> p c t", c=4)), reads=[ps], writes=[raw])

        load_h(0)
        for i in range(NCH):
            if i + 1 < NCH:
                load_h(i + 1)
            h = hb[i % 2]
            emit_norm(kb, C, nb, h, lnw, xn, 128)
            for k in range(8):
                kb.mm(sm[:, 0:32], xn[:, k, :], wdt[:, k, :], start=(k == 0), stop=(k == 7), reads=[xn, wdt], writes=[sm])
            kb.op("dve", lambda v: v.tensor_tensor(out=dtt[:, :], in0=sm[:, 0:32], in1=dtb[:, :], op=ALU.add),
                  reads=[sm, dtb], writes=[dtt])
            kb.op("act", lambda a: a.activation(out=dtt[:, :], in_=dtt[:, :], func=AF.Exp), reads=[dtt], writes=[dtt])
            kb.op("act", lambda a: a.activation(out=dtt[:, :], in_=dtt[:, :], func=AF.Ln, bias=one[:, :]),
                  reads=[dtt, one], writes=[dtt])
            kb.op("dve", lambda v: v.tensor_tensor(out=ad[:, :], in0=dtt[:, :], in1=aneg[:, :], op=ALU.mult),
                  reads=[dtt, aneg], writes=[ad])
            kb.mm(sm[:, 32:64], triU[:, :], ad[:, :], start=True, stop=True, reads=[triU, ad], writes=[sm])
            kb.mm(sm[:, 64:96], C["ones_f"][:, :], ad[:, :], start=True, stop=True, reads=[C["ones_f"], ad], writes=[sm])
            kb.op("dve", lambda v: v.tensor_copy(out=acum[:, :], in_=sm[:, 32:64]), reads=[sm], writes=[acum])
            kb.op("act", lambda a: a.activation(out=eacum[:, :], in_=sm[:, 32:64], func=AF.Exp), reads=[sm], writes=[eacum])
            kb.op("act", lambda a: a.activation(out=cdec[:, :], in_=sm[:, 64:96], func=AF.Exp), reads=[sm], writes=[cdec])
            kb.op("dve", lambda v: v.tensor_tensor(out=dst[:, :], in0=sm[:, 64:96], in1=acum[:, :], op=ALU.subtract),
                  reads=[sm, acum], writes=[dst])
            kb.op("act", lambda a: a.activation(out=dst[:, :], in_=dst[:, :], func=AF.Exp), reads=[dst], writes=[dst])
            kb.op("dve", lambda v: v.tensor_tensor(out=dst[:, :], in0=dst[:, :], in1=dtt[:, :], op=ALU.mult),
                  reads=[dst, dtt], writes=[dst])
            for cc4 in range(8):
                proj_slice(cc4)
            for cc4 in range(8):
                ps = pj[cc4 % 2]
                for c in range(4):
                    cc = cc4 * 4 + c
                    for k in range(4):
                        kb.mm(ps[:, c * 128:(c + 1) * 128], diag[:, k * 32 + cc, :], raw[:, cc, k:k + 128],
                              start=(k == 0), stop=(k == 3), reads=[diag, raw], writes=[ps], sig=(k == 3 and c == 3))
                for c in range(4):
                    cc = cc4 * 4 + c
                    kb.op("act", lambda a: a.activation(out=xbcT[:, cc, :], in_=ps[:, c * 128:(c + 1) * 128], func=AF.Silu,
                                                        bias=cbT[:, cc:cc + 1]), reads=[ps, cbT], writes=[xbcT])
            kb.op("pool", lambda g_: g_.tensor_copy(out=raw[:, :, 0:3], in_=raw[:, :, 128:131]), reads=[raw], writes=[raw])
            for rd in range(3):
                for j in range(8):
                    cc = rd * 8 + j
                    kb.tr(tp[:, j * 128:(j + 1) * 128], xbcT[:, cc, :], C["ident_b"][:, :], reads=[xbcT, C["ident_b"]],
                          writes=[tp], sig=(j == 7))
                if rd < 2:
                    kb.op("act", lambda a: a.copy(out=xs_tm[:, rd * 16:(rd + 1) * 16, :].rearrange("p h d -> p (h d)"),
                                                  in_=tp[:, :]), reads=[tp], writes=[xs_tm])
                else:
                    kb.op("act", lambda a: a.copy(out=B_tm[:, :, :].rearrange("p g n -> p (g n)"), in_=tp[:, :]),
                          reads=[tp], writes=[B_tm])
            dtb_ = dtt[:, :].unsqueeze(2).broadcast_to([128, 32, 64])
            dst_ = dst[:, :].unsqueeze(2).broadcast_to([128, 32, 64])
            kb.op("dve", lambda v: v.tensor_tensor(out=xd[:, :, :], in0=xs_tm[:, :, :], in1=dtb_, op=ALU.mult),
                  reads=[xs_tm, dtt], writes=[xd])
            kb.op("pool", lambda g_: g_.tensor_tensor(out=xdd[:, :, :], in0=xs_tm[:, :, :], in1=dst_, op=ALU.mult),
                  reads=[xs_tm, dst], writes=[xdd])
            yp = ypre[i % 2]
            def S1a(g):
                hs = slice(4 * g, 4 * g + 4)
                cbm, decT = cbm2[g % 2], decT2[g % 2]
                kb.mm(cbp[:, 0:128], xbcT[:, 16 + g, :], xbcT[:, 24 + g, :], start=True, stop=True, reads=[xbcT], writes=[cbp])
                kb.op("dve", lambda v: v.tensor_tensor(out=cbm[:, :], in0=cbp[:, 0:128], in1=triU[:, :], op=ALU.mult),
                      reads=[cbp, triU], writes=[cbm])
                kb.op("pool", lambda g_: g_.tensor_tensor(out=Rm[:, :, :], in0=triU[:, :].unsqueeze(1).broadcast_to([128, 4, 128]),
                                                          in1=ad[:, hs].unsqueeze(2).broadcast_to([128, 4, 128]), op=ALU.mult),
                      reads=[triU, ad], writes=[Rm])
                kb.mm(df[:, :], strictL[:, :], Rm[:, :, :].rearrange("p r l -> p (r l)"), start=True, stop=True,
                      reads=[strictL, Rm], writes=[df])
                kb.op("act", lambda a: a.activation(out=decT[:, :, :].rearrange("p r l -> p (r l)"), in_=df[:, :], func=AF.Exp),
                      reads=[df], writes=[decT])

            def S1b(g):
                cbm, decT, MT = cbm2[g % 2], decT2[g % 2], MT2[g % 2]
                kb.op("dve", lambda v: v.tensor_tensor(out=MT[:, :, :], in0=decT[:, :, :],
                                                       in1=cbm[:, :].unsqueeze(1).broadcast_to([128, 4, 128]), op=ALU.mult),
                      reads=[decT, cbm], writes=[MT])

            def S2(g):
                hs = slice(4 * g, 4 * g + 4)
                MT = MT2[g % 2]
                for r in range(4):
                    kb.mm(yy[:, r * 64:(r + 1) * 64], MT[:, r, :], xd[:, 4 * g + r, :], start=True, stop=True,
                          reads=[MT, xd], writes=[yy], sig=False)
                kb.mm(yy[:, 256:512], xbcT[:, 24 + g, :], prevb[:, hs, :].rearrange("p h d -> p (h d)"), start=True, stop=True,
                      reads=[xbcT, prevb], writes=[yy])
                ea_ = eacum[:, hs].unsqueeze(2).broadcast_to([128, 4, 64])
                kb.op("dve", lambda v: v.tensor_tensor(out=t1[:, :, :], in0=yy[:, 256:512].rearrange("p (h d) -> p h d", d=64),
                                                       in1=ea_, op=ALU.mult), reads=[yy, eacum], writes=[t1])
                kb.op("dve", lambda v: v.tensor_tensor(out=t1[:, :, :], in0=t1[:, :, :],
                                                       in1=yy[:, 0:256].rearrange("p (h d) -> p h d", d=64), op=ALU.add),
                      reads=[yy, t1], writes=[t1])
                kb.op("pool", lambda g_: g_.tensor_tensor(out=t2[:, :, :], in0=xs_tm[:, hs, :],
                                                          in1=dsk[:, hs].unsqueeze(2).broadcast_to([128, 4, 64]), op=ALU.mult),
                      reads=[xs_tm, dsk], writes=[t2])
                kb.op("dve", lambda v: v.tensor_tensor(out=yp[:, hs, :], in0=t1[:, :, :], in1=t2[:, :, :], op=ALU.add),
                      reads=[t1, t2], writes=[yp])
                kb.mm(stp_[:, 0:256], B_tm[:, g, :], xdd[:, hs, :].rearrange("p h d -> p (h d)"), start=True, stop=True,
                      reads=[B_tm, xdd], writes=[stp_])
                kb.op("dve", lambda v: v.tensor_tensor(out=prev[:, hs, :], in0=prev[:, hs, :],
                                                       in1=cdec[:, hs].unsqueeze(2).broadcast_to([128, 4, 64]), op=ALU.mult),
                      reads=[prev, cdec], writes=[prev])
                kb.op("dve", lambda v: v.tensor_tensor(out=prev[:, hs, :], in0=prev[:, hs, :],
                                                       in1=stp_[:, 0:256].rearrange("p (h d) -> p h d", d=64), op=ALU.add),
                      reads=[prev, stp_], writes=[prev])
                kb.op("pool", lambda g_: g_.tensor_copy(out=prevb[:, hs, :], in_=prev[:, hs, :]), reads=[prev], writes=[prevb])

            S1a(0)
            S1b(0)
            for g in range(8):
                if g + 1 < 8:
                    S1a(g + 1)
                S2(g)
                if g + 1 < 8:
                    S1b(g + 1)
            kb.dma("sp", ydram[i * 128:(i + 1) * 128, :], yp[:, :, :].rearrange("p h d -> p (h d)"), reads=[yp])
    with kb.phase("mamba_b"):
        lnw = load_vec_pk(kb, "lnw", P.inp("ln_mix")[layer], 8)
        wz = kb.sb("wz", [128, 8, 2048], BF16)
        for k in range(8):
            kb.dma("pool", wz[:, k, :], w_in[k * 128:(k + 1) * 128, 0:2048], writes=[wz])
        wout = load_w_bf16(kb, "wout", P.inp("ssm_w_out")[inst], 16, D)
        nw = load_bcast(kb, "nw", P.inp("ssm_norm_w")[inst], 2048)
        pj = [kb.ps("pj", [128, 512], F32) for _ in range(2)]
        tp = kb.ps("tp", [128, 1024], BF16)
        nb = NormBufs(kb, 128)
        hb = [kb.sb("h", [128, 8, 128], F32) for _ in range(3)]
        yb = [kb.sb("yb", [128, 2048], F32) for _ in range(2)]
        xn = kb.sb("xn", [128, 8, 128], BF16)
        zs = kb.sb("zs", [128, 2048], F32)
        sq = kb.sb("sq2", [128, 2048], F32)
        gss = kb.sb("gss", [128, 8], F32)
        gnb2 = [kb.sb("gnb", [128, 2048], BF16) for _ in range(2)]
        oT = kb.sb("oT", [128, 16, 128], BF16)
        hout = [kb.sb("hout", [128, 8, 128], F32) for _ in range(2)]

        xn2 = [xn, kb.sb("xnb", [128, 8, 128], BF16)]
        zs2 = [zs, kb.sb("zsb", [128, 2048], F32)]

        def load_hh(i):
            kb.dma("sp", hb[i % 3][:, :, :], hview(hT, i * 128, 128), writes=[hb[i % 3]])

        def load_y(i):
            kb.dma("sp", yb[i % 2][:, :], ydram[i * 128:(i + 1) * 128, :], writes=[yb[i % 2]])

        def front(i):
            h = hb[i % 3]
            xn_ = xn2[i % 2]
            zs_ = zs2[i % 2]
            emit_norm(kb, C, nb, h, lnw, xn_, 128)
            for ct in range(4):
                ps = pj[ct % 2]
                for k in range(8):
                    kb.mm(ps[:, :], xn_[:, k, :], wz[:, k, ct * 512:(ct + 1) * 512], start=(k == 0), stop=(k == 7),
                          reads=[xn_, wz], writes=[ps])
                kb.op("act", lambda a: a.activation(out=zs_[:, ct * 512:(ct + 1) * 512], in_=ps[:, :], func=AF.Silu),
                      reads=[ps], writes=[zs_])

        def mid(i):
            y = yb[i % 2]
            zs_ = zs2[i % 2]
            gnb = gnb2[i % 2]
            kb.op("dve", lambda v: v.tensor_tensor(out=zs_[:, :], in0=zs_[:, :], in1=y[:, :], op=ALU.mult),
                  reads=[zs_, y], writes=[zs_])
            kb.op("act", lambda a: a.activation(out=sq[:, :], in_=zs_[:, :], func=AF.Square), reads=[zs_], writes=[sq])
            kb.op("dve", lambda v: v.tensor_reduce(out=gss[:, :], in_=sq[:, :].rearrange("p (g c) -> p g c", g=8),
                                                   axis=AX.X, op=ALU.add), reads=[sq], writes=[gss])
            kb.op("act", lambda a: a.activation(out=gss[:, :], in_=gss[:, :], func=AF.Sqrt, scale=1.0 / 256,
                                                bias=C["eps"][:, :]), reads=[gss, C["eps"]], writes=[gss])
            kb.op("dve", lambda v: v.reciprocal(out=gss[:, :], in_=gss[:, :]), reads=[gss], writes=[gss])
            kb.op("dve", lambda v: v.tensor_tensor(out=sq[:, :].rearrange("p (g c) -> p g c", g=8),
                                                   in0=zs_[:, :].rearrange("p (g c) -> p g c", g=8),
                                                   in1=gss[:, :].unsqueeze(2).broadcast_to([128, 8, 256]), op=ALU.mult),
                  reads=[zs_, gss], writes=[sq])
            kb.op("pool", lambda g_: g_.tensor_tensor(out=gnb[:, :], in0=sq[:, :], in1=nw[:, :], op=ALU.mult),
                  reads=[sq, nw], writes=[gnb])

        def tail(i):
            h = hb[i % 3]
            gnb = gnb2[i % 2]
            for rd in range(2):
                for j in range(8):
                    k = rd * 8 + j
                    kb.tr(tp[:, j * 128:(j + 1) * 128], gnb[:, k * 128:(k + 1) * 128], C["ident_b"][:, :],
                          reads=[gnb, C["ident_b"]], writes=[tp], sig=(j == 7))
                kb.op("act", lambda a: a.copy(out=oT[:, rd * 8:(rd + 1) * 8, :], in_=tp[:, :].rearrange("p (k t) -> p k t", k=8)),
                      reads=[tp], writes=[oT])
            ho = hout[i % 2]
            for half in range(2):
                ps = pj[half]
                for dcl in range(4):
                    dc = half * 4 + dcl
                    for k in range(16):
                        kb.mm(ps[:, dcl * 128:(dcl + 1) * 128], wout[:, k, dc * 128:(dc + 1) * 128], oT[:, k, :],
                              start=(k == 0), stop=(k == 15), reads=[wout, oT], writes=[ps], sig=(k == 15 and dcl == 3))
                kb.op("dve", lambda v: v.tensor_tensor(out=ho[:, half * 4:(half + 1) * 4, :],
                                                       in0=ps[:, :].rearrange("p (c t) -> p c t", c=4),
                                                       in1=h[:, half * 4:(half + 1) * 4, :], op=ALU.add),
                      reads=[ps, h], writes=[ho])
            kb.dma("sp", hview(hT, i * 128, 128), ho[:, :, :], reads=[ho])

        load_hh(0)
        load_hh(1)
        load_y(0)
        front(0)
        for i in range(NCH + 1):
            if i + 1 < NCH:
                front(i + 1)
            if i < NCH:
                if i + 1 < NCH:
                    load_y(i + 1)
                mid(i)
            if i >= 1:
                tail(i - 1)
            if i + 2 < NCH:
                load_hh(i + 2)
DEPTH = 4
IN_SHAPES = {
    "x": ([S, D], "f"), "positions": ([S, 1], "i"),
    "ln_ffn1": ([4, D], "f"), "ffn1_w_in": ([4, D, 2 * DFF], "f"), "ffn1_w_out": ([4, DFF, D], "f"),
    "ln_mix": ([4, D], "f"), "ln_ffn2": ([4, D], "f"), "ffn2_w_in": ([4, D, 2 * DFF], "f"),
    "ffn2_w_out": ([4, DFF, D], "f"),
    "ssm_w_in": ([2, D, 6176], "f"), "ssm_conv_w": ([2, 4, 4096], "f"), "ssm_conv_b": ([2, 4096], "f"),
    "ssm_dt_bias": ([2, 32], "f"), "ssm_a_log": ([2, 32], "f"), "ssm_d": ([2, 32], "f"),
    "ssm_norm_w": ([2, 2048], "f"), "ssm_w_out": ([2, 2048, D], "f"),
    "swa_w_qkv": ([1, D, 1536], "f"), "swa_b_qkv": ([1, 1536], "f"), "swa_sinks": ([1, 16], "f"),
    "swa_w_o": ([1, D, D], "f"), "swa_b_o": ([1, D], "f"),
    "nsa_w_in": ([1, D, 2608], "f"), "nsa_pe_k": ([1, 32, 64], "f"), "nsa_k_w1": ([1, 2048, 256], "f"),
    "nsa_k_w2": ([1, 256, 64], "f"), "nsa_pe_v": ([1, 32, 64], "f"), "nsa_v_w1": ([1, 2048, 256], "f"),
    "nsa_v_w2": ([1, 256, 64], "f"), "nsa_w_o": ([1, D, D], "f"), "final_norm": ([D], "f"),
}


def host_consts():
    c = {}
    c["c_ident"] = np.eye(128, dtype=np.float32)
    attn_host_consts(c)
    mamba_host_consts(c)
    return c


class Prog:
    def __init__(self):
        self.nc = bass.Bass("TRN2", target_bir_lowering=False)
        self.ins = {}
        self.consts = host_consts()

    def inp(self, name):
        if name not in self.ins:
            if name in IN_SHAPES:
                shp, k = IN_SHAPES[name]
                dt = F32 if k == "f" else I32
            else:
                shp = list(self.consts[name].shape)
                dt = F32
            self.ins[name] = self.nc.dram_tensor(name, shp, dt, kind="ExternalInput").ap()
        return self.ins[name]


def build_program(plan=None):
    P = Prog()
    nc = P.nc
    if plan is None:
        plan = ["in"]
        for i in range(DEPTH):
            plan += [("ffn1", i), ("mix", i), ("ffn2", i)]
        plan += ["final"]
    out = nc.dram_tensor("out", [S, D], F32, kind="ExternalOutput").ap()
    hT = nc.dram_tensor("hT", [D, S], F32, kind="Internal").ap()
    kb = KB(nc)
    cin = {"c_ident": P.inp("c_ident")}
    C = setup_consts(kb, cin)
    for st in plan:
        if st == "in":
            ph_in_transpose(kb, C, P.inp("x"), hT)
        elif st == "final":
            ph_final(kb, C, hT, P.inp("final_norm"), out)
        elif st[0] in ("ffn1", "ffn2"):
            i = st[1]
            ph_ffn(kb, C, hT, P.inp("ln_" + st[0])[i], P.inp(st[0] + "_w_in")[i], P.inp(st[0] + "_w_out")[i])
        elif st[0] == "mix":
            i = st[1]
            kind, inst = i % 3, i // 3
            if kind == 0:
                ph_mamba(kb, C, P, hT, i, inst)
            elif kind == 1:
                ph_swa(kb, C, P, hT, i, inst)
            else:
                ph_nsa(kb, C, P, hT, i, inst)
    kb.barrier()
    kb.close()
    P.kb = kb
    return P


def make_in_maps(P, inputs, ncores=8):
    maps = []
    for c in range(ncores):
        m = {}
        for name in P.ins:
            if name in P.consts:
                m[name] = P.consts[name]
            elif name == "x":
                m[name] = np.ascontiguousarray(np.asarray(inputs["x"])[c])
            elif name == "positions":
                m[name] = np.ascontiguousarray(np.asarray(inputs["positions"])[c].reshape(S, 1)).astype(np.int32)
            else:
                m[name] = np.ascontiguousarray(np.asarray(inputs[name]))
        maps.append(m)
    return maps


_CACHE = {}


def kernel(**inputs):
    if "P" not in _CACHE:
        _CACHE["P"] = build_program()
    P = _CACHE["P"]
    maps = make_in_maps(P, inputs)
    res = run_bass_kernel_spmd(P.nc, maps, core_ids=list(range(8)))
    return np.stack([np.asarray(res.results[c]["out"]) for c in range(8)], axis=0).astype(np.float32)
```

```python
import numpy as np
from contextlib import ExitStack, contextmanager
import concourse.bass as bass
import concourse.mybir as mybir
from concourse.bass_utils import run_bass_kernel_spmd

F32 = mybir.dt.float32
BF16 = mybir.dt.bfloat16
I32 = mybir.dt.int32
AF = mybir.ActivationFunctionType
ALU = mybir.AluOpType
AX = mybir.AxisListType

SAME_ENG_RAW_SYNC = True


class Res:
    __slots__ = ("w", "r", "name")

    def __init__(self, name=""):
        self.w = {}
        self.r = {}
        self.name = name


class Tile:
    def __init__(self, t, name, nparts=0):
        self.t = t
        self.res = Res(name)
        self.parts = [Res(f"{name}.{i}") for i in range(nparts)]

    def __getitem__(self, idx):
        return self.t[idx]


def _res(x):
    if isinstance(x, Res):
        return [x]
    if isinstance(x, Tile):
        return [x.res] + x.parts
    raise TypeError(x)


class KB:
    ENGS = ("pe", "act", "dve", "pool", "sp")

    def __init__(self, nc, nslot=12):
        self.nc = nc
        self.stack = ExitStack()
        self.eng = {"pe": nc.tensor, "act": nc.scalar, "dve": nc.vector, "pool": nc.gpsimd, "sp": nc.sync}
        self.sem = {k: self.stack.enter_context(nc.semaphore("s_" + k)) for k in self.ENGS}
        self.cnt = {k: 0 for k in self.ENGS}
        self.seen = {k: {} for k in self.ENGS}
        self.nslot = nslot
        self.dq = ("sp", "pool", "act")
        self.dsem = {q: [self.stack.enter_context(nc.semaphore(f"d_{q}{i}")) for i in range(nslot)] for q in self.dq}
        self.dcnt = {q: [0] * nslot for q in self.dq}
        self.dnext = {q: 0 for q in self.dq}
        self.pstack = None
        self.uid = 0
        self.ninst = 0

    @contextmanager
    def phase(self, name=""):
        old = self.pstack
        self.pstack = ExitStack()
        self.nphase = getattr(self, "nphase", 0) + 1
        scope = self.nc.named_scope(f"ph{self.nphase:02d}_{name}")
        scope.__enter__()
        try:
            yield
        finally:
            self.barrier()
            scope.__exit__(None, None, None)
            self.pstack.close()
            self.pstack = old

    def _nm(self, name):
        self.uid += 1
        return f"{name}_{self.uid}"

    def sb(self, name, shape, dt, nparts=0, glob=False):
        nm = self._nm(name)
        st = self.stack if (glob or self.pstack is None) else self.pstack
        t = st.enter_context(self.nc.sbuf_tensor(nm, list(shape), dt))
        return Tile(t, nm, nparts)

    def ps(self, name, shape, dt=F32, nparts=0, glob=False):
        nm = self._nm(name)
        st = self.stack if (glob or self.pstack is None) else self.pstack
        t = st.enter_context(self.nc.psum_tensor(nm, list(shape), dt))
        return Tile(t, nm, nparts)

    def _semof(self, key):
        return self.sem[key] if isinstance(key, str) else self.dsem[key[1]][key[2]]

    def _wait(self, e, deps):
        need = {}
        for key, c in deps:
            if c <= 0:
                continue
            if c <= self.seen[e].get(key, 0):
                continue
            if need.get(key, 0) < c:
                need[key] = c
        for key, c in need.items():
            self.eng[e].wait_ge(self._semof(key), c)
            self.seen[e][key] = c
            self.ninst += 1

    def _deps(self, e, reads, writes, strict=False):
        deps = []
        for x in reads:
            for r in _res(x):
                for key, c in r.w.items():
                    if key == e and not strict:
                        if e == "pe" or not SAME_ENG_RAW_SYNC:
                            continue
                    deps.append((key, c))
        for x in writes:
            for r in _res(x):
                for key, c in list(r.w.items()) + list(r.r.items()):
                    if key == e and not strict:
                        if e == "pe" or not SAME_ENG_RAW_SYNC:
                            continue
                    deps.append((key, c))
        return deps

    def _mark(self, key, c, reads, writes):
        for x in reads:
            for r in _res(x):
                r.r[key] = c
        for x in writes:
            for r in _res(x):
                r.w[key] = c

    def op(self, e, fn, reads=(), writes=(), sig=True):
        self._wait(e, self._deps(e, reads, writes))
        ins = fn(self.eng[e])
        self.ninst += 1
        if sig:
            self.cnt[e] += 1
            ins.then_inc(self.sem[e], 1)
            c = self.cnt[e]
        else:
            c = self.cnt[e] + 1
        self._mark(e, c, reads, writes)
        return ins

    def mm(self, out, lhsT, rhs, start, stop, reads=(), writes=(), sig=None, **kw):
        if sig is None:
            sig = stop
        return self.op("pe", lambda pe: pe.matmul(out, lhsT=lhsT, rhs=rhs, start=start, stop=stop, **kw),
                       reads, writes, sig=sig)

    def tr(self, out, in_, ident, reads=(), writes=(), sig=True):
        return self.op("pe", lambda pe: pe.transpose(out, in_, ident), reads, writes, sig=sig)

    def dma(self, q, out, in_, reads=(), writes=(), **kw):
        i = self.dnext[q]
        self.dnext[q] = (i + 1) % self.nslot
        key = ("d", q, i)
        deps = self._deps(q, reads, writes, strict=True) + [(key, self.dcnt[q][i])]
        self._wait(q, deps)
        ins = self.eng[q].dma_start(out=out, in_=in_, **kw)
        self.ninst += 1
        self.dcnt[q][i] += 16
        ins.then_inc(self.dsem[q][i], 16)
        self._mark(key, self.dcnt[q][i], reads, writes)
        return ins

    def barrier(self, engines=None):
        allk = [(k, self.cnt[k]) for k in self.ENGS]
        for q in self.dq:
            for i in range(self.nslot):
                allk.append((("d", q, i), self.dcnt[q][i]))
        for e in (engines or self.ENGS):
            self._wait(e, [(k, c) for k, c in allk if k != e])

    def close(self):
        self.stack.close()
S = 4096
D = 1024
DFF = 2816
EPS = 1e-6
NCH = S // 128


def hview(hT, t0, n):
    return hT[:, t0:t0 + n].rearrange("(k p) t -> p k t", p=128)


def setup_consts(kb, cin):
    C = {}
    C["ident_f"] = kb.sb("ident_f", [128, 128], F32, glob=True)
    kb.dma("sp", C["ident_f"][:, :], cin["c_ident"][:, :], writes=[C["ident_f"]])
    C["ident_b"] = kb.sb("ident_b", [128, 128], BF16, glob=True)
    kb.op("dve", lambda v: v.tensor_copy(out=C["ident_b"][:, :], in_=C["ident_f"][:, :]),
          reads=[C["ident_f"]], writes=[C["ident_b"]])
    C["ones_b"] = kb.sb("ones_b", [128, 128], BF16, glob=True)
    kb.op("dve", lambda v: v.memset(C["ones_b"][:, :], 1.0), writes=[C["ones_b"]])
    C["ones_f"] = kb.sb("ones_f", [128, 128], F32, glob=True)
    kb.op("dve", lambda v: v.memset(C["ones_f"][:, :], 1.0), writes=[C["ones_f"]])
    C["eps"] = kb.sb("eps", [128, 1], F32, glob=True)
    kb.op("dve", lambda v: v.memset(C["eps"][:, :], EPS), writes=[C["eps"]])
    return C


def load_vec_pk(kb, name, ap1d, nk):
    t = kb.sb(name, [128, nk], F32)
    kb.dma("sp", t[:, :], ap1d.rearrange("(k p) -> p k", p=128), writes=[t], allow_slow_non_contiguous=True)
    return t


class NormBufs:
    def __init__(self, kb, n=512, ss=None, off=0):
        self.sq = [kb.sb("sq", [128, n], BF16) for _ in range(2)]
        self.ss_t = ss if ss is not None else kb.ps("ss", [128, n], F32)
        self.off = off
        self.rstd = kb.sb("rstd", [128, n], F32)
        self.i = 0


def emit_norm(kb, C, nb, h, lnw, xn, n, hoff=0, xoff=0):
    hs = slice(hoff, hoff + n)
    xs = slice(xoff, xoff + n)
    for k in range(8):
        sq = nb.sq[nb.i % 2]
        nb.i += 1
        kb.op("act", lambda a: a.activation(out=sq[:, :n], in_=h[:, k, hs], func=AF.Square), reads=[h], writes=[sq])
        kb.mm(nb.ss_t[:, nb.off:nb.off + n], C["ones_b"][:, :], sq[:, :n], start=(k == 0), stop=(k == 7),
              reads=[sq, C["ones_b"]], writes=[nb.ss_t], sig=True)
    kb.op("act", lambda a: a.activation(out=nb.rstd[:, :n], in_=nb.ss_t[:, nb.off:nb.off + n], func=AF.Sqrt,
                                        scale=1.0 / D, bias=C["eps"][:, :]),
          reads=[nb.ss_t, C["eps"]], writes=[nb.rstd])
    kb.op("dve", lambda v: v.reciprocal(out=nb.rstd[:, :n], in_=nb.rstd[:, :n]), reads=[nb.rstd], writes=[nb.rstd])
    for k in range(8):
        kb.op("dve", lambda v: v.scalar_tensor_tensor(out=xn[:, k, xs], in0=h[:, k, hs], scalar=lnw[:, k:k + 1],
                                                      in1=nb.rstd[:, :n], op0=ALU.mult, op1=ALU.mult),
              reads=[h, lnw, nb.rstd], writes=[xn])


def ph_in_transpose(kb, C, x_ap, hT):
    with kb.phase("in_t"):
        xb = [kb.sb("xin", [128, D], F32) for _ in range(2)]
        tps = [kb.ps("tps", [128, 512], F32) for _ in range(2)]
        st = [kb.sb("xst", [128, 8, 128], F32) for _ in range(2)]
        for c in range(NCH):
            xt = xb[c % 2]
            kb.dma("sp", xt[:, :], x_ap[c * 128:(c + 1) * 128, :], writes=[xt])
            s = st[c % 2]
            for g4 in range(2):
                ps = tps[g4]
                for k in range(4):
                    kb.tr(ps[:, k * 128:(k + 1) * 128], xt[:, (g4 * 4 + k) * 128:(g4 * 4 + k + 1) * 128],
                          C["ident_f"][:, :], reads=[xt, C["ident_f"]], writes=[ps], sig=(k == 3))
                e = "act" if g4 == 0 else "dve"
                if e == "act":
                    kb.op(e, lambda a: a.copy(out=s[:, g4 * 4:(g4 + 1) * 4, :],
                                              in_=ps[:, :].rearrange("p (k t) -> p k t", k=4)),
                          reads=[ps], writes=[s])
                else:
                    kb.op(e, lambda v: v.tensor_copy(out=s[:, g4 * 4:(g4 + 1) * 4, :],
                                                     in_=ps[:, :].rearrange("p (k t) -> p k t", k=4)),
                          reads=[ps], writes=[s])
            kb.dma("pool", hview(hT, c * 128, 128), s[:, :, :], reads=[s])


def ph_ffn(kb, C, hT, lnw_ap, w_in_ap, w_out_ap):
    T = 1024
    NT = S // T
    NH = T // 512
    NJ = DFF // 128
    with kb.phase("ffn"):
        lnw = load_vec_pk(kb, "lnw", lnw_ap, 8)
        hbuf = [kb.sb("h", [128, 8, T], F32) for _ in range(2)]
        xn = kb.sb("xn", [128, 8, T], BF16)
        act = kb.sb("act", [128, NJ, T], BF16)
        nb = NormBufs(kb)
        NW = 3
        wbuf = [kb.sb("win", [128, 8, 2, 128], BF16) for _ in range(NW)]
        NO = 3
        wobuf = [kb.sb("wo", [128, NJ, 128], BF16) for _ in range(NO)]
        gps = [kb.ps("g", [128, 512], F32) for _ in range(2)]
        ups = [kb.ps("u", [128, 512], F32) for _ in range(2)]
        ops_ = [kb.ps("o", [128, 512], F32) for _ in range(2)]
        sgb = [kb.sb("sg", [128, 512], F32) for _ in range(2)]
        hob = [kb.sb("ho", [128, 512], F32) for _ in range(2)]

        def load_h(tt):
            kb.dma("sp", hbuf[tt % 2][:, :, :], hview(hT, tt * T, T), writes=[hbuf[tt % 2]])

        win_jobs = [(tt, j) for tt in range(NT) for j in range(NJ)]
        wo_jobs = [(tt, dc) for tt in range(NT) for dc in range(8)]
        st = {"wi": 0, "wo": 0}

        def issue_win(upto):
            while st["wi"] <= upto and st["wi"] < len(win_jobs):
                _, j = win_jobs[st["wi"]]
                w = wbuf[st["wi"] % NW]
                for gu in range(2):
                    c0 = gu * DFF + j * 128
                    kb.dma("pool", w[:, :, gu, :], w_in_ap[:, c0:c0 + 128].rearrange("(k p) c -> p k c", p=128),
                           writes=[w])
                st["wi"] += 1

        def issue_wo(upto):
            while st["wo"] <= upto and st["wo"] < len(wo_jobs):
                _, dc = wo_jobs[st["wo"]]
                w = wobuf[st["wo"] % NO]
                kb.dma("pool", w[:, :, :], w_out_ap[:, dc * 128:(dc + 1) * 128].rearrange("(j p) c -> p j c", p=128),
                       writes=[w])
                st["wo"] += 1

        load_h(0)
        issue_win(1)
        cnt = 0
        for tt in range(NT):
            if tt + 1 < NT:
                load_h(tt + 1)
            h = hbuf[tt % 2]
            for hf in range(NH):
                emit_norm(kb, C, nb, h, lnw, xn, 512, hoff=hf * 512, xoff=hf * 512)
            for j in range(NJ):
                ji = tt * NJ + j
                issue_win(ji + 2)
                if j == 8:
                    issue_wo(tt * 8)
                if j == 16:
                    issue_wo(tt * 8 + 1)
                w = wbuf[ji % NW]
                for hf in range(NH):
                    sl = slice(hf * 512, (hf + 1) * 512)
                    g_ps = gps[cnt % 2]
                    u_ps = ups[cnt % 2]
                    sg = sgb[cnt % 2]
                    cnt += 1
                    for k in range(8):
                        kb.mm(g_ps[:, :], w[:, k, 0, :], xn[:, k, sl], start=(k == 0), stop=(k == 7),
                              reads=[w, xn], writes=[g_ps])
                    for k in range(8):
                        kb.mm(u_ps[:, :], w[:, k, 1, :], xn[:, k, sl], start=(k == 0), stop=(k == 7),
                              reads=[w, xn], writes=[u_ps])
                    kb.op("act", lambda a: a.activation(out=sg[:, :], in_=g_ps[:, :], func=AF.Silu),
                          reads=[g_ps], writes=[sg])
                    kb.op("dve", lambda v: v.tensor_tensor(out=act[:, j, sl], in0=sg[:, :], in1=u_ps[:, :],
                                                           op=ALU.mult),
                          reads=[sg, u_ps], writes=[act])
            for dc in range(8):
                oi = tt * 8 + dc
                issue_wo(oi + 2)
                if dc >= 6:
                    issue_win((tt + 1) * NJ + (dc - 6))
                wo = wobuf[oi % NO]
                for hf in range(NH):
                    sl = slice(hf * 512, (hf + 1) * 512)
                    o_ps = ops_[cnt % 2]
                    ho = hob[cnt % 2]
                    cnt += 1
                    for j in range(NJ):
                        kb.mm(o_ps[:, :], wo[:, j, :], act[:, j, sl], start=(j == 0), stop=(j == NJ - 1),
                              reads=[wo, act], writes=[o_ps])
                    kb.op("dve", lambda v: v.scalar_tensor_tensor(out=ho[:, :], in0=o_ps[:, :], scalar=0.5,
                                                                  in1=h[:, dc, sl], op0=ALU.mult, op1=ALU.add),
                          reads=[o_ps, h], writes=[ho])
                    t0 = tt * T + hf * 512
                    kb.dma("sp", hT[dc * 128:(dc + 1) * 128, t0:t0 + 512], ho[:, :], reads=[ho])


def ph_final(kb, C, hT, fnw_ap, out_ap):
    with kb.phase("final"):
        lnw = load_vec_pk(kb, "fnw", fnw_ap, 8)
        hbuf = [kb.sb("h", [128, 8, 512], F32) for _ in range(2)]
        xn = [kb.sb("xnf", [128, 8, 512], F32) for _ in range(2)]
        nb = NormBufs(kb)
        tps = [kb.ps("tpo", [128, 512], F32) for _ in range(2)]
        ob = [kb.sb("ob", [128, D], F32) for _ in range(2)]
        n = 0
        for tt in range(S // 512):
            h = hbuf[tt % 2]
            kb.dma("sp", h[:, :, :], hview(hT, tt * 512, 512), writes=[h])
            x = xn[tt % 2]
            emit_norm(kb, C, nb, h, lnw, x, 512)
            for c in range(4):
                o = ob[n % 2]
                n += 1
                for g4 in range(2):
                    ps = tps[g4]
                    for k in range(4):
                        kk = g4 * 4 + k
                        kb.tr(ps[:, k * 128:(k + 1) * 128], x[:, kk, c * 128:(c + 1) * 128], C["ident_f"][:, :],
                              reads=[x, C["ident_f"]], writes=[ps], sig=(k == 3))
                    if g4 == 0:
                        kb.op("act", lambda a: a.copy(out=o[:, 0:512], in_=ps[:, :]), reads=[ps], writes=[o])
                    else:
                        kb.op("dve", lambda v: v.tensor_copy(out=o[:, 512:1024], in_=ps[:, :]), reads=[ps], writes=[o])
                t0 = tt * 512 + c * 128
                kb.dma("pool", out_ap[t0:t0 + 128, :], o[:, :], reads=[o])
NEG = -30000.0
BIG = 10000.0
SCALE = 0.125
TWO_PI = 6.283185307179586
CW1 = 6.28125
CW2 = TWO_PI - CW1
MAGIC = 12582912.0


def attn_host_consts(c):
    k = np.arange(128)[:, None]
    q = np.arange(128)[None, :]
    cur = np.where(k <= q, 0.0, NEG).astype(np.float32)
    prev = np.where(k > q, 0.0, NEG).astype(np.float32)
    c["c_mask_cur"] = np.tile(cur, (1, 4))
    c["c_mask_prev"] = np.tile(prev, (1, 4))
    invf = (500000.0 ** (-np.arange(0, 16, 2, dtype=np.float32) / 16.0)).astype(np.float32)
    c["c_invf"] = np.tile(invf[None, :], (128, 1)).astype(np.float32)
    L = np.zeros((32, 128), np.float32)
    for j in range(8):
        L[j] = np.where(16 * (j - 1) + 31 <= np.arange(128), 0.0, NEG)
    c["c_lneg"] = np.tile(L, (1, 4))
    E = np.zeros((32, 384), np.float32)
    for j in range(8):
        E[j, j + 127] = 1.0
    c["c_eshift"] = E
    X = np.zeros((64, 4096), np.float32)
    for kk in range(4096):
        X[kk // 64, kk] = 1.0
    c["c_expand"] = X
    T = np.zeros((128, 128), np.float32)
    for ql in range(128):
        cb = 1 if ql >= 64 else 0
        for y in range(128):
            jj = y - 64
            T[ql, y] = BIG if jj == cb else (-BIG if jj > cb else 0.0)
    c["c_ftab"] = T
    n_cmp = 255
    sel_lo = np.arange(64)[:, None] * 64
    cmp_lo = np.arange(n_cmp)[None, :] * 16
    ov = np.clip(np.minimum(sel_lo + 64, cmp_lo + 32) - np.maximum(sel_lo, cmp_lo), 0, None) / 32.0
    ovT = np.zeros((256, 64), np.float32)
    ovT[:n_cmp] = ov.T
    c["c_ovT"] = ovT


def ensure_rope(kb, C, P):
    if "cos" in C:
        return
    cos = kb.sb("cos", [128, 32, 8], F32, glob=True)
    sin = kb.sb("sin", [128, 32, 8], F32, glob=True)
    with kb.phase("rope"):
        pi_ = kb.sb("posi", [32, 128], I32)
        kb.dma("sp", pi_[:, :], P.inp("positions").rearrange("(c p) o -> c (p o)", p=128), writes=[pi_])
        pf = kb.sb("posf", [32, 128], F32)
        kb.op("dve", lambda v: v.tensor_copy(out=pf[:, :], in_=pi_[:, :]), reads=[pi_], writes=[pf])
        ps = kb.ps("pps", [128, 32], F32)
        kb.tr(ps[:, :], pf[:, :], C["ident_f"][:32, :32], reads=[pf, C["ident_f"]], writes=[ps])
        pt = kb.sb("post", [128, 32], F32)
        kb.op("dve", lambda v: v.tensor_copy(out=pt[:, :], in_=ps[:, :]), reads=[ps], writes=[pt])
        invf = kb.sb("invf", [128, 8], F32)
        kb.dma("sp", invf[:, :], P.inp("c_invf")[:, :], writes=[invf])
        ang = kb.sb("ang", [128, 32, 8], F32)
        kb.op("dve", lambda v: v.tensor_tensor(out=ang[:, :, :], in0=pt[:, :].unsqueeze(2).broadcast_to([128, 32, 8]),
                                               in1=invf[:, :].unsqueeze(1).broadcast_to([128, 32, 8]), op=ALU.mult),
              reads=[pt, invf], writes=[ang])
        a2 = ang[:, :, :].rearrange("p c f -> p (c f)")
        kf = kb.sb("kf", [128, 256], F32)
        r = kb.sb("rr", [128, 256], F32)
        m = kb.sb("mm", [128, 256], F32)
        kb.op("dve", lambda v: v.tensor_scalar(out=kf[:, :], in0=a2, scalar1=1.0 / TWO_PI, scalar2=MAGIC,
                                               op0=ALU.mult, op1=ALU.add), reads=[ang], writes=[kf])
        kb.op("dve", lambda v: v.tensor_scalar(out=kf[:, :], in0=kf[:, :], scalar1=-MAGIC, scalar2=None,
                                               op0=ALU.add), reads=[kf], writes=[kf])
        kb.op("dve", lambda v: v.scalar_tensor_tensor(out=r[:, :], in0=kf[:, :], scalar=-CW1, in1=a2,
                                                      op0=ALU.mult, op1=ALU.add), reads=[kf, ang], writes=[r])
        kb.op("dve", lambda v: v.scalar_tensor_tensor(out=r[:, :], in0=kf[:, :], scalar=-CW2, in1=r[:, :],
                                                      op0=ALU.mult, op1=ALU.add), reads=[kf, r], writes=[r])
        PI_C = 3.1415925
        kb.op("dve", lambda v: v.tensor_scalar(out=r[:, :], in0=r[:, :], scalar1=PI_C, scalar2=-PI_C,
                                               op0=ALU.min, op1=ALU.max), reads=[r], writes=[r])
        kb.op("act", lambda a: a.activation(out=sin[:, :, :].rearrange("p c f -> p (c f)"), in_=r[:, :], func=AF.Sin),
              reads=[r], writes=[sin])
        kb.op("dve", lambda v: v.tensor_scalar(out=m[:, :], in0=r[:, :], scalar1=np.pi / 2, scalar2=-TWO_PI,
                                               op0=ALU.is_gt, op1=ALU.mult), reads=[r], writes=[m])
        kb.op("dve", lambda v: v.scalar_tensor_tensor(out=m[:, :], in0=r[:, :], scalar=np.pi / 2, in1=m[:, :],
                                                      op0=ALU.add, op1=ALU.add), reads=[r, m], writes=[m])
        kb.op("dve", lambda v: v.tensor_scalar(out=m[:, :], in0=m[:, :], scalar1=PI_C, scalar2=-PI_C,
                                               op0=ALU.min, op1=ALU.max), reads=[m], writes=[m])
        kb.op("act", lambda a: a.activation(out=cos[:, :, :].rearrange("p c f -> p (c f)"), in_=m[:, :], func=AF.Sin),
              reads=[m], writes=[cos])
    C["cos"] = cos
    C["sin"] = sin


def load_w_bf16(kb, name, w_ap, nk, F, q="pool"):
    t = kb.sb(name, [128, nk, F], BF16)
    for k in range(nk):
        kb.dma(q, t[:, k, :], w_ap[k * 128:(k + 1) * 128, :], writes=[t])
    return t


def load_const_bf16(kb, name, ap, shape):
    t = kb.sb(name, shape, BF16)
    kb.dma("pool", t[:, :], ap[:, :], writes=[t])
    return t


def emit_rope(kb, C, x3, nshape, i, tmp, reads_writes):
    cosb = C["cos"][:, i, :]
    sinb = C["sin"][:, i, :]
    for _ in nshape:
        cosb = cosb.unsqueeze(1)
        sinb = sinb.unsqueeze(1)
    shp = [128] + list(nshape) + [8]
    cosb = cosb.broadcast_to(shp)
    sinb = sinb.broadcast_to(shp)
    pre = (slice(None),) * (1 + len(nshape))
    x1 = x3[pre + (slice(0, 8),)]
    x2 = x3[pre + (slice(8, 16),)]
    t1, t2, t3, t4 = tmp
    rw = reads_writes
    kb.op("dve", lambda v: v.tensor_tensor(out=t1, in0=x1, in1=cosb, op=ALU.mult), reads=[rw, C["cos"]], writes=[rw])
    kb.op("dve", lambda v: v.tensor_tensor(out=t2, in0=x2, in1=sinb, op=ALU.mult), reads=[rw, C["sin"]], writes=[rw])
    kb.op("dve", lambda v: v.tensor_tensor(out=t3, in0=x2, in1=cosb, op=ALU.mult), reads=[rw, C["cos"]], writes=[rw])
    kb.op("dve", lambda v: v.tensor_tensor(out=t4, in0=x1, in1=sinb, op=ALU.mult), reads=[rw, C["sin"]], writes=[rw])
    kb.op("dve", lambda v: v.tensor_tensor(out=x1, in0=t1, in1=t2, op=ALU.subtract), reads=[rw], writes=[rw])
    kb.op("dve", lambda v: v.tensor_tensor(out=x2, in0=t3, in1=t4, op=ALU.add), reads=[rw], writes=[rw])


def ph_nsa_pre(kb, C, P, hT, layer, inst):
    ensure_rope(kb, C, P)
    kcmpT = kb.sb("kcmpT", [64, 4, 256], BF16, glob=True)
    vcmp = kb.sb("vcmp", [128, 2, 4, 128], BF16, glob=True)
    w_in = P.inp("nsa_w_in")[inst]
    with kb.phase("nsa_pre"):
        lnw = load_vec_pk(kb, "lnw", P.inp("ln_mix")[layer], 8)
        wkv = kb.sb("wkv", [128, 8, 512], BF16)
        for k in range(8):
            kb.dma("pool", wkv[:, k, :], w_in[k * 128:(k + 1) * 128, 1024:1536], writes=[wkv])
        w1 = {}
        for nm in ("k", "v"):
            w1[nm] = kb.sb("w1" + nm, [64, 32, 256], BF16)
            kb.dma("pool", w1[nm][:, :, :], P.inp(f"nsa_{nm}_w1")[inst].rearrange("(l d) h -> d l h", d=64),
                   writes=[w1[nm]])
        w2 = {}
        for nm in ("k", "v"):
            w2[nm] = kb.sb("w2" + nm, [128, 2, 64], BF16)
            kb.dma("pool", w2[nm][:, :, :], P.inp(f"nsa_{nm}_w2")[inst].rearrange("(c p) d -> p c d", p=128),
                   writes=[w2[nm]])
        peT = {}
        aux = kb.ps("aux", [128, 512], F32)
        for nm in ("k", "v"):
            pe32 = kb.sb("pe32" + nm, [32, 64], F32)
            kb.dma("sp", pe32[:, :], P.inp(f"nsa_pe_{nm}")[inst][:, :], writes=[pe32])
            pps = aux
            kb.tr(pps[0:64, 0:32], pe32[:, :], C["ident_f"][:32, :32], reads=[pe32, C["ident_f"]], writes=[pps])
            peT[nm] = kb.sb("peT" + nm, [64, 32], BF16)
            kb.op("dve", lambda v: v.tensor_copy(out=peT[nm][:, :], in_=pps[0:64, 0:32]), reads=[pps], writes=[peT[nm]])
        kT = kb.sb("kcT", [64, 4, S], BF16)
        vT = kb.sb("vcT", [64, 4, S], BF16)
        hb = [kb.sb("h", [128, 8, 128], F32) for _ in range(2)]
        xn = kb.sb("xn", [128, 8, 128], BF16)
        nb = NormBufs(kb, 128)
        pj = [kb.ps("pj", [128, 512], F32) for _ in range(2)]
        kv = kb.sb("kv", [128, 512], F32)
        kvb = kb.sb("kvb", [128, 512], BF16)
        tmp = kb.sb("rt", [128, 4, 4, 8], F32)
        tp = [kb.ps("tp", [64, 512], BF16) for _ in range(2)]

        def load_h(i):
            kb.dma("sp", hb[i % 2][:, :, :], hview(hT, i * 128, 128), writes=[hb[i % 2]])
        load_h(0)
        for i in range(NCH):
            if i + 1 < NCH:
                load_h(i + 1)
            h = hb[i % 2]
            emit_norm(kb, C, nb, h, lnw, xn, 128)
            ps = pj[i % 2]
            for k in range(8):
                kb.mm(ps[:, :], xn[:, k, :], wkv[:, k, :], start=(k == 0), stop=(k == 7), reads=[xn, wkv], writes=[ps])
            kb.op("act", lambda a: a.copy(out=kv[:, :], in_=ps[:, :]), reads=[ps], writes=[kv])
            x3 = kv[:, 0:256].rearrange("p (h d) -> p h d", d=64)
            emit_rope(kb, C, x3, [4], i, [tmp[:, j, :, :] for j in range(4)], kv)
            kb.op("dve", lambda v: v.tensor_copy(out=kvb[:, :], in_=kv[:, :]), reads=[kv], writes=[kvb])
            for half, dst in ((0, kT), (1, vT)):
                t = tp[half]
                for g in range(4):
                    c0 = half * 256 + g * 64
                    kb.tr(t[:, g * 128:(g + 1) * 128], kvb[:, c0:c0 + 64], C["ident_b"][:, :],
                          reads=[kvb, C["ident_b"]], writes=[t], sig=(g == 3))
                eng = "act" if half == 0 else "pool"
                if half == 0:
                    kb.op("act", lambda a: a.copy(out=dst[:, :, i * 128:(i + 1) * 128],
                                                  in_=t[:, :].rearrange("p (g t) -> p g t", g=4)),
                          reads=[t], writes=[dst])
                else:
                    kb.op("dve", lambda v: v.tensor_copy(out=dst[:, :, i * 128:(i + 1) * 128],
                                                         in_=t[:, :].rearrange("p (g t) -> p g t", g=4)),
                          reads=[t], writes=[dst])
        hps = pj
        bps = aux
        ops_ = aux
        kb.op("dve", lambda v: v.memset(vcmp[:, :, :, :], 1.0), writes=[vcmp])
        kb.op("dve", lambda v: v.memset(kcmpT[:, :, :], 0.0), writes=[kcmpT])
        ov32 = kb.sb("ov32", [128, 2, 64], F32)
        kb.dma("sp", ov32[:, :, :], P.inp("c_ovT").rearrange("(c p) j -> p c j", p=128), writes=[ov32])
        for g in range(4):
            kb.op("dve", lambda v: v.tensor_copy(out=vcmp[:, :, g, 65:128], in_=ov32[:, :, 1:64]), reads=[ov32], writes=[vcmp])
        n = 0
        for nm, src in (("k", kT), ("v", vT)):
            bias = kb.sb("cb" + nm, [128, 2], F32)
            for hc in range(2):
                for l in range(32):
                    kb.mm(bps[:, 32 + hc:33 + hc], w1[nm][:, l, hc * 128:(hc + 1) * 128], peT[nm][:, l:l + 1],
                          start=(l == 0), stop=(l == 31), reads=[w1[nm], peT[nm]], writes=[bps])
            kb.op("dve", lambda v: v.tensor_copy(out=bias[:, :], in_=bps[:, 32:34]), reads=[bps], writes=[bias])
            for g in range(4):
                hid = kb.sb("hid", [128, 2, 256], BF16)
                for hc in range(2):
                    hp = hps[n % 2]
                    n += 1
                    for l in range(32):
                        rhs = src[:, g, :].rearrange("p (c s) -> p c s", s=16)[:, l // 16:l // 16 + 255, l % 16]
                        kb.mm(hp[:, 0:255], w1[nm][:, l, hc * 128:(hc + 1) * 128], rhs, start=(l == 0), stop=(l == 31),
                              reads=[w1[nm], src], writes=[hp])
                    kb.op("act", lambda a: a.activation(out=hid[:, hc, 0:255], in_=hp[:, 0:255], func=AF.Silu,
                                                        bias=bias[:, hc:hc + 1]),
                          reads=[hp, bias], writes=[hid])
                if nm == "k":
                    for hc in range(2):
                        kb.mm(ops_[0:64, 256:511], w2[nm][:, hc, :], hid[:, hc, 0:255], start=(hc == 0), stop=(hc == 1),
                              reads=[w2[nm], hid], writes=[ops_])
                    kb.op("dve", lambda v: v.tensor_copy(out=kcmpT[:, g, 0:255], in_=ops_[0:64, 256:511]),
                          reads=[ops_], writes=[kcmpT])
                else:
                    for cc in range(2):
                        M = 128 if cc == 0 else 127
                        for hc in range(2):
                            kb.mm(ops_[0:M, 256 + cc * 64:256 + (cc + 1) * 64], hid[:, hc, cc * 128:cc * 128 + M], w2[nm][:, hc, :],
                                  start=(hc == 0), stop=(hc == 1), reads=[w2[nm], hid], writes=[ops_])
                        kb.op("dve", lambda v: v.tensor_copy(out=vcmp[0:M, cc, g, 0:64], in_=ops_[0:M, 256 + cc * 64:256 + (cc + 1) * 64]),
                              reads=[ops_], writes=[vcmp])
    C["kcmpT"] = kcmpT
    C["vcmp"] = vcmp


def ph_attn(kb, C, P, hT, layer, inst, kind):
    ensure_rope(kb, C, P)
    nsa = kind == "nsa"
    if nsa:
        ph_nsa_pre(kb, C, P, hT, layer, inst)
        w_in = P.inp("nsa_w_in")[inst]
        F = 2608
        w_o = P.inp("nsa_w_o")[inst]
    else:
        w_in = P.inp("swa_w_qkv")[inst]
        F = 1536
        w_o = P.inp("swa_w_o")[inst]
    NCT = (F + 511) // 512
    with kb.phase("attn"):
        lnw = load_vec_pk(kb, "lnw", P.inp("ln_mix")[layer], 8)
        wq = load_w_bf16(kb, "wq", w_in, 8, F)
        wo = load_w_bf16(kb, "wo", w_o, 8, D)
        mcur = load_const_bf16(kb, "mcur", P.inp("c_mask_cur"), [128, 512])
        mprev = load_const_bf16(kb, "mprev", P.inp("c_mask_prev"), [128, 512])
        if nsa:
            lneg = load_const_bf16(kb, "lneg", P.inp("c_lneg"), [32, 512])
            esh = load_const_bf16(kb, "esh", P.inp("c_eshift"), [32, 384])
            ftab = kb.sb("ftab", [128, 128], F32)
            kb.dma("sp", ftab[:, :], P.inp("c_ftab")[:, :], writes=[ftab])
            kcmpT, vcmp = C["kcmpT"], C["vcmp"]
            branches = ["sel", "win"]
            kcol = {"sel": 1536, "win": 2048}
            vcol = {"sel": 1792, "win": 2304}
            nback = {"sel": 10 ** 6, "win": 4}
        else:
            bias = kb.sb("bqkv", [128, F], F32)
            kb.dma("sp", bias[:, :], P.inp("swa_b_qkv")[inst].partition_broadcast(128), writes=[bias])
            bo = load_vec_pk(kb, "bo", P.inp("swa_b_o")[inst], 8)
            esink = kb.sb("esink", [128, 16], F32)
            kb.dma("sp", esink[:, :], P.inp("swa_sinks")[inst].partition_broadcast(128), writes=[esink])
            kb.op("act", lambda a: a.activation(out=esink[:, :], in_=esink[:, :], func=AF.Exp), reads=[esink], writes=[esink])
            branches = ["win"]
            kcol = {"win": 1024}
            vcol = {"win": 1280}
            nback = {"win": 1}
        ring = {b: (8 if b == "win" else NCH) for b in branches}
        KT = {b: kb.sb("KT" + b, [128 if b == "sel" else 64, 4, ring[b] * 128], BF16) for b in branches}
        if nsa:
            for g in range(4):
                kb.dma("pool", KT["sel"][64:128, g, :], P.inp("c_expand")[:, :], writes=[KT["sel"]])
            qsel_res = [Res("qsel%d" % g) for g in range(4)]
        VC = {b: kb.sb("VC" + b, [128, ring[b], 4, 65], BF16) for b in branches}
        for b in branches:
            kb.op("pool", lambda g_: g_.memset(VC[b][:, :, :, :], 1.0), writes=[VC[b]])
        hb = [kb.sb("h", [128, 8, 128], F32) for _ in range(2)]
        xn = kb.sb("xn", [128, 8, 128], BF16)
        nb = NormBufs(kb, 128)
        pj = [kb.ps("pj", [128, 512], F32) for _ in range(2)]
        tpp = kb.ps("tp", [128, 1024], BF16)
        stp = [kb.ps("st", [128, 512], F32) for _ in range(2)]
        opp = [kb.ps("op", [128, 512], F32) for _ in range(2)]
        qkv = kb.sb("qkv", [128, F], F32)
        qkb = kb.sb("qkb", [128, F], BF16)
        QT = kb.sb("QT", [128, 16, 128], BF16)
        rtq = kb.sb("rtq", [128, 4, 16, 8], F32)
        rtk = kb.sb("rtk", [128, 4, 3, 4, 8], F32)
        pT = [kb.sb("pT", [128, 512], BF16) for _ in range(4)]
        osb = kb.sb("osb", [128, 16, 64], F32)
        osbb = kb.sb("osbb", [128, D], BF16)
        oT = kb.sb("oT", [128, 8, 128], BF16)
        otmp = kb.sb("otmp", [128, 4, 64], F32)
        den = kb.sb("den", [128, 4], F32)
        coef = kb.sb("coef", [128, 4], F32)
        hout = [kb.sb("hout", [128, 8, 128], F32) for _ in range(2)]
        if nsa:
            gates = kb.sb("gates", [128, 48], F32)
            imp = kb.sb("imp", [128, 64], F32)
            top8 = kb.sb("top8", [128, 8], F32)
            nselb = kb.sb("nselb", [128, 64], BF16)
        st = {"p": 0, "s": 0, "o": 0}

        def load_h(i):
            kb.dma("sp", hb[i % 2][:, :, :], hview(hT, i * 128, 128), writes=[hb[i % 2]])

        def make_unit(items, lhsT_k, M, rhs_q, masks, v_rhs, ncol, o_ref, kreads, vreads, first_in_group, post):
            slot = {}

            def A():
                sps = stp[st["s"] % 2]
                st["s"] += 1
                pt = pT[st["p"] % 4]
                st["p"] += 1
                slot["pt"] = pt
                kb.mm(sps[0:M, :], lhsT_k, rhs_q, start=True, stop=(len(masks) == 0), reads=kreads + [QT], writes=[sps])
                for mi, (ml, mr, mrd) in enumerate(masks):
                    kb.mm(sps[0:M, :], ml, mr, start=False, stop=(mi == len(masks) - 1), reads=mrd, writes=[sps])
                kb.op("act", lambda a: a.activation(out=pt[0:M, :], in_=sps[0:M, :], func=AF.Exp, scale=SCALE),
                      reads=[sps], writes=[pt])

            def B():
                o_ps = o_ref["t"]
                pt = slot["pt"]
                if first_in_group:
                    kb.op("dve", lambda v: v.memset(o_ps[:, :], 0.0), writes=[o_ps])
                for r in range(4):
                    kb.mm(o_ps[:, r * 128:r * 128 + ncol], pt[0:M, r * 128:(r + 1) * 128], v_rhs, start=False, stop=False,
                          reads=[pt] + vreads, writes=[o_ps], sig=(r == 3), skip_group_check=True)
                if post is not None:
                    post()
            items.append((A, B))

        def run_items(items):
            LA = 1
            for n, (A, B) in enumerate(items):
                A()
                if n >= LA:
                    items[n - LA][1]()
            for n in range(max(0, len(items) - LA), len(items)):
                items[n][1]()

        def new_ref():
            ref = {"t": opp[st["o"] % 2]}
            st["o"] += 1
            return ref

        def evac(o_ps, g, ncol, first, gate_col, extra_den=None):
            o3 = o_ps[:, :].rearrange("p (r c) -> p r c", c=128)
            kb.op("dve", lambda v: v.tensor_scalar(out=den[:, :].unsqueeze(2), in0=o3[:, :, 64:65], scalar1=1e-30, scalar2=None,
                                                   op0=ALU.max), reads=[o_ps], writes=[den])
            if extra_den is not None:
                kb.op("dve", lambda v: v.tensor_tensor(out=den[:, :], in0=den[:, :], in1=extra_den, op=ALU.add),
                      reads=[den, esink], writes=[den])
            kb.op("dve", lambda v: v.reciprocal(out=den[:, :], in_=den[:, :]), reads=[den], writes=[den])
            if gate_col is not None:
                gv = gates[:, :].rearrange("p (h b) -> p h b", b=3)[:, 4 * g:4 * g + 4, gate_col]
                kb.op("dve", lambda v: v.tensor_tensor(out=coef[:, :], in0=den[:, :], in1=gv, op=ALU.mult),
                      reads=[den, gates], writes=[coef])
                cf = coef
            else:
                cf = den
            cb_ = cf[:, :].unsqueeze(2).broadcast_to([128, 4, 64])
            if first:
                kb.op("dve", lambda v: v.tensor_tensor(out=osb[:, 4 * g:4 * g + 4, :], in0=o3[:, :, 0:64], in1=cb_,
                                                       op=ALU.mult), reads=[o_ps, cf], writes=[osb])
            else:
                kb.op("dve", lambda v: v.tensor_tensor(out=otmp[:, :, :], in0=o3[:, :, 0:64], in1=cb_, op=ALU.mult),
                      reads=[o_ps, cf], writes=[otmp])
                kb.op("dve", lambda v: v.tensor_tensor(out=osb[:, 4 * g:4 * g + 4, :], in0=osb[:, 4 * g:4 * g + 4, :],
                                                       in1=otmp[:, :, :], op=ALU.add), reads=[otmp, osb], writes=[osb])

        load_h(0)
        for i in range(NCH):
            if i + 1 < NCH:
                load_h(i + 1)
            h = hb[i % 2]
            emit_norm(kb, C, nb, h, lnw, xn, 128)
            for ct in range(NCT):
                c0 = ct * 512
                c1 = min(F, c0 + 512)
                ps = pj[ct % 2]
                for k in range(8):
                    kb.mm(ps[:, 0:c1 - c0], xn[:, k, :], wq[:, k, c0:c1], start=(k == 0), stop=(k == 7),
                          reads=[xn, wq], writes=[ps])
                if nsa:
                    kb.op("act", lambda a: a.copy(out=qkv[:, c0:c1], in_=ps[:, 0:c1 - c0]), reads=[ps], writes=[qkv])
                else:
                    kb.op("dve", lambda v: v.tensor_tensor(out=qkv[:, c0:c1], in0=ps[:, 0:c1 - c0], in1=bias[:, c0:c1],
                                                           op=ALU.add), reads=[ps, bias], writes=[qkv])
            q3 = qkv[:, 0:1024].rearrange("p (h d) -> p h d", d=64)
            emit_rope(kb, C, q3, [16], i, [rtq[:, j, :, :] for j in range(4)], qkv)
            if nsa:
                k4 = qkv[:, 1024:2560].rearrange("p (b h d) -> p b h d", b=3, d=64)[:, :, 0:4, :]
                emit_rope(kb, C, k4, [3, 4], i, [rtk[:, j, :, :, :] for j in range(4)], qkv)
                kb.op("act", lambda a: a.activation(out=gates[:, :], in_=qkv[:, 2560:2608], func=AF.Sigmoid),
                      reads=[qkv], writes=[gates])
            else:
                k3 = qkv[:, 1024:1280].rearrange("p (h d) -> p h d", d=64)
                emit_rope(kb, C, k3, [4], i, [rtk[:, j, 0, :, :] for j in range(4)], qkv)
            kb.op("act", lambda a: a.copy(out=qkb[:, 0:F], in_=qkv[:, 0:F]), reads=[qkv], writes=[qkb])
            for hh in range(2):
                for j in range(8):
                    hd = hh * 8 + j
                    kb.tr(tpp[0:64, j * 128:(j + 1) * 128], qkb[:, hd * 64:(hd + 1) * 64], C["ident_b"][:, :],
                          reads=[qkb, C["ident_b"]], writes=[tpp], sig=(j == 7))
                kb.op("act", lambda a: a.copy(out=QT[0:64, hh * 8:(hh + 1) * 8, :],
                                              in_=tpp[0:64, :].rearrange("p (j t) -> p j t", j=8)),
                      reads=[tpp], writes=[QT])
            for bi, b in enumerate(branches):
                for g in range(4):
                    c0 = kcol[b] + g * 64
                    kb.tr(tpp[0:64, (bi * 4 + g) * 128:(bi * 4 + g + 1) * 128], qkb[:, c0:c0 + 64], C["ident_b"][:, :],
                          reads=[qkb, C["ident_b"]], writes=[tpp], sig=(g == 3))
                isl = i % ring[b]
                kb.op("act", lambda a: a.copy(out=KT[b][0:64, :, isl * 128:(isl + 1) * 128],
                                              in_=tpp[0:64, bi * 512:(bi + 1) * 512].rearrange("p (g t) -> p g t", g=4)),
                      reads=[tpp], writes=[KT[b]])
                kb.op("pool", lambda g_: g_.tensor_copy(out=VC[b][:, isl, :, 0:64],
                                                        in_=qkv[:, vcol[b]:vcol[b] + 256].rearrange("p (g d) -> p g d", d=64)),
                      reads=[qkv], writes=[VC[b]])
            items = []

            def selection(g, o_ps):
                o3 = o_ps[:, :].rearrange("p (r c) -> p r c", c=128)
                nst = QT[64:128, 4 * g:4 * g + 4, :]
                kb.op("dve", lambda v: v.memset(imp[:, 0:1], 0.0), writes=[imp])
                for r in range(4):
                    if r == 0:
                        kb.op("dve", lambda v: v.tensor_scalar(out=imp[:, 1:64], in0=o3[:, 0, 65:128], scalar1=den[:, 0:1],
                                                               scalar2=None, op0=ALU.mult),
                              reads=[o_ps, den], writes=[imp])
                    else:
                        kb.op("dve", lambda v: v.scalar_tensor_tensor(out=imp[:, 1:64], in0=o3[:, r, 65:128],
                                                                      scalar=den[:, r:r + 1], in1=imp[:, 1:64],
                                                                      op0=ALU.mult, op1=ALU.add),
                              reads=[o_ps, den, imp], writes=[imp])
                kb.op("dve", lambda v: v.tensor_tensor(out=imp[:, :], in0=imp[:, :], in1=ftab[:, 64 - 2 * i:128 - 2 * i],
                                                       op=ALU.add), reads=[imp, ftab], writes=[imp])
                kb.op("dve", lambda v: v.tensor_scalar(out=imp[:, 0:1], in0=imp[:, 0:1], scalar1=BIG, scalar2=None,
                                                       op0=ALU.add), reads=[imp], writes=[imp])
                kb.op("dve", lambda v: v.max(out=top8[:, :], in_=imp[:, :]), reads=[imp], writes=[top8])
                kb.op("dve", lambda v: v.tensor_scalar(out=imp[:, :], in0=imp[:, :], scalar1=top8[:, 7:8], scalar2=-1.0,
                                                       op0=ALU.is_ge, op1=ALU.add), reads=[imp, top8], writes=[imp])
                kb.op("dve", lambda v: v.tensor_scalar(out=nselb[:, :], in0=imp[:, :], scalar1=-NEG, scalar2=None,
                                                       op0=ALU.mult), reads=[imp], writes=[nselb])
                kb.tr(tpp[0:64, 0:128], nselb[:, :], C["ident_b"][:, :], reads=[nselb, C["ident_b"]], writes=[tpp])
                kb.op("dve", lambda v: v.tensor_copy(out=nst,
                                                     in_=tpp[0:64, 0:128].unsqueeze(1).broadcast_to([64, 4, 128])),
                      reads=[tpp], writes=[qsel_res[g]])

            if nsa:
                for g in range(4):
                    rhs_q = QT[0:64, 4 * g:4 * g + 4, :]
                    ref = new_ref()
                    ncv = min(255, 8 * i + 7)
                    ccs = [cc for cc in range(2) if min(128, ncv - 128 * cc) > 0]
                    for ci, cc in enumerate(ccs):
                        M = min(128, ncv - 128 * cc)
                        o = 8 * i - 128 * cc
                        masks = []
                        if 0 <= o <= 128:
                            masks.append((esh[:, 128 - o:128 - o + M], lneg[:, :], [esh, lneg]))
                        post = None
                        if ci == len(ccs) - 1:
                            def post(g=g, ref=ref):
                                evac(ref["t"], g, 128, True, 0)
                                selection(g, ref["t"])
                        make_unit(items, kcmpT[:, g, cc * 128:cc * 128 + M], M, rhs_q, masks, vcmp[0:M, cc, g, :], 128, ref,
                                  [kcmpT], [vcmp], ci == 0, post)
            for g in range(4):
                for bi_, b in enumerate(branches):
                    ref = new_ref()
                    lo = max(0, i - nback[b])
                    kcs = list(range(lo, i + 1))
                    for ki, kc in enumerate(kcs):
                        masks = []
                        if kc == i:
                            masks.append((C["ident_b"][:, :], mcur[:, :], [C["ident_b"], mcur]))
                        elif b == "win" and kc == i - nback[b]:
                            masks.append((C["ident_b"][:, :], mprev[:, :], [C["ident_b"], mprev]))
                        ks_ = kc % ring[b]
                        post = None
                        if ki == len(kcs) - 1:
                            if nsa:
                                def post(g=g, ref=ref, b=b):
                                    evac(ref["t"], g, 65, False, 1 if b == "sel" else 2)
                            else:
                                def post(g=g, ref=ref):
                                    evac(ref["t"], g, 65, True, None, extra_den=esink[:, 4 * g:4 * g + 4])
                        if b == "sel":
                            lhs = KT[b][:, g, ks_ * 128:(ks_ + 1) * 128]
                            rq = QT[:, 4 * g:4 * g + 4, :]
                            kr = [KT[b], qsel_res[g]]
                        else:
                            lhs = KT[b][0:64, g, ks_ * 128:(ks_ + 1) * 128]
                            rq = QT[0:64, 4 * g:4 * g + 4, :]
                            kr = [KT[b]]
                        make_unit(items, lhs, 128, rq, masks, VC[b][:, ks_, g, :], 65, ref, kr, [VC[b]], ki == 0, post)
            run_items(items)
            kb.op("act", lambda a: a.copy(out=osbb[:, :], in_=osb[:, :, :].rearrange("p h d -> p (h d)")),
                  reads=[osb], writes=[osbb])
            for k in range(8):
                kb.tr(tpp[:, k * 128:(k + 1) * 128], osbb[:, k * 128:(k + 1) * 128], C["ident_b"][:, :],
                      reads=[osbb, C["ident_b"]], writes=[tpp], sig=(k == 7))
            kb.op("act", lambda a: a.copy(out=oT[:, :, :], in_=tpp[:, :].rearrange("p (k t) -> p k t", k=8)),
                  reads=[tpp], writes=[oT])
            ho = hout[i % 2]
            for half in range(2):
                ps = pj[half]
                for dcl in range(4):
                    dc = half * 4 + dcl
                    for k in range(8):
                        kb.mm(ps[:, dcl * 128:(dcl + 1) * 128], wo[:, k, dc * 128:(dc + 1) * 128], oT[:, k, :],
                              start=(k == 0), stop=(k == 7), reads=[wo, oT], writes=[ps], sig=(k == 7 and dcl == 3))
                hv = h[:, half * 4:(half + 1) * 4, :]
                kb.op("dve", lambda v: v.tensor_tensor(out=ho[:, half * 4:(half + 1) * 4, :],
                                                       in0=ps[:, :].rearrange("p (c t) -> p c t", c=4), in1=hv, op=ALU.add),
                      reads=[ps, h], writes=[ho])
            if not nsa:
                kb.op("dve", lambda v: v.tensor_tensor(out=ho[:, :, :], in0=ho[:, :, :],
                                                       in1=bo[:, :].unsqueeze(2).broadcast_to([128, 8, 128]), op=ALU.add),
                      reads=[ho, bo], writes=[ho])
            kb.dma("sp", hview(hT, i * 128, 128), ho[:, :, :], reads=[ho])


def ph_swa(kb, C, P, hT, layer, inst):
    ph_attn(kb, C, P, hT, layer, inst, "swa")


def ph_nsa(kb, C, P, hT, layer, inst):
    ph_attn(kb, C, P, hT, layer, inst, "nsa")
def mamba_host_consts(c):
    s1 = np.arange(128)[:, None]
    s2 = np.arange(128)[None, :]
    c["c_triU"] = (s1 <= s2).astype(np.float32)
    c["c_strictL"] = (s1 > s2).astype(np.float32)


def load_bcast(kb, name, ap1d, n, q="sp"):
    t = kb.sb(name, [128, n], F32)
    kb.dma(q, t[:, :], ap1d.partition_broadcast(128), writes=[t])
    return t


def ph_mamba(kb, C, P, hT, layer, inst):
    nc = kb.nc
    if "ydram" not in C:
        C["ydram"] = nc.dram_tensor("ydram", [S, 2048], F32, kind="Internal").ap()
    ydram = C["ydram"]
    w_in = P.inp("ssm_w_in")[inst]
    with kb.phase("mamba_a"):
        lnw = load_vec_pk(kb, "lnw", P.inp("ln_mix")[layer], 8)
        wx = kb.sb("wx", [128, 8, 4096], BF16)
        wdt = kb.sb("wdt", [128, 8, 32], BF16)
        for k in range(8):
            kb.dma("pool", wx[:, k, :], w_in[k * 128:(k + 1) * 128, 2048:6144], writes=[wx])
            kb.dma("pool", wdt[:, k, :], w_in[k * 128:(k + 1) * 128, 6144:6176], writes=[wdt])
        triU = kb.sb("triU", [128, 128], F32)
        kb.dma("sp", triU[:, :], P.inp("c_triU")[:, :], writes=[triU])
        strictL = kb.sb("strictL", [128, 128], F32)
        kb.dma("sp", strictL[:, :], P.inp("c_strictL")[:, :], writes=[strictL])
        pj = [kb.ps("pj", [128, 512], F32) for _ in range(2)]
        tp = kb.ps("tp", [128, 1024], BF16)
        sm = kb.ps("sm", [128, 512], F32)
        cbp = kb.ps("cbp", [128, 512], F32)
        stp_ = kb.ps("stp", [128, 512], F32)
        df = kb.ps("df", [128, 512], F32)
        yy = kb.ps("yy", [128, 512], F32)
        nb = NormBufs(kb, 128, ss=sm, off=128)
        cw_raw = kb.sb("cw_raw", [128, 128], F32)
        kb.dma("sp", cw_raw[:, :], P.inp("ssm_conv_w")[inst].rearrange("k (cc p) -> (k cc) p", p=128), writes=[cw_raw])
        kb.tr(pj[0][:, 0:128], cw_raw[:, :], C["ident_f"][:, :], reads=[cw_raw, C["ident_f"]], writes=[pj[0]])
        cwT = kb.sb("cwT", [128, 128], F32)
        kb.op("dve", lambda v: v.tensor_copy(out=cwT[:, :], in_=pj[0][:, 0:128]), reads=[pj[0]], writes=[cwT])
        cb_raw = kb.sb("cb_raw", [32, 128], F32)
        kb.dma("sp", cb_raw[:, :], P.inp("ssm_conv_b")[inst].rearrange("(cc p) -> cc p", p=128), writes=[cb_raw])
        kb.tr(pj[1][:, 0:32], cb_raw[:, :], C["ident_f"][:32, :32], reads=[cb_raw, C["ident_f"]], writes=[pj[1]])
        cbT = kb.sb("cbT", [128, 32], F32)
        kb.op("dve", lambda v: v.tensor_copy(out=cbT[:, :], in_=pj[1][:, 0:32]), reads=[pj[1]], writes=[cbT])
        diag = kb.sb("diag", [128, 128, 128], BF16)
        for idx in range(128):
            e = "dve" if idx % 2 == 0 else "pool"
            kb.op(e, lambda v: v.tensor_scalar(out=diag[:, idx, :], in0=C["ident_f"][:, :], scalar1=cwT[:, idx:idx + 1],
                                               scalar2=None, op0=ALU.mult), reads=[C["ident_f"], cwT], writes=[diag])
        dtb = load_bcast(kb, "dtb", P.inp("ssm_dt_bias")[inst], 32)
        aneg = load_bcast(kb, "aneg", P.inp("ssm_a_log")[inst], 32)
        kb.op("act", lambda a: a.activation(out=aneg[:, :], in_=aneg[:, :], func=AF.Exp), reads=[aneg], writes=[aneg])
        kb.op("dve", lambda v: v.tensor_scalar(out=aneg[:, :], in0=aneg[:, :], scalar1=-1.0, scalar2=None, op0=ALU.mult),
              reads=[aneg], writes=[aneg])
        dsk = load_bcast(kb, "dsk", P.inp("ssm_d")[inst], 32)
        one = kb.sb("one", [128, 1], F32)
        kb.op("dve", lambda v: v.memset(one[:, :], 1.0), writes=[one])

        hb = [kb.sb("h", [128, 8, 128], F32) for _ in range(2)]
        xn = kb.sb("xn", [128, 8, 128], BF16)
        raw = kb.sb("raw", [128, 32, 131], BF16)
        kb.op("dve", lambda v: v.memset(raw[:, :, :], 0.0), writes=[raw])
        xbcT = kb.sb("xbcT", [128, 32, 128], BF16)
        xs_tm = kb.sb("xs_tm", [128, 32, 64], BF16)
        B_tm = kb.sb("B_tm", [128, 8, 128], BF16)
        dtt = kb.sb("dtt", [128, 32], F32)
        ad = kb.sb("ad", [128, 32], F32)
        acum = kb.sb("acum", [128, 32], F32)
        eacum = kb.sb("eacum", [128, 32], F32)
        dst = kb.sb("dst", [128, 32], F32)
        cdec = kb.sb("cdec", [128, 32], F32)
        xd = kb.sb("xd", [128, 32, 64], BF16)
        xdd = kb.sb("xdd", [128, 32, 64], BF16)
        cbm2 = [kb.sb("cbm", [128, 128], BF16) for _ in range(2)]
        Rm = kb.sb("Rm", [128, 16, 128], F32)
        decT2 = [kb.sb("decT", [128, 4, 128], BF16) for _ in range(2)]
        MT2 = [kb.sb("MT", [128, 4, 128], BF16) for _ in range(2)]
        t1 = kb.sb("t1", [128, 4, 64], F32)
        t2 = kb.sb("t2", [128, 4, 64], F32)
        ypre = [kb.sb("ypre", [128, 32, 64], F32) for _ in range(1)]
        prev = kb.sb("prev", [128, 32, 64], F32)
        prevb = kb.sb("prevb", [128, 32, 64], BF16)
        kb.op("dve", lambda v: v.memset(prev[:, :, :], 0.0), writes=[prev])
        kb.op("dve", lambda v: v.memset(prevb[:, :, :], 0.0), writes=[prevb])

        def load_h(i):
            kb.dma("sp", hb[i % 2][:, :, :], hview(hT, i * 128, 128), writes=[hb[i % 2]])
        def proj_slice(cc4):
            ps = pj[cc4 % 2]
            for c in range(4):
                cc = cc4 * 4 + c
                for k in range(8):
                    kb.mm(ps[:, c * 128:(c + 1) * 128], wx[:, k, cc * 128:(cc + 1) * 128], xn[:, k, :],
                          start=(k == 0), stop=(k == 7), reads=[wx, xn], writes=[ps], sig=(k == 7 and c == 3))
            kb.op("act", lambda a: a.copy(out=raw[:, cc4 * 4:(cc4 + 1) * 4, 3:131],
                                          in_=ps[:, :].rearrange("p (c t) -> p c t", c=4)), reads=[ps], writes=[raw])

        load_h(0)
        for i in range(NCH):
            if i + 1 < NCH:
                load_h(i + 1)
            h = hb[i % 2]
            emit_norm(kb, C, nb, h, lnw, xn, 128)
            for k in range(8):
                kb.mm(sm[:, 0:32], xn[:, k, :], wdt[:, k, :], start=(k == 0), stop=(k == 7), reads=[xn, wdt], writes=[sm])
            kb.op("dve", lambda v: v.tensor_tensor(out=dtt[:, :], in0=sm[:, 0:32], in1=dtb[:, :], op=ALU.add),
                  reads=[sm, dtb], writes=[dtt])
            kb.op("act", lambda a: a.activation(out=dtt[:, :], in_=dtt[:, :], func=AF.Exp), reads=[dtt], writes=[dtt])
            kb.op("act", lambda a: a.activation(out=dtt[:, :], in_=dtt[:, :], func=AF.Ln, bias=one[:, :]),
                  reads=[dtt, one], writes=[dtt])
            kb.op("dve", lambda v: v.tensor_tensor(out=ad[:, :], in0=dtt[:, :], in1=aneg[:, :], op=ALU.mult),
                  reads=[dtt, aneg], writes=[ad])
            kb.mm(sm[:, 32:64], triU[:, :], ad[:, :], start=True, stop=True, reads=[triU, ad], writes=[sm])
            kb.mm(sm[:, 64:96], C["ones_f"][:, :], ad[:, :], start=True, stop=True, reads=[C["ones_f"], ad], writes=[sm])
            kb.op("dve", lambda v: v.tensor_copy(out=acum[:, :], in_=sm[:, 32:64]), reads=[sm], writes=[acum])
            kb.op("act", lambda a: a.activation(out=eacum[:, :], in_=sm[:, 32:64], func=AF.Exp), reads=[sm], writes=[eacum])
            kb.op("act", lambda a: a.activation(out=cdec[:, :], in_=sm[:, 64:96], func=AF.Exp), reads=[sm], writes=[cdec])
            kb.op("dve", lambda v: v.tensor_tensor(out=dst[:, :], in0=sm[:, 64:96], in1=acum[:, :], op=ALU.subtract),
                  reads=[sm, acum], writes=[dst])
            kb.op("act", lambda a: a.activation(out=dst[:, :], in_=dst[:, :], func=AF.Exp), reads=[dst], writes=[dst])
            kb.op("dve", lambda v: v.tensor_tensor(out=dst[:, :], in0=dst[:, :], in1=dtt[:, :], op=ALU.mult),
                  reads=[dst, dtt], writes=[dst])
            for cc4 in range(8):
                proj_slice(cc4)
            for cc4 in range(8):
                ps = pj[cc4 % 2]
                for c in range(4):
                    cc = cc4 * 4 + c
                    for k in range(4):
                        kb.mm(ps[:, c * 128:(c + 1) * 128], diag[:, k * 32 + cc, :], raw[:, cc, k:k + 128],
                              start=(k == 0), stop=(k == 3), reads=[diag, raw], writes=[ps], sig=(k == 3 and c == 3))
                for c in range(4):
                    cc = cc4 * 4 + c
                    kb.op("act", lambda a: a.activation(out=xbcT[:, cc, :], in_=ps[:, c * 128:(c + 1) * 128], func=AF.Silu,
                                                        bias=cbT[:, cc:cc + 1]), reads=[ps, cbT], writes=[xbcT])
            kb.op("pool", lambda g_: g_.tensor_copy(out=raw[:, :, 0:3], in_=raw[:, :, 128:131]), reads=[raw], writes=[raw])
            for rd in range(3):
                for j in range(8):
                    cc = rd * 8 + j
                    kb.tr(tp[:, j * 128:(j + 1) * 128], xbcT[:, cc, :], C["ident_b"][:, :], reads=[xbcT, C["ident_b"]],
                          writes=[tp], sig=(j == 7))
                if rd < 2:
                    kb.op("act", lambda a: a.copy(out=xs_tm[:, rd * 16:(rd + 1) * 16, :].rearrange("p h d -> p (h d)"),
                                                  in_=tp[:, :]), reads=[tp], writes=[xs_tm])
                else:
                    kb.op("act", lambda a: a.copy(out=B_tm[:, :, :].rearrange("p g n -> p (g n)"), in_=tp[:, :]),
                          reads=[tp], writes=[B_tm])
            dtb_ = dtt[:, :].unsqueeze(2).broadcast_to([128, 32, 64])
            dst_ = dst[:, :].unsqueeze(2).broadcast_to([128, 32, 64])
            kb.op("dve", lambda v: v.tensor_tensor(out=xd[:, :, :], in0=xs_tm[:, :, :], in1=dtb_, op=ALU.mult),
                  reads=[xs_tm, dtt], writes=[xd])
            kb.op("pool", lambda g_: g_.tensor_tensor(out=xdd[:, :, :], in0=xs_tm[:, :, :], in1=dst_, op=ALU.mult),
                  reads=[xs_tm, dst], writes=[xdd])
            def fill_rm(hh):
                hsl = slice(hh * 16, (hh + 1) * 16)
                kb.op("dve", lambda v: v.tensor_tensor(out=Rm[:, :, :], in0=triU[:, :].unsqueeze(1).broadcast_to([128, 16, 128]),
                                                       in1=ad[:, hsl].unsqueeze(2).broadcast_to([128, 16, 128]), op=ALU.mult),
                      reads=[triU, ad], writes=[Rm])
            fill_rm(0)
            yp = ypre[0]
            def S1a(g):
                hs = slice(4 * g, 4 * g + 4)
                cbm, decT = cbm2[g % 2], decT2[g % 2]
                kb.mm(cbp[:, 0:128], xbcT[:, 16 + g, :], xbcT[:, 24 + g, :], start=True, stop=True, reads=[xbcT], writes=[cbp])
                kb.op("dve", lambda v: v.tensor_tensor(out=cbm[:, :], in0=cbp[:, 0:128], in1=triU[:, :], op=ALU.mult),
                      reads=[cbp, triU], writes=[cbm])
                kb.mm(df[:, :], strictL[:, :], Rm[:, (4 * g) % 16:(4 * g) % 16 + 4, :].rearrange("p r l -> p (r l)"), start=True, stop=True,
                      reads=[strictL, Rm], writes=[df])
                kb.op("act", lambda a: a.activation(out=decT[:, :, :].rearrange("p r l -> p (r l)"), in_=df[:, :], func=AF.Exp),
                      reads=[df], writes=[decT])

            def S1b(g):
                cbm, decT, MT = cbm2[g % 2], decT2[g % 2], MT2[g % 2]
                kb.op("dve", lambda v: v.tensor_tensor(out=MT[:, :, :], in0=decT[:, :, :],
                                                       in1=cbm[:, :].unsqueeze(1).broadcast_to([128, 4, 128]), op=ALU.mult),
                      reads=[decT, cbm], writes=[MT])

            def S2(g):
                hs = slice(4 * g, 4 * g + 4)
                MT = MT2[g % 2]
                for r in range(4):
                    kb.mm(yy[:, r * 64:(r + 1) * 64], MT[:, r, :], xd[:, 4 * g + r, :], start=True, stop=True,
                          reads=[MT, xd], writes=[yy], sig=False)
                kb.mm(yy[:, 256:512], xbcT[:, 24 + g, :], prevb[:, hs, :].rearrange("p h d -> p (h d)"), start=True, stop=True,
                      reads=[xbcT, prevb], writes=[yy])
                ea_ = eacum[:, hs].unsqueeze(2).broadcast_to([128, 4, 64])
                kb.op("dve", lambda v: v.tensor_tensor(out=t1[:, :, :], in0=yy[:, 256:512].rearrange("p (h d) -> p h d", d=64),
                                                       in1=ea_, op=ALU.mult), reads=[yy, eacum], writes=[t1])
                kb.op("dve", lambda v: v.tensor_tensor(out=t1[:, :, :], in0=t1[:, :, :],
                                                       in1=yy[:, 0:256].rearrange("p (h d) -> p h d", d=64), op=ALU.add),
                      reads=[yy, t1], writes=[t1])
                kb.op("dve", lambda v: v.tensor_tensor(out=t2[:, :, :], in0=xs_tm[:, hs, :],
                                                       in1=dsk[:, hs].unsqueeze(2).broadcast_to([128, 4, 64]), op=ALU.mult),
                      reads=[xs_tm, dsk], writes=[t2])
                kb.op("dve", lambda v: v.tensor_tensor(out=yp[:, hs, :], in0=t1[:, :, :], in1=t2[:, :, :], op=ALU.add),
                      reads=[t1, t2], writes=[yp])
                kb.mm(stp_[:, 0:256], B_tm[:, g, :], xdd[:, hs, :].rearrange("p h d -> p (h d)"), start=True, stop=True,
                      reads=[B_tm, xdd], writes=[stp_])
                kb.op("dve", lambda v: v.tensor_tensor(out=prev[:, hs, :], in0=prev[:, hs, :],
                                                       in1=cdec[:, hs].unsqueeze(2).broadcast_to([128, 4, 64]), op=ALU.mult),
                      reads=[prev, cdec], writes=[prev])
                kb.op("dve", lambda v: v.tensor_tensor(out=prev[:, hs, :], in0=prev[:, hs, :],
                                                       in1=stp_[:, 0:256].rearrange("p (h d) -> p h d", d=64), op=ALU.add),
                      reads=[prev, stp_], writes=[prev])
                kb.op("act", lambda a: a.copy(out=prevb[:, hs, :], in_=prev[:, hs, :]), reads=[prev], writes=[prevb])

            S1a(0)
            S1b(0)
            for g in range(8):
                if g == 3:
                    fill_rm(1)
                if g + 1 < 8:
                    S1a(g + 1)
                S2(g)
                if g + 1 < 8:
                    S1b(g + 1)
            kb.dma("sp", ydram[i * 128:(i + 1) * 128, :], yp[:, :, :].rearrange("p h d -> p (h d)"), reads=[yp])
    with kb.phase("mamba_b"):
        lnw = load_vec_pk(kb, "lnw", P.inp("ln_mix")[layer], 8)
        wz = kb.sb("wz", [128, 8, 2048], BF16)
        for k in range(8):
            kb.dma("pool", wz[:, k, :], w_in[k * 128:(k + 1) * 128, 0:2048], writes=[wz])
        wout = load_w_bf16(kb, "wout", P.inp("ssm_w_out")[inst], 16, D)
        nw = load_bcast(kb, "nw", P.inp("ssm_norm_w")[inst], 2048)
        pj = [kb.ps("pj", [128, 512], F32) for _ in range(2)]
        tp = kb.ps("tp", [128, 1024], BF16)
        nb = NormBufs(kb, 128)
        hb = [kb.sb("h", [128, 8, 128], F32) for _ in range(3)]
        yb = [kb.sb("yb", [128, 2048], F32) for _ in range(2)]
        xn = kb.sb("xn", [128, 8, 128], BF16)
        zs = kb.sb("zs", [128, 2048], F32)
        sq = kb.sb("sq2", [128, 2048], F32)
        gss = kb.sb("gss", [128, 8], F32)
        gnb2 = [kb.sb("gnb", [128, 2048], BF16) for _ in range(2)]
        oT = kb.sb("oT", [128, 16, 128], BF16)
        hout = [kb.sb("hout", [128, 8, 128], F32) for _ in range(2)]

        xn2 = [xn, kb.sb("xnb", [128, 8, 128], BF16)]
        zs2 = [zs, kb.sb("zsb", [128, 2048], F32)]

        def load_hh(i):
            kb.dma("sp", hb[i % 3][:, :, :], hview(hT, i * 128, 128), writes=[hb[i % 3]])

        def load_y(i):
            kb.dma("sp", yb[i % 2][:, :], ydram[i * 128:(i + 1) * 128, :], writes=[yb[i % 2]])

        def front(i):
            h = hb[i % 3]
            xn_ = xn2[i % 2]
            zs_ = zs2[i % 2]
            emit_norm(kb, C, nb, h, lnw, xn_, 128)
            for ct in range(4):
                ps = pj[ct % 2]
                for k in range(8):
                    kb.mm(ps[:, :], xn_[:, k, :], wz[:, k, ct * 512:(ct + 1) * 512], start=(k == 0), stop=(k == 7),
                          reads=[xn_, wz], writes=[ps])
                kb.op("act", lambda a: a.activation(out=zs_[:, ct * 512:(ct + 1) * 512], in_=ps[:, :], func=AF.Silu),
                      reads=[ps], writes=[zs_])

        def mid(i):
            y = yb[i % 2]
            zs_ = zs2[i % 2]
            gnb = gnb2[i % 2]
            kb.op("dve", lambda v: v.tensor_tensor(out=zs_[:, :], in0=zs_[:, :], in1=y[:, :], op=ALU.mult),
                  reads=[zs_, y], writes=[zs_])
            kb.op("act", lambda a: a.activation(out=sq[:, :], in_=zs_[:, :], func=AF.Square), reads=[zs_], writes=[sq])
            kb.op("dve", lambda v: v.tensor_reduce(out=gss[:, :], in_=sq[:, :].rearrange("p (g c) -> p g c", g=8),
                                                   axis=AX.X, op=ALU.add), reads=[sq], writes=[gss])
            kb.op("act", lambda a: a.activation(out=gss[:, :], in_=gss[:, :], func=AF.Sqrt, scale=1.0 / 256,
                                                bias=C["eps"][:, :]), reads=[gss, C["eps"]], writes=[gss])
            kb.op("dve", lambda v: v.reciprocal(out=gss[:, :], in_=gss[:, :]), reads=[gss], writes=[gss])
            kb.op("dve", lambda v: v.tensor_tensor(out=sq[:, :].rearrange("p (g c) -> p g c", g=8),
                                                   in0=zs_[:, :].rearrange("p (g c) -> p g c", g=8),
                                                   in1=gss[:, :].unsqueeze(2).broadcast_to([128, 8, 256]), op=ALU.mult),
                  reads=[zs_, gss], writes=[sq])
            kb.op("pool", lambda g_: g_.tensor_tensor(out=gnb[:, :], in0=sq[:, :], in1=nw[:, :], op=ALU.mult),
                  reads=[sq, nw], writes=[gnb])

        def tail(i):
            h = hb[i % 3]
            gnb = gnb2[i % 2]
            for rd in range(2):
                for j in range(8):
                    k = rd * 8 + j
                    kb.tr(tp[:, j * 128:(j + 1) * 128], gnb[:, k * 128:(k + 1) * 128], C["ident_b"][:, :],
                          reads=[gnb, C["ident_b"]], writes=[tp], sig=(j == 7))
                kb.op("act", lambda a: a.copy(out=oT[:, rd * 8:(rd + 1) * 8, :], in_=tp[:, :].rearrange("p (k t) -> p k t", k=8)),
                      reads=[tp], writes=[oT])
            ho = hout[i % 2]
            for half in range(2):
                ps = pj[half]
                for dcl in range(4):
                    dc = half * 4 + dcl
                    for k in range(16):
                        kb.mm(ps[:, dcl * 128:(dcl + 1) * 128], wout[:, k, dc * 128:(dc + 1) * 128], oT[:, k, :],
                              start=(k == 0), stop=(k == 15), reads=[wout, oT], writes=[ps], sig=(k == 15 and dcl == 3))
                kb.op("dve", lambda v: v.tensor_tensor(out=ho[:, half * 4:(half + 1) * 4, :],
                                                       in0=ps[:, :].rearrange("p (c t) -> p c t", c=4),
                                                       in1=h[:, half * 4:(half + 1) * 4, :], op=ALU.add),
                      reads=[ps, h], writes=[ho])
            kb.dma("sp", hview(hT, i * 128, 128), ho[:, :, :], reads=[ho])

        load_hh(0)
        load_hh(1)
        load_y(0)
        front(0)
        for i in range(NCH + 1):
            if i + 1 < NCH:
                front(i + 1)
            if i < NCH:
                if i + 1 < NCH:
                    load_y(i + 1)
                mid(i)
            if i >= 1:
                tail(i - 1)
            if i + 2 < NCH:
                load_hh(i + 2)
DEPTH = 4
IN_SHAPES = {
    "x": ([S, D], "f"), "positions": ([S, 1], "i"),
    "ln_ffn1": ([4, D], "f"), "ffn1_w_in": ([4, D, 2 * DFF], "f"), "ffn1_w_out": ([4, DFF, D], "f"),
    "ln_mix": ([4, D], "f"), "ln_ffn2": ([4, D], "f"), "ffn2_w_in": ([4, D, 2 * DFF], "f"),
    "ffn2_w_out": ([4, DFF, D], "f"),
    "ssm_w_in": ([2, D, 6176], "f"), "ssm_conv_w": ([2, 4, 4096], "f"), "ssm_conv_b": ([2, 4096], "f"),
    "ssm_dt_bias": ([2, 32], "f"), "ssm_a_log": ([2, 32], "f"), "ssm_d": ([2, 32], "f"),
    "ssm_norm_w": ([2, 2048], "f"), "ssm_w_out": ([2, 2048, D], "f"),
    "swa_w_qkv": ([1, D, 1536], "f"), "swa_b_qkv": ([1, 1536], "f"), "swa_sinks": ([1, 16], "f"),
    "swa_w_o": ([1, D, D], "f"), "swa_b_o": ([1, D], "f"),
    "nsa_w_in": ([1, D, 2608], "f"), "nsa_pe_k": ([1, 32, 64], "f"), "nsa_k_w1": ([1, 2048, 256], "f"),
    "nsa_k_w2": ([1, 256, 64], "f"), "nsa_pe_v": ([1, 32, 64], "f"), "nsa_v_w1": ([1, 2048, 256], "f"),
    "nsa_v_w2": ([1, 256, 64], "f"), "nsa_w_o": ([1, D, D], "f"), "final_norm": ([D], "f"),
}


def host_consts():
    c = {}
    c["c_ident"] = np.eye(128, dtype=np.float32)
    attn_host_consts(c)
    mamba_host_consts(c)
    return c


class Prog:
    def __init__(self):
        self.nc = bass.Bass("TRN2", target_bir_lowering=False)
        self.ins = {}
        self.consts = host_consts()

    def inp(self, name):
        if name not in self.ins:
            if name in IN_SHAPES:
                shp, k = IN_SHAPES[name]
                dt = F32 if k == "f" else I32
            else:
                shp = list(self.consts[name].shape)
                dt = F32
            self.ins[name] = self.nc.dram_tensor(name, shp, dt, kind="ExternalInput").ap()
        return self.ins[name]


def build_program(plan=None):
    P = Prog()
    nc = P.nc
    if plan is None:
        plan = ["in"]
        for i in range(DEPTH):
            plan += [("ffn1", i), ("mix", i), ("ffn2", i)]
        plan += ["final"]
    out = nc.dram_tensor("out", [S, D], F32, kind="ExternalOutput").ap()
    hT = nc.dram_tensor("hT", [D, S], F32, kind="Internal").ap()
    kb = KB(nc)
    cin = {"c_ident": P.inp("c_ident")}
    C = setup_consts(kb, cin)
    for st in plan:
        if st == "in":
            ph_in_transpose(kb, C, P.inp("x"), hT)
        elif st == "final":
            ph_final(kb, C, hT, P.inp("final_norm"), out)
        elif st[0] in ("ffn1", "ffn2"):
            i = st[1]
            ph_ffn(kb, C, hT, P.inp("ln_" + st[0])[i], P.inp(st[0] + "_w_in")[i], P.inp(st[0] + "_w_out")[i])
        elif st[0] == "mix":
            i = st[1]
            kind, inst = i % 3, i // 3
            if kind == 0:
                ph_mamba(kb, C, P, hT, i, inst)
            elif kind == 1:
                ph_swa(kb, C, P, hT, i, inst)
            else:
                ph_nsa(kb, C, P, hT, i, inst)
    kb.barrier()
    kb.close()
    P.kb = kb
    return P


def make_in_maps(P, inputs, ncores=8):
    maps = []
    for c in range(ncores):
        m = {}
        for name in P.ins:
            if name in P.consts:
                m[name] = P.consts[name]
            elif name == "x":
                m[name] = np.ascontiguousarray(np.asarray(inputs["x"])[c])
            elif name == "positions":
                m[name] = np.ascontiguousarray(np.asarray(inputs["positions"])[c].reshape(S, 1)).astype(np.int32)
            else:
                m[name] = np.ascontiguousarray(np.asarray(inputs[name]))
        maps.append(m)
    return maps


_CACHE = {}


def kernel(**inputs):
    if "P" not in _CACHE:
        _CACHE["P"] = build_program()
    P = _CACHE["P"]
    maps = make_in_maps(P, inputs)
    res = run_bass_kernel_spmd(P.nc, maps, core_ids=list(range(8)))
    return np.stack([np.asarray(res.results[c]["out"]) for c in range(8)], axis=0).astype(np.float32)
```

```python
import numpy as np
from contextlib import ExitStack, contextmanager
import concourse.bass as bass
import concourse.mybir as mybir
from concourse.bass_utils import run_bass_kernel_spmd

F32 = mybir.dt.float32
BF16 = mybir.dt.bfloat16
I32 = mybir.dt.int32
AF = mybir.ActivationFunctionType
ALU = mybir.AluOpType
AX = mybir.AxisListType

SAME_ENG_RAW_SYNC = True


class Res:
    __slots__ = ("w", "r", "name")

    def __init__(self, name=""):
        self.w = {}
        self.r = {}
        self.name = name


class Tile:
    def __init__(self, t, name, nparts=0):
        self.t = t
        self.res = Res(name)
        self.parts = [Res(f"{name}.{i}") for i in range(nparts)]

    def __getitem__(self, idx):
        return self.t[idx]


def _res(x):
    if isinstance(x, Res):
        return [x]
    if isinstance(x, Tile):
        return [x.res] + x.parts
    raise TypeError(x)


class KB:
    ENGS = ("pe", "act", "dve", "pool", "sp")

    def __init__(self, nc, nslot=12):
        self.nc = nc
        self.stack = ExitStack()
        self.eng = {"pe": nc.tensor, "act": nc.scalar, "dve": nc.vector, "pool": nc.gpsimd, "sp": nc.sync}
        self.sem = {k: self.stack.enter_context(nc.semaphore("s_" + k)) for k in self.ENGS}
        self.cnt = {k: 0 for k in self.ENGS}
        self.seen = {k: {} for k in self.ENGS}
        self.nslot = nslot
        self.dq = ("sp", "pool", "act")
        self.dsem = {q: [self.stack.enter_context(nc.semaphore(f"d_{q}{i}")) for i in range(nslot)] for q in self.dq}
        self.dcnt = {q: [0] * nslot for q in self.dq}
        self.dnext = {q: 0 for q in self.dq}
        self.pstack = None
        self.uid = 0
        self.ninst = 0

    @contextmanager
    def phase(self, name=""):
        old = self.pstack
        self.pstack = ExitStack()
        self.nphase = getattr(self, "nphase", 0) + 1
        scope = self.nc.named_scope(f"ph{self.nphase:02d}_{name}")
        scope.__enter__()
        try:
            yield
        finally:
            self.barrier()
            scope.__exit__(None, None, None)
            self.pstack.close()
            self.pstack = old

    def _nm(self, name):
        self.uid += 1
        return f"{name}_{self.uid}"

    def sb(self, name, shape, dt, nparts=0, glob=False):
        nm = self._nm(name)
        st = self.stack if (glob or self.pstack is None) else self.pstack
        t = st.enter_context(self.nc.sbuf_tensor(nm, list(shape), dt))
        return Tile(t, nm, nparts)

    def ps(self, name, shape, dt=F32, nparts=0, glob=False):
        nm = self._nm(name)
        st = self.stack if (glob or self.pstack is None) else self.pstack
        t = st.enter_context(self.nc.psum_tensor(nm, list(shape), dt))
        return Tile(t, nm, nparts)

    def _semof(self, key):
        return self.sem[key] if isinstance(key, str) else self.dsem[key[1]][key[2]]

    def _wait(self, e, deps):
        need = {}
        for key, c in deps:
            if c <= 0:
                continue
            if c <= self.seen[e].get(key, 0):
                continue
            if need.get(key, 0) < c:
                need[key] = c
        for key, c in need.items():
            self.eng[e].wait_ge(self._semof(key), c)
            self.seen[e][key] = c
            self.ninst += 1

    def _deps(self, e, reads, writes, strict=False):
        deps = []
        for x in reads:
            for r in _res(x):
                for key, c in r.w.items():
                    if key == e and not strict:
                        if e == "pe" or not SAME_ENG_RAW_SYNC:
                            continue
                    deps.append((key, c))
        for x in writes:
            for r in _res(x):
                for key, c in list(r.w.items()) + list(r.r.items()):
                    if key == e and not strict:
                        if e == "pe" or not SAME_ENG_RAW_SYNC:
                            continue
                    deps.append((key, c))
        return deps

    def _mark(self, key, c, reads, writes):
        for x in reads:
            for r in _res(x):
                r.r[key] = c
        for x in writes:
            for r in _res(x):
                r.w[key] = c

    def op(self, e, fn, reads=(), writes=(), sig=True):
        self._wait(e, self._deps(e, reads, writes))
        ins = fn(self.eng[e])
        self.ninst += 1
        if sig:
            self.cnt[e] += 1
            ins.then_inc(self.sem[e], 1)
            c = self.cnt[e]
        else:
            c = self.cnt[e] + 1
        self._mark(e, c, reads, writes)
        return ins

    def mm(self, out, lhsT, rhs, start, stop, reads=(), writes=(), sig=None, **kw):
        if sig is None:
            sig = stop
        return self.op("pe", lambda pe: pe.matmul(out, lhsT=lhsT, rhs=rhs, start=start, stop=stop, **kw),
                       reads, writes, sig=sig)

    def tr(self, out, in_, ident, reads=(), writes=(), sig=True):
        return self.op("pe", lambda pe: pe.transpose(out, in_, ident), reads, writes, sig=sig)

    def dma(self, q, out, in_, reads=(), writes=(), **kw):
        i = self.dnext[q]
        self.dnext[q] = (i + 1) % self.nslot
        key = ("d", q, i)
        deps = self._deps(q, reads, writes, strict=True) + [(key, self.dcnt[q][i])]
        self._wait(q, deps)
        ins = self.eng[q].dma_start(out=out, in_=in_, **kw)
        self.ninst += 1
        self.dcnt[q][i] += 16
        ins.then_inc(self.dsem[q][i], 16)
        self._mark(key, self.dcnt[q][i], reads, writes)
        return ins

    def barrier(self, engines=None):
        allk = [(k, self.cnt[k]) for k in self.ENGS]
        for q in self.dq:
            for i in range(self.nslot):
                allk.append((("d", q, i), self.dcnt[q][i]))
        for e in (engines or self.ENGS):
            self._wait(e, [(k, c) for k, c in allk if k != e])

    def close(self):
        self.stack.close()
S = 4096
D = 1024
DFF = 2816
EPS = 1e-6
NCH = S // 128


def hview(hT, t0, n):
    return hT[:, t0:t0 + n].rearrange("(k p) t -> p k t", p=128)


def setup_consts(kb, cin):
    C = {}
    C["ident_f"] = kb.sb("ident_f", [128, 128], F32, glob=True)
    kb.dma("sp", C["ident_f"][:, :], cin["c_ident"][:, :], writes=[C["ident_f"]])
    C["ident_b"] = kb.sb("ident_b", [128, 128], BF16, glob=True)
    kb.op("dve", lambda v: v.tensor_copy(out=C["ident_b"][:, :], in_=C["ident_f"][:, :]),
          reads=[C["ident_f"]], writes=[C["ident_b"]])
    C["ones_b"] = kb.sb("ones_b", [128, 128], BF16, glob=True)
    kb.op("dve", lambda v: v.memset(C["ones_b"][:, :], 1.0), writes=[C["ones_b"]])
    C["ones_f"] = kb.sb("ones_f", [128, 128], F32, glob=True)
    kb.op("dve", lambda v: v.memset(C["ones_f"][:, :], 1.0), writes=[C["ones_f"]])
    C["eps"] = kb.sb("eps", [128, 1], F32, glob=True)
    kb.op("dve", lambda v: v.memset(C["eps"][:, :], EPS), writes=[C["eps"]])
    return C


def load_vec_pk(kb, name, ap1d, nk):
    t = kb.sb(name, [128, nk], F32)
    kb.dma("sp", t[:, :], ap1d.rearrange("(k p) -> p k", p=128), writes=[t], allow_slow_non_contiguous=True)
    return t


class NormBufs:
    def __init__(self, kb, n=512, ss=None, off=0):
        self.sq = [kb.sb("sq", [128, n], BF16) for _ in range(2)]
        self.ss_t = ss if ss is not None else kb.ps("ss", [128, n], F32)
        self.off = off
        self.rstd = kb.sb("rstd", [128, n], F32)
        self.i = 0


def emit_norm(kb, C, nb, h, lnw, xn, n, hoff=0, xoff=0):
    hs = slice(hoff, hoff + n)
    xs = slice(xoff, xoff + n)
    for k in range(8):
        sq = nb.sq[nb.i % 2]
        nb.i += 1
        kb.op("act", lambda a: a.activation(out=sq[:, :n], in_=h[:, k, hs], func=AF.Square), reads=[h], writes=[sq])
        kb.mm(nb.ss_t[:, nb.off:nb.off + n], C["ones_b"][:, :], sq[:, :n], start=(k == 0), stop=(k == 7),
              reads=[sq, C["ones_b"]], writes=[nb.ss_t], sig=True)
    kb.op("act", lambda a: a.activation(out=nb.rstd[:, :n], in_=nb.ss_t[:, nb.off:nb.off + n], func=AF.Sqrt,
                                        scale=1.0 / D, bias=C["eps"][:, :]),
          reads=[nb.ss_t, C["eps"]], writes=[nb.rstd])
    kb.op("dve", lambda v: v.reciprocal(out=nb.rstd[:, :n], in_=nb.rstd[:, :n]), reads=[nb.rstd], writes=[nb.rstd])
    for k in range(8):
        kb.op("dve", lambda v: v.scalar_tensor_tensor(out=xn[:, k, xs], in0=h[:, k, hs], scalar=lnw[:, k:k + 1],
                                                      in1=nb.rstd[:, :n], op0=ALU.mult, op1=ALU.mult),
              reads=[h, lnw, nb.rstd], writes=[xn])


def ph_in_transpose(kb, C, x_ap, hT):
    with kb.phase("in_t"):
        xb = [kb.sb("xin", [128, D], F32) for _ in range(2)]
        tps = [kb.ps("tps", [128, 512], F32) for _ in range(2)]
        st = [kb.sb("xst", [128, 8, 128], F32) for _ in range(2)]
        for c in range(NCH):
            xt = xb[c % 2]
            kb.dma("sp", xt[:, :], x_ap[c * 128:(c + 1) * 128, :], writes=[xt])
            s = st[c % 2]
            for g4 in range(2):
                ps = tps[g4]
                for k in range(4):
                    kb.tr(ps[:, k * 128:(k + 1) * 128], xt[:, (g4 * 4 + k) * 128:(g4 * 4 + k + 1) * 128],
                          C["ident_f"][:, :], reads=[xt, C["ident_f"]], writes=[ps], sig=(k == 3))
                e = "act" if g4 == 0 else "dve"
                if e == "act":
                    kb.op(e, lambda a: a.copy(out=s[:, g4 * 4:(g4 + 1) * 4, :],
                                              in_=ps[:, :].rearrange("p (k t) -> p k t", k=4)),
                          reads=[ps], writes=[s])
                else:
                    kb.op(e, lambda v: v.tensor_copy(out=s[:, g4 * 4:(g4 + 1) * 4, :],
                                                     in_=ps[:, :].rearrange("p (k t) -> p k t", k=4)),
                          reads=[ps], writes=[s])
            kb.dma("pool", hview(hT, c * 128, 128), s[:, :, :], reads=[s])


def ph_ffn(kb, C, hT, lnw_ap, w_in_ap, w_out_ap):
    T = 1024
    NT = S // T
    NH = T // 512
    NJ = DFF // 128
    with kb.phase("ffn"):
        lnw = load_vec_pk(kb, "lnw", lnw_ap, 8)
        hbuf = [kb.sb("h", [128, 8, T], F32) for _ in range(2)]
        xn = kb.sb("xn", [128, 8, T], BF16)
        act = kb.sb("act", [128, NJ, T], BF16)
        nb = NormBufs(kb)
        NW = 3
        wbuf = [kb.sb("win", [128, 8, 2, 128], BF16) for _ in range(NW)]
        NO = 3
        wobuf = [kb.sb("wo", [128, NJ, 128], BF16) for _ in range(NO)]
        gps = [kb.ps("g", [128, 512], F32) for _ in range(2)]
        ups = [kb.ps("u", [128, 512], F32) for _ in range(2)]
        ops_ = [kb.ps("o", [128, 512], F32) for _ in range(2)]
        sgb = [kb.sb("sg", [128, 512], F32) for _ in range(2)]
        hob = [kb.sb("ho", [128, 512], F32) for _ in range(2)]

        def load_h(tt):
            kb.dma("sp", hbuf[tt % 2][:, :, :], hview(hT, tt * T, T), writes=[hbuf[tt % 2]])

        win_jobs = [(tt, j) for tt in range(NT) for j in range(NJ)]
        wo_jobs = [(tt, dc) for tt in range(NT) for dc in range(8)]
        st = {"wi": 0, "wo": 0}

        def issue_win(upto):
            while st["wi"] <= upto and st["wi"] < len(win_jobs):
                _, j = win_jobs[st["wi"]]
                w = wbuf[st["wi"] % NW]
                for gu in range(2):
                    c0 = gu * DFF + j * 128
                    kb.dma("pool", w[:, :, gu, :], w_in_ap[:, c0:c0 + 128].rearrange("(k p) c -> p k c", p=128),
                           writes=[w])
                st["wi"] += 1

        def issue_wo(upto):
            while st["wo"] <= upto and st["wo"] < len(wo_jobs):
                _, dc = wo_jobs[st["wo"]]
                w = wobuf[st["wo"] % NO]
                kb.dma("pool", w[:, :, :], w_out_ap[:, dc * 128:(dc + 1) * 128].rearrange("(j p) c -> p j c", p=128),
                       writes=[w])
                st["wo"] += 1

        load_h(0)
        issue_win(1)
        cnt = 0
        for tt in range(NT):
            if tt + 1 < NT:
                load_h(tt + 1)
            h = hbuf[tt % 2]
            for hf in range(NH):
                emit_norm(kb, C, nb, h, lnw, xn, 512, hoff=hf * 512, xoff=hf * 512)
            for j in range(NJ):
                ji = tt * NJ + j
                issue_win(ji + 2)
                if j == 8:
                    issue_wo(tt * 8)
                if j == 16:
                    issue_wo(tt * 8 + 1)
                w = wbuf[ji % NW]
                for hf in range(NH):
                    sl = slice(hf * 512, (hf + 1) * 512)
                    g_ps = gps[cnt % 2]
                    u_ps = ups[cnt % 2]
                    sg = sgb[cnt % 2]
                    cnt += 1
                    for k in range(8):
                        kb.mm(g_ps[:, :], w[:, k, 0, :], xn[:, k, sl], start=(k == 0), stop=(k == 7),
                              reads=[w, xn], writes=[g_ps])
                    for k in range(8):
                        kb.mm(u_ps[:, :], w[:, k, 1, :], xn[:, k, sl], start=(k == 0), stop=(k == 7),
                              reads=[w, xn], writes=[u_ps])
                    kb.op("act", lambda a: a.activation(out=sg[:, :], in_=g_ps[:, :], func=AF.Silu),
                          reads=[g_ps], writes=[sg])
                    kb.op("dve", lambda v: v.tensor_tensor(out=act[:, j, sl], in0=sg[:, :], in1=u_ps[:, :],
                                                           op=ALU.mult),
                          reads=[sg, u_ps], writes=[act])
            for dc in range(8):
                oi = tt * 8 + dc
                issue_wo(oi + 2)
                if dc >= 6:
                    issue_win((tt + 1) * NJ + (dc - 6))
                wo = wobuf[oi % NO]
                for hf in range(NH):
                    sl = slice(hf * 512, (hf + 1) * 512)
                    o_ps = ops_[cnt % 2]
                    ho = hob[cnt % 2]
                    cnt += 1
                    for j in range(NJ):
                        kb.mm(o_ps[:, :], wo[:, j, :], act[:, j, sl], start=(j == 0), stop=(j == NJ - 1),
                              reads=[wo, act], writes=[o_ps])
                    kb.op("dve", lambda v: v.scalar_tensor_tensor(out=ho[:, :], in0=o_ps[:, :], scalar=0.5,
                                                                  in1=h[:, dc, sl], op0=ALU.mult, op1=ALU.add),
                          reads=[o_ps, h], writes=[ho])
                    t0 = tt * T + hf * 512
                    kb.dma("sp", hT[dc * 128:(dc + 1) * 128, t0:t0 + 512], ho[:, :], reads=[ho])


def ph_final(kb, C, hT, fnw_ap, out_ap):
    with kb.phase("final"):
        lnw = load_vec_pk(kb, "fnw", fnw_ap, 8)
        hbuf = [kb.sb("h", [128, 8, 512], F32) for _ in range(2)]
        xn = [kb.sb("xnf", [128, 8, 512], F32) for _ in range(2)]
        nb = NormBufs(kb)
        tps = [kb.ps("tpo", [128, 512], F32) for _ in range(2)]
        ob = [kb.sb("ob", [128, D], F32) for _ in range(2)]
        n = 0
        for tt in range(S // 512):
            h = hbuf[tt % 2]
            kb.dma("sp", h[:, :, :], hview(hT, tt * 512, 512), writes=[h])
            x = xn[tt % 2]
            emit_norm(kb, C, nb, h, lnw, x, 512)
            for c in range(4):
                o = ob[n % 2]
                n += 1
                for g4 in range(2):
                    ps = tps[g4]
                    for k in range(4):
                        kk = g4 * 4 + k
                        kb.tr(ps[:, k * 128:(k + 1) * 128], x[:, kk, c * 128:(c + 1) * 128], C["ident_f"][:, :],
                              reads=[x, C["ident_f"]], writes=[ps], sig=(k == 3))
                    if g4 == 0:
                        kb.op("act", lambda a: a.copy(out=o[:, 0:512], in_=ps[:, :]), reads=[ps], writes=[o])
                    else:
                        kb.op("dve", lambda v: v.tensor_copy(out=o[:, 512:1024], in_=ps[:, :]), reads=[ps], writes=[o])
                t0 = tt * 512 + c * 128
                kb.dma("pool", out_ap[t0:t0 + 128, :], o[:, :], reads=[o])
NEG = -30000.0
BIG = 10000.0
SCALE = 0.125
TWO_PI = 6.283185307179586
CW1 = 6.28125
CW2 = TWO_PI - CW1
MAGIC = 12582912.0


def attn_host_consts(c):
    k = np.arange(128)[:, None]
    q = np.arange(128)[None, :]
    cur = np.where(k <= q, 0.0, NEG).astype(np.float32)
    prev = np.where(k > q, 0.0, NEG).astype(np.float32)
    c["c_mask_cur"] = np.tile(cur, (1, 4))
    c["c_mask_prev"] = np.tile(prev, (1, 4))
    invf = (500000.0 ** (-np.arange(0, 16, 2, dtype=np.float32) / 16.0)).astype(np.float32)
    c["c_invf"] = np.tile(invf[None, :], (128, 1)).astype(np.float32)
    L = np.zeros((32, 128), np.float32)
    for j in range(8):
        L[j] = np.where(16 * (j - 1) + 31 <= np.arange(128), 0.0, NEG)
    c["c_lneg"] = np.tile(L, (1, 4))
    E = np.zeros((32, 384), np.float32)
    for j in range(8):
        E[j, j + 127] = 1.0
    c["c_eshift"] = E
    X = np.zeros((64, 4096), np.float32)
    for kk in range(4096):
        X[kk // 64, kk] = 1.0
    c["c_expand"] = X
    T = np.zeros((128, 128), np.float32)
    for ql in range(128):
        cb = 1 if ql >= 64 else 0
        for y in range(128):
            jj = y - 64
            T[ql, y] = BIG if jj == cb else (-BIG if jj > cb else 0.0)
    c["c_ftab"] = T
    n_cmp = 255
    sel_lo = np.arange(64)[:, None] * 64
    cmp_lo = np.arange(n_cmp)[None, :] * 16
    ov = np.clip(np.minimum(sel_lo + 64, cmp_lo + 32) - np.maximum(sel_lo, cmp_lo), 0, None) / 32.0
    ovT = np.zeros((256, 64), np.float32)
    ovT[:n_cmp] = ov.T
    c["c_ovT"] = ovT


def ensure_rope(kb, C, P):
    if "cos" in C:
        return
    cos = kb.sb("cos", [128, 32, 8], F32, glob=True)
    sin = kb.sb("sin", [128, 32, 8], F32, glob=True)
    with kb.phase("rope"):
        pi_ = kb.sb("posi", [32, 128], I32)
        kb.dma("sp", pi_[:, :], P.inp("positions").rearrange("(c p) o -> c (p o)", p=128), writes=[pi_])
        pf = kb.sb("posf", [32, 128], F32)
        kb.op("dve", lambda v: v.tensor_copy(out=pf[:, :], in_=pi_[:, :]), reads=[pi_], writes=[pf])
        ps = kb.ps("pps", [128, 32], F32)
        kb.tr(ps[:, :], pf[:, :], C["ident_f"][:32, :32], reads=[pf, C["ident_f"]], writes=[ps])
        pt = kb.sb("post", [128, 32], F32)
        kb.op("dve", lambda v: v.tensor_copy(out=pt[:, :], in_=ps[:, :]), reads=[ps], writes=[pt])
        invf = kb.sb("invf", [128, 8], F32)
        kb.dma("sp", invf[:, :], P.inp("c_invf")[:, :], writes=[invf])
        ang = kb.sb("ang", [128, 32, 8], F32)
        kb.op("dve", lambda v: v.tensor_tensor(out=ang[:, :, :], in0=pt[:, :].unsqueeze(2).broadcast_to([128, 32, 8]),
                                               in1=invf[:, :].unsqueeze(1).broadcast_to([128, 32, 8]), op=ALU.mult),
              reads=[pt, invf], writes=[ang])
        a2 = ang[:, :, :].rearrange("p c f -> p (c f)")
        kf = kb.sb("kf", [128, 256], F32)
        r = kb.sb("rr", [128, 256], F32)
        m = kb.sb("mm", [128, 256], F32)
        kb.op("dve", lambda v: v.tensor_scalar(out=kf[:, :], in0=a2, scalar1=1.0 / TWO_PI, scalar2=MAGIC,
                                               op0=ALU.mult, op1=ALU.add), reads=[ang], writes=[kf])
        kb.op("dve", lambda v: v.tensor_scalar(out=kf[:, :], in0=kf[:, :], scalar1=-MAGIC, scalar2=None,
                                               op0=ALU.add), reads=[kf], writes=[kf])
        kb.op("dve", lambda v: v.scalar_tensor_tensor(out=r[:, :], in0=kf[:, :], scalar=-CW1, in1=a2,
                                                      op0=ALU.mult, op1=ALU.add), reads=[kf, ang], writes=[r])
        kb.op("dve", lambda v: v.scalar_tensor_tensor(out=r[:, :], in0=kf[:, :], scalar=-CW2, in1=r[:, :],
                                                      op0=ALU.mult, op1=ALU.add), reads=[kf, r], writes=[r])
        PI_C = 3.1415925
        kb.op("dve", lambda v: v.tensor_scalar(out=r[:, :], in0=r[:, :], scalar1=PI_C, scalar2=-PI_C,
                                               op0=ALU.min, op1=ALU.max), reads=[r], writes=[r])
        kb.op("act", lambda a: a.activation(out=sin[:, :, :].rearrange("p c f -> p (c f)"), in_=r[:, :], func=AF.Sin),
              reads=[r], writes=[sin])
        kb.op("dve", lambda v: v.tensor_scalar(out=m[:, :], in0=r[:, :], scalar1=np.pi / 2, scalar2=-TWO_PI,
                                               op0=ALU.is_gt, op1=ALU.mult), reads=[r], writes=[m])
        kb.op("dve", lambda v: v.scalar_tensor_tensor(out=m[:, :], in0=r[:, :], scalar=np.pi / 2, in1=m[:, :],
                                                      op0=ALU.add, op1=ALU.add), reads=[r, m], writes=[m])
        kb.op("dve", lambda v: v.tensor_scalar(out=m[:, :], in0=m[:, :], scalar1=PI_C, scalar2=-PI_C,
                                               op0=ALU.min, op1=ALU.max), reads=[m], writes=[m])
        kb.op("act", lambda a: a.activation(out=cos[:, :, :].rearrange("p c f -> p (c f)"), in_=m[:, :], func=AF.Sin),
              reads=[m], writes=[cos])
    C["cos"] = cos
    C["sin"] = sin


def load_w_bf16(kb, name, w_ap, nk, F, q="pool"):
    t = kb.sb(name, [128, nk, F], BF16)
    for k in range(nk):
        kb.dma(q, t[:, k, :], w_ap[k * 128:(k + 1) * 128, :], writes=[t])
    return t


def load_const_bf16(kb, name, ap, shape):
    t = kb.sb(name, shape, BF16)
    kb.dma("pool", t[:, :], ap[:, :], writes=[t])
    return t


def emit_rope(kb, C, x3, nshape, i, tmp, reads_writes):
    cosb = C["cos"][:, i, :]
    sinb = C["sin"][:, i, :]
    for _ in nshape:
        cosb = cosb.unsqueeze(1)
        sinb = sinb.unsqueeze(1)
    shp = [128] + list(nshape) + [8]
    cosb = cosb.broadcast_to(shp)
    sinb = sinb.broadcast_to(shp)
    pre = (slice(None),) * (1 + len(nshape))
    x1 = x3[pre + (slice(0, 8),)]
    x2 = x3[pre + (slice(8, 16),)]
    t1, t2, t3, t4 = tmp
    rw = reads_writes
    kb.op("dve", lambda v: v.tensor_tensor(out=t1, in0=x1, in1=cosb, op=ALU.mult), reads=[rw, C["cos"]], writes=[rw])
    kb.op("dve", lambda v: v.tensor_tensor(out=t2, in0=x2, in1=sinb, op=ALU.mult), reads=[rw, C["sin"]], writes=[rw])
    kb.op("dve", lambda v: v.tensor_tensor(out=t3, in0=x2, in1=cosb, op=ALU.mult), reads=[rw, C["cos"]], writes=[rw])
    kb.op("dve", lambda v: v.tensor_tensor(out=t4, in0=x1, in1=sinb, op=ALU.mult), reads=[rw, C["sin"]], writes=[rw])
    kb.op("dve", lambda v: v.tensor_tensor(out=x1, in0=t1, in1=t2, op=ALU.subtract), reads=[rw], writes=[rw])
    kb.op("dve", lambda v: v.tensor_tensor(out=x2, in0=t3, in1=t4, op=ALU.add), reads=[rw], writes=[rw])


def ph_nsa_pre(kb, C, P, hT, layer, inst):
    ensure_rope(kb, C, P)
    kcmpT = kb.sb("kcmpT", [64, 4, 256], BF16, glob=True)
    vcmp = kb.sb("vcmp", [128, 2, 4, 128], BF16, glob=True)
    w_in = P.inp("nsa_w_in")[inst]
    with kb.phase("nsa_pre"):
        lnw = load_vec_pk(kb, "lnw", P.inp("ln_mix")[layer], 8)
        wkv = kb.sb("wkv", [128, 8, 512], BF16)
        for k in range(8):
            kb.dma("pool", wkv[:, k, :], w_in[k * 128:(k + 1) * 128, 1024:1536], writes=[wkv])
        w1 = {}
        for nm in ("k", "v"):
            w1[nm] = kb.sb("w1" + nm, [64, 32, 256], BF16)
            kb.dma("pool", w1[nm][:, :, :], P.inp(f"nsa_{nm}_w1")[inst].rearrange("(l d) h -> d l h", d=64),
                   writes=[w1[nm]])
        w2 = {}
        for nm in ("k", "v"):
            w2[nm] = kb.sb("w2" + nm, [128, 2, 64], BF16)
            kb.dma("pool", w2[nm][:, :, :], P.inp(f"nsa_{nm}_w2")[inst].rearrange("(c p) d -> p c d", p=128),
                   writes=[w2[nm]])
        peT = {}
        aux = kb.ps("aux", [128, 512], F32)
        for nm in ("k", "v"):
            pe32 = kb.sb("pe32" + nm, [32, 64], F32)
            kb.dma("sp", pe32[:, :], P.inp(f"nsa_pe_{nm}")[inst][:, :], writes=[pe32])
            pps = aux
            kb.tr(pps[0:64, 0:32], pe32[:, :], C["ident_f"][:32, :32], reads=[pe32, C["ident_f"]], writes=[pps])
            peT[nm] = kb.sb("peT" + nm, [64, 32], BF16)
            kb.op("dve", lambda v: v.tensor_copy(out=peT[nm][:, :], in_=pps[0:64, 0:32]), reads=[pps], writes=[peT[nm]])
        kT = kb.sb("kcT", [64, 4, S], BF16)
        vT = kb.sb("vcT", [64, 4, S], BF16)
        hb = [kb.sb("h", [128, 8, 128], F32) for _ in range(2)]
        xn = kb.sb("xn", [128, 8, 128], BF16)
        nb = NormBufs(kb, 128)
        pj = [kb.ps("pj", [128, 512], F32) for _ in range(2)]
        kv = kb.sb("kv", [128, 512], F32)
        kvb = kb.sb("kvb", [128, 512], BF16)
        tmp = kb.sb("rt", [128, 4, 4, 8], F32)
        tp = [kb.ps("tp", [64, 512], BF16) for _ in range(2)]

        def load_h(i):
            kb.dma("sp", hb[i % 2][:, :, :], hview(hT, i * 128, 128), writes=[hb[i % 2]])
        load_h(0)
        for i in range(NCH):
            if i + 1 < NCH:
                load_h(i + 1)
            h = hb[i % 2]
            emit_norm(kb, C, nb, h, lnw, xn, 128)
            ps = pj[i % 2]
            for k in range(8):
                kb.mm(ps[:, :], xn[:, k, :], wkv[:, k, :], start=(k == 0), stop=(k == 7), reads=[xn, wkv], writes=[ps])
            kb.op("act", lambda a: a.copy(out=kv[:, :], in_=ps[:, :]), reads=[ps], writes=[kv])
            x3 = kv[:, 0:256].rearrange("p (h d) -> p h d", d=64)
            emit_rope(kb, C, x3, [4], i, [tmp[:, j, :, :] for j in range(4)], kv)
            kb.op("dve", lambda v: v.tensor_copy(out=kvb[:, :], in_=kv[:, :]), reads=[kv], writes=[kvb])
            for half, dst in ((0, kT), (1, vT)):
                t = tp[half]
                for g in range(4):
                    c0 = half * 256 + g * 64
                    kb.tr(t[:, g * 128:(g + 1) * 128], kvb[:, c0:c0 + 64], C["ident_b"][:, :],
                          reads=[kvb, C["ident_b"]], writes=[t], sig=(g == 3))
                eng = "act" if half == 0 else "pool"
                if half == 0:
                    kb.op("act", lambda a: a.copy(out=dst[:, :, i * 128:(i + 1) * 128],
                                                  in_=t[:, :].rearrange("p (g t) -> p g t", g=4)),
                          reads=[t], writes=[dst])
                else:
                    kb.op("dve", lambda v: v.tensor_copy(out=dst[:, :, i * 128:(i + 1) * 128],
                                                         in_=t[:, :].rearrange("p (g t) -> p g t", g=4)),
                          reads=[t], writes=[dst])
        hps = pj
        bps = aux
        ops_ = aux
        kb.op("dve", lambda v: v.memset(vcmp[:, :, :, :], 1.0), writes=[vcmp])
        kb.op("dve", lambda v: v.memset(kcmpT[:, :, :], 0.0), writes=[kcmpT])
        ov32 = kb.sb("ov32", [128, 2, 64], F32)
        kb.dma("sp", ov32[:, :, :], P.inp("c_ovT").rearrange("(c p) j -> p c j", p=128), writes=[ov32])
        for g in range(4):
            kb.op("dve", lambda v: v.tensor_copy(out=vcmp[:, :, g, 65:128], in_=ov32[:, :, 1:64]), reads=[ov32], writes=[vcmp])
        n = 0
        for nm, src in (("k", kT), ("v", vT)):
            bias = kb.sb("cb" + nm, [128, 2], F32)
            for hc in range(2):
                for l in range(32):
                    kb.mm(bps[:, 32 + hc:33 + hc], w1[nm][:, l, hc * 128:(hc + 1) * 128], peT[nm][:, l:l + 1],
                          start=(l == 0), stop=(l == 31), reads=[w1[nm], peT[nm]], writes=[bps])
            kb.op("dve", lambda v: v.tensor_copy(out=bias[:, :], in_=bps[:, 32:34]), reads=[bps], writes=[bias])
            for g in range(4):
                hid = kb.sb("hid", [128, 2, 256], BF16)
                for hc in range(2):
                    hp = hps[n % 2]
                    n += 1
                    for l in range(32):
                        rhs = src[:, g, :].rearrange("p (c s) -> p c s", s=16)[:, l // 16:l // 16 + 255, l % 16]
                        kb.mm(hp[:, 0:255], w1[nm][:, l, hc * 128:(hc + 1) * 128], rhs, start=(l == 0), stop=(l == 31),
                              reads=[w1[nm], src], writes=[hp])
                    kb.op("act", lambda a: a.activation(out=hid[:, hc, 0:255], in_=hp[:, 0:255], func=AF.Silu,
                                                        bias=bias[:, hc:hc + 1]),
                          reads=[hp, bias], writes=[hid])
                if nm == "k":
                    for hc in range(2):
                        kb.mm(ops_[0:64, 256:511], w2[nm][:, hc, :], hid[:, hc, 0:255], start=(hc == 0), stop=(hc == 1),
                              reads=[w2[nm], hid], writes=[ops_])
                    kb.op("dve", lambda v: v.tensor_copy(out=kcmpT[:, g, 0:255], in_=ops_[0:64, 256:511]),
                          reads=[ops_], writes=[kcmpT])
                else:
                    for cc in range(2):
                        M = 128 if cc == 0 else 127
                        for hc in range(2):
                            kb.mm(ops_[0:M, 256 + cc * 64:256 + (cc + 1) * 64], hid[:, hc, cc * 128:cc * 128 + M], w2[nm][:, hc, :],
                                  start=(hc == 0), stop=(hc == 1), reads=[w2[nm], hid], writes=[ops_])
                        kb.op("dve", lambda v: v.tensor_copy(out=vcmp[0:M, cc, g, 0:64], in_=ops_[0:M, 256 + cc * 64:256 + (cc + 1) * 64]),
                              reads=[ops_], writes=[vcmp])
    C["kcmpT"] = kcmpT
    C["vcmp"] = vcmp


def ph_attn(kb, C, P, hT, layer, inst, kind):
    ensure_rope(kb, C, P)
    nsa = kind == "nsa"
    if nsa:
        ph_nsa_pre(kb, C, P, hT, layer, inst)
        w_in = P.inp("nsa_w_in")[inst]
        F = 2608
        w_o = P.inp("nsa_w_o")[inst]
    else:
        w_in = P.inp("swa_w_qkv")[inst]
        F = 1536
        w_o = P.inp("swa_w_o")[inst]
    NCT = (F + 511) // 512
    with kb.phase("attn"):
        lnw = load_vec_pk(kb, "lnw", P.inp("ln_mix")[layer], 8)
        wq = load_w_bf16(kb, "wq", w_in, 8, F)
        wo = load_w_bf16(kb, "wo", w_o, 8, D)
        mcur = load_const_bf16(kb, "mcur", P.inp("c_mask_cur"), [128, 512])
        mprev = load_const_bf16(kb, "mprev", P.inp("c_mask_prev"), [128, 512])
        if nsa:
            lneg = load_const_bf16(kb, "lneg", P.inp("c_lneg"), [32, 512])
            esh = load_const_bf16(kb, "esh", P.inp("c_eshift"), [32, 384])
            ftab = kb.sb("ftab", [128, 128], F32)
            kb.dma("sp", ftab[:, :], P.inp("c_ftab")[:, :], writes=[ftab])
            kcmpT, vcmp = C["kcmpT"], C["vcmp"]
            branches = ["sel", "win"]
            kcol = {"sel": 1536, "win": 2048}
            vcol = {"sel": 1792, "win": 2304}
            nback = {"sel": 10 ** 6, "win": 4}
        else:
            bias = kb.sb("bqkv", [128, F], F32)
            kb.dma("sp", bias[:, :], P.inp("swa_b_qkv")[inst].partition_broadcast(128), writes=[bias])
            bo = load_vec_pk(kb, "bo", P.inp("swa_b_o")[inst], 8)
            esink = kb.sb("esink", [128, 16], F32)
            kb.dma("sp", esink[:, :], P.inp("swa_sinks")[inst].partition_broadcast(128), writes=[esink])
            kb.op("act", lambda a: a.activation(out=esink[:, :], in_=esink[:, :], func=AF.Exp), reads=[esink], writes=[esink])
            branches = ["win"]
            kcol = {"win": 1024}
            vcol = {"win": 1280}
            nback = {"win": 1}
        ring = {b: (8 if b == "win" else NCH) for b in branches}
        KT = {b: kb.sb("KT" + b, [128 if b == "sel" else 64, 4, ring[b] * 128], BF16) for b in branches}
        if nsa:
            for g in range(4):
                kb.dma("pool", KT["sel"][64:128, g, :], P.inp("c_expand")[:, :], writes=[KT["sel"]])
            qsel_res = [Res("qsel%d" % g) for g in range(4)]
        VC = {b: kb.sb("VC" + b, [128, ring[b], 4, 65], BF16) for b in branches}
        for b in branches:
            kb.op("pool", lambda g_: g_.memset(VC[b][:, :, :, :], 1.0), writes=[VC[b]])
        hb = [kb.sb("h", [128, 8, 128], F32) for _ in range(2)]
        xn = kb.sb("xn", [128, 8, 128], BF16)
        nb = NormBufs(kb, 128)
        pj = [kb.ps("pj", [128, 512], F32) for _ in range(2)]
        tpp = kb.ps("tp", [128, 1024], BF16)
        stp = [kb.ps("st", [128, 512], F32) for _ in range(2)]
        opp = [kb.ps("op", [128, 512], F32) for _ in range(2)]
        qkv = kb.sb("qkv", [128, F], F32)
        qkb = kb.sb("qkb", [128, F], BF16)
        QT = kb.sb("QT", [128, 16, 128], BF16)
        rtq = kb.sb("rtq", [128, 4, 16, 8], F32)
        rtk = kb.sb("rtk", [128, 4, 3, 4, 8], F32)
        pT = [kb.sb("pT", [128, 512], BF16) for _ in range(4)]
        osb = kb.sb("osb", [128, 16, 64], F32)
        osbb = kb.sb("osbb", [128, D], BF16)
        oT = kb.sb("oT", [128, 8, 128], BF16)
        otmp = kb.sb("otmp", [128, 4, 64], F32)
        den = kb.sb("den", [128, 4], F32)
        coef = kb.sb("coef", [128, 4], F32)
        hout = [kb.sb("hout", [128, 8, 128], F32) for _ in range(2)]
        if nsa:
            gates = kb.sb("gates", [128, 48], F32)
            imp = kb.sb("imp", [128, 64], F32)
            top8 = kb.sb("top8", [128, 8], F32)
            nselb = kb.sb("nselb", [128, 64], BF16)
        st = {"p": 0, "s": 0, "o": 0}

        def load_h(i):
            kb.dma("sp", hb[i % 2][:, :, :], hview(hT, i * 128, 128), writes=[hb[i % 2]])

        def make_unit(items, lhsT_k, M, rhs_q, masks, v_rhs, ncol, o_ref, kreads, vreads, first_in_group, post):
            slot = {}

            def A():
                sps = stp[st["s"] % 2]
                st["s"] += 1
                pt = pT[st["p"] % 4]
                st["p"] += 1
                slot["pt"] = pt
                kb.mm(sps[0:M, :], lhsT_k, rhs_q, start=True, stop=(len(masks) == 0), reads=kreads + [QT], writes=[sps])
                for mi, (ml, mr, mrd) in enumerate(masks):
                    kb.mm(sps[0:M, :], ml, mr, start=False, stop=(mi == len(masks) - 1), reads=mrd, writes=[sps])
                kb.op("act", lambda a: a.activation(out=pt[0:M, :], in_=sps[0:M, :], func=AF.Exp, scale=SCALE),
                      reads=[sps], writes=[pt])

            def B():
                o_ps = o_ref["t"]
                pt = slot["pt"]
                if first_in_group:
                    kb.op("dve", lambda v: v.memset(o_ps[:, :], 0.0), writes=[o_ps])
                for r in range(4):
                    kb.mm(o_ps[:, r * 128:r * 128 + ncol], pt[0:M, r * 128:(r + 1) * 128], v_rhs, start=False, stop=False,
                          reads=[pt] + vreads, writes=[o_ps], sig=(r == 3), skip_group_check=True)
                if post is not None:
                    post()
            items.append((A, B))

        def run_items(items):
            LA = 1
            for n, (A, B) in enumerate(items):
                A()
                if n >= LA:
                    items[n - LA][1]()
            for n in range(max(0, len(items) - LA), len(items)):
                items[n][1]()

        def new_ref():
            ref = {"t": opp[st["o"] % 2]}
            st["o"] += 1
            return ref

        def evac(o_ps, g, ncol, first, gate_col, extra_den=None):
            o3 = o_ps[:, :].rearrange("p (r c) -> p r c", c=128)
            kb.op("dve", lambda v: v.tensor_scalar(out=den[:, :].unsqueeze(2), in0=o3[:, :, 64:65], scalar1=1e-30, scalar2=None,
                                                   op0=ALU.max), reads=[o_ps], writes=[den])
            if extra_den is not None:
                kb.op("dve", lambda v: v.tensor_tensor(out=den[:, :], in0=den[:, :], in1=extra_den, op=ALU.add),
                      reads=[den, esink], writes=[den])
            kb.op("dve", lambda v: v.reciprocal(out=den[:, :], in_=den[:, :]), reads=[den], writes=[den])
            if gate_col is not None:
                gv = gates[:, :].rearrange("p (h b) -> p h b", b=3)[:, 4 * g:4 * g + 4, gate_col]
                kb.op("dve", lambda v: v.tensor_tensor(out=coef[:, :], in0=den[:, :], in1=gv, op=ALU.mult),
                      reads=[den, gates], writes=[coef])
                cf = coef
            else:
                cf = den
            cb_ = cf[:, :].unsqueeze(2).broadcast_to([128, 4, 64])
            if first:
                kb.op("dve", lambda v: v.tensor_tensor(out=osb[:, 4 * g:4 * g + 4, :], in0=o3[:, :, 0:64], in1=cb_,
                                                       op=ALU.mult), reads=[o_ps, cf], writes=[osb])
            else:
                kb.op("dve", lambda v: v.tensor_tensor(out=otmp[:, :, :], in0=o3[:, :, 0:64], in1=cb_, op=ALU.mult),
                      reads=[o_ps, cf], writes=[otmp])
                kb.op("dve", lambda v: v.tensor_tensor(out=osb[:, 4 * g:4 * g + 4, :], in0=osb[:, 4 * g:4 * g + 4, :],
                                                       in1=otmp[:, :, :], op=ALU.add), reads=[otmp, osb], writes=[osb])

        load_h(0)
        for i in range(NCH):
            if i + 1 < NCH:
                load_h(i + 1)
            h = hb[i % 2]
            emit_norm(kb, C, nb, h, lnw, xn, 128)
            for ct in range(NCT):
                c0 = ct * 512
                c1 = min(F, c0 + 512)
                ps = pj[ct % 2]
                for k in range(8):
                    kb.mm(ps[:, 0:c1 - c0], xn[:, k, :], wq[:, k, c0:c1], start=(k == 0), stop=(k == 7),
                          reads=[xn, wq], writes=[ps])
                if nsa:
                    kb.op("act", lambda a: a.copy(out=qkv[:, c0:c1], in_=ps[:, 0:c1 - c0]), reads=[ps], writes=[qkv])
                else:
                    kb.op("dve", lambda v: v.tensor_tensor(out=qkv[:, c0:c1], in0=ps[:, 0:c1 - c0], in1=bias[:, c0:c1],
                                                           op=ALU.add), reads=[ps, bias], writes=[qkv])
            q3 = qkv[:, 0:1024].rearrange("p (h d) -> p h d", d=64)
            emit_rope(kb, C, q3, [16], i, [rtq[:, j, :, :] for j in range(4)], qkv)
            if nsa:
                k4 = qkv[:, 1024:2560].rearrange("p (b h d) -> p b h d", b=3, d=64)[:, :, 0:4, :]
                emit_rope(kb, C, k4, [3, 4], i, [rtk[:, j, :, :, :] for j in range(4)], qkv)
                kb.op("act", lambda a: a.activation(out=gates[:, :], in_=qkv[:, 2560:2608], func=AF.Sigmoid),
                      reads=[qkv], writes=[gates])
            else:
                k3 = qkv[:, 1024:1280].rearrange("p (h d) -> p h d", d=64)
                emit_rope(kb, C, k3, [4], i, [rtk[:, j, 0, :, :] for j in range(4)], qkv)
            kb.op("act", lambda a: a.copy(out=qkb[:, 0:F], in_=qkv[:, 0:F]), reads=[qkv], writes=[qkb])
            for hh in range(2):
                for j in range(8):
                    hd = hh * 8 + j
                    kb.tr(tpp[0:64, j * 128:(j + 1) * 128], qkb[:, hd * 64:(hd + 1) * 64], C["ident_b"][:, :],
                          reads=[qkb, C["ident_b"]], writes=[tpp], sig=(j == 7))
                kb.op("act", lambda a: a.copy(out=QT[0:64, hh * 8:(hh + 1) * 8, :],
                                              in_=tpp[0:64, :].rearrange("p (j t) -> p j t", j=8)),
                      reads=[tpp], writes=[QT])
            for bi, b in enumerate(branches):
                for g in range(4):
                    c0 = kcol[b] + g * 64
                    kb.tr(tpp[0:64, (bi * 4 + g) * 128:(bi * 4 + g + 1) * 128], qkb[:, c0:c0 + 64], C["ident_b"][:, :],
                          reads=[qkb, C["ident_b"]], writes=[tpp], sig=(g == 3))
                isl = i % ring[b]
                kb.op("act", lambda a: a.copy(out=KT[b][0:64, :, isl * 128:(isl + 1) * 128],
                                              in_=tpp[0:64, bi * 512:(bi + 1) * 512].rearrange("p (g t) -> p g t", g=4)),
                      reads=[tpp], writes=[KT[b]])
                kb.op("act", lambda a: a.copy(out=VC[b][:, isl, :, 0:64],
                                              in_=qkv[:, vcol[b]:vcol[b] + 256].rearrange("p (g d) -> p g d", d=64)),
                      reads=[qkv], writes=[VC[b]])
            items = []

            def selection(g, o_ps):
                o3 = o_ps[:, :].rearrange("p (r c) -> p r c", c=128)
                nst = QT[64:128, 4 * g:4 * g + 4, :]
                kb.op("dve", lambda v: v.memset(imp[:, 0:1], 0.0), writes=[imp])
                for r in range(4):
                    if r == 0:
                        kb.op("dve", lambda v: v.tensor_scalar(out=imp[:, 1:64], in0=o3[:, 0, 65:128], scalar1=den[:, 0:1],
                                                               scalar2=None, op0=ALU.mult),
                              reads=[o_ps, den], writes=[imp])
                    else:
                        kb.op("dve", lambda v: v.scalar_tensor_tensor(out=imp[:, 1:64], in0=o3[:, r, 65:128],
                                                                      scalar=den[:, r:r + 1], in1=imp[:, 1:64],
                                                                      op0=ALU.mult, op1=ALU.add),
                              reads=[o_ps, den, imp], writes=[imp])
                kb.op("dve", lambda v: v.tensor_tensor(out=imp[:, :], in0=imp[:, :], in1=ftab[:, 64 - 2 * i:128 - 2 * i],
                                                       op=ALU.add), reads=[imp, ftab], writes=[imp])
                kb.op("dve", lambda v: v.tensor_scalar(out=imp[:, 0:1], in0=imp[:, 0:1], scalar1=BIG, scalar2=None,
                                                       op0=ALU.add), reads=[imp], writes=[imp])
                kb.op("dve", lambda v: v.max(out=top8[:, :], in_=imp[:, :]), reads=[imp], writes=[top8])
                kb.op("dve", lambda v: v.tensor_scalar(out=imp[:, :], in0=imp[:, :], scalar1=top8[:, 7:8], scalar2=-1.0,
                                                       op0=ALU.is_ge, op1=ALU.add), reads=[imp, top8], writes=[imp])
                kb.op("dve", lambda v: v.tensor_scalar(out=nselb[:, :], in0=imp[:, :], scalar1=-NEG, scalar2=None,
                                                       op0=ALU.mult), reads=[imp], writes=[nselb])
                kb.tr(tpp[0:64, 0:128], nselb[:, :], C["ident_b"][:, :], reads=[nselb, C["ident_b"]], writes=[tpp])
                kb.op("dve", lambda v: v.tensor_copy(out=nst,
                                                     in_=tpp[0:64, 0:128].unsqueeze(1).broadcast_to([64, 4, 128])),
                      reads=[tpp], writes=[qsel_res[g]])

            if nsa:
                for g in range(4):
                    rhs_q = QT[0:64, 4 * g:4 * g + 4, :]
                    ref = new_ref()
                    ncv = min(255, 8 * i + 7)
                    ccs = [cc for cc in range(2) if min(128, ncv - 128 * cc) > 0]
                    for ci, cc in enumerate(ccs):
                        M = min(128, ncv - 128 * cc)
                        o = 8 * i - 128 * cc
                        masks = []
                        if 0 <= o <= 128:
                            masks.append((esh[:, 128 - o:128 - o + M], lneg[:, :], [esh, lneg]))
                        post = None
                        if ci == len(ccs) - 1:
                            def post(g=g, ref=ref):
                                evac(ref["t"], g, 128, True, 0)
                                selection(g, ref["t"])
                        make_unit(items, kcmpT[:, g, cc * 128:cc * 128 + M], M, rhs_q, masks, vcmp[0:M, cc, g, :], 128, ref,
                                  [kcmpT], [vcmp], ci == 0, post)
            for g in range(4):
                for bi_, b in enumerate(branches):
                    ref = new_ref()
                    lo = max(0, i - nback[b])
                    kcs = list(range(lo, i + 1))
                    for ki, kc in enumerate(kcs):
                        masks = []
                        if kc == i:
                            masks.append((C["ident_b"][:, :], mcur[:, :], [C["ident_b"], mcur]))
                        elif b == "win" and kc == i - nback[b]:
                            masks.append((C["ident_b"][:, :], mprev[:, :], [C["ident_b"], mprev]))
                        ks_ = kc % ring[b]
                        post = None
                        if ki == len(kcs) - 1:
                            if nsa:
                                def post(g=g, ref=ref, b=b):
                                    evac(ref["t"], g, 65, False, 1 if b == "sel" else 2)
                            else:
                                def post(g=g, ref=ref):
                                    evac(ref["t"], g, 65, True, None, extra_den=esink[:, 4 * g:4 * g + 4])
                        if b == "sel":
                            lhs = KT[b][:, g, ks_ * 128:(ks_ + 1) * 128]
                            rq = QT[:, 4 * g:4 * g + 4, :]
                            kr = [KT[b], qsel_res[g]]
                        else:
                            lhs = KT[b][0:64, g, ks_ * 128:(ks_ + 1) * 128]
                            rq = QT[0:64, 4 * g:4 * g + 4, :]
                            kr = [KT[b]]
                        make_unit(items, lhs, 128, rq, masks, VC[b][:, ks_, g, :], 65, ref, kr, [VC[b]], ki == 0, post)
            run_items(items)
            kb.op("act", lambda a: a.copy(out=osbb[:, :], in_=osb[:, :, :].rearrange("p h d -> p (h d)")),
                  reads=[osb], writes=[osbb])
            for k in range(8):
                kb.tr(tpp[:, k * 128:(k + 1) * 128], osbb[:, k * 128:(k + 1) * 128], C["ident_b"][:, :],
                      reads=[osbb, C["ident_b"]], writes=[tpp], sig=(k == 7))
            kb.op("act", lambda a: a.copy(out=oT[:, :, :], in_=tpp[:, :].rearrange("p (k t) -> p k t", k=8)),
                  reads=[tpp], writes=[oT])
            ho = hout[i % 2]
            for half in range(2):
                ps = pj[half]
                for dcl in range(4):
                    dc = half * 4 + dcl
                    for k in range(8):
                        kb.mm(ps[:, dcl * 128:(dcl + 1) * 128], wo[:, k, dc * 128:(dc + 1) * 128], oT[:, k, :],
                              start=(k == 0), stop=(k == 7), reads=[wo, oT], writes=[ps], sig=(k == 7 and dcl == 3))
                hv = h[:, half * 4:(half + 1) * 4, :]
                kb.op("dve", lambda v: v.tensor_tensor(out=ho[:, half * 4:(half + 1) * 4, :],
                                                       in0=ps[:, :].rearrange("p (c t) -> p c t", c=4), in1=hv, op=ALU.add),
                      reads=[ps, h], writes=[ho])
            if not nsa:
                kb.op("dve", lambda v: v.tensor_tensor(out=ho[:, :, :], in0=ho[:, :, :],
                                                       in1=bo[:, :].unsqueeze(2).broadcast_to([128, 8, 128]), op=ALU.add),
                      reads=[ho, bo], writes=[ho])
            kb.dma("sp", hview(hT, i * 128, 128), ho[:, :, :], reads=[ho])


def ph_swa(kb, C, P, hT, layer, inst):
    ph_attn(kb, C, P, hT, layer, inst, "swa")


def ph_nsa(kb, C, P, hT, layer, inst):
    ph_attn(kb, C, P, hT, layer, inst, "nsa")
def mamba_host_consts(c):
    s1 = np.arange(128)[:, None]
    s2 = np.arange(128)[None, :]
    c["c_triU"] = (s1 <= s2).astype(np.float32)
    c["c_strictL"] = (s1 > s2).astype(np.float32)


def load_bcast(kb, name, ap1d, n, q="sp"):
    t = kb.sb(name, [128, n], F32)
    kb.dma(q, t[:, :], ap1d.partition_broadcast(128), writes=[t])
    return t


def ph_mamba(kb, C, P, hT, layer, inst):
    nc = kb.nc
    if "ydram" not in C:
        C["ydram"] = nc.dram_tensor("ydram", [S, 2048], F32, kind="Internal").ap()
    ydram = C["ydram"]
    w_in = P.inp("ssm_w_in")[inst]
    with kb.phase("mamba_a"):
        lnw = load_vec_pk(kb, "lnw", P.inp("ln_mix")[layer], 8)
        wx = kb.sb("wx", [128, 8, 4096], BF16)
        wdt = kb.sb("wdt", [128, 8, 32], BF16)
        for k in range(8):
            kb.dma("pool", wx[:, k, :], w_in[k * 128:(k + 1) * 128, 2048:6144], writes=[wx])
            kb.dma("pool", wdt[:, k, :], w_in[k * 128:(k + 1) * 128, 6144:6176], writes=[wdt])
        triU = kb.sb("triU", [128, 128], F32)
        kb.dma("sp", triU[:, :], P.inp("c_triU")[:, :], writes=[triU])
        strictL = kb.sb("strictL", [128, 128], F32)
        kb.dma("sp", strictL[:, :], P.inp("c_strictL")[:, :], writes=[strictL])
        pj = [kb.ps("pj", [128, 512], F32) for _ in range(2)]
        tp = kb.ps("tp", [128, 1024], BF16)
        sm = kb.ps("sm", [128, 512], F32)
        cbp = kb.ps("cbp", [128, 512], F32)
        stp_ = kb.ps("stp", [128, 512], F32)
        df = kb.ps("df", [128, 512], F32)
        yy = kb.ps("yy", [128, 512], F32)
        nb = NormBufs(kb, 128, ss=sm, off=128)
        cw_raw = kb.sb("cw_raw", [128, 128], F32)
        kb.dma("sp", cw_raw[:, :], P.inp("ssm_conv_w")[inst].rearrange("k (cc p) -> (k cc) p", p=128), writes=[cw_raw])
        kb.tr(pj[0][:, 0:128], cw_raw[:, :], C["ident_f"][:, :], reads=[cw_raw, C["ident_f"]], writes=[pj[0]])
        cwT = kb.sb("cwT", [128, 128], F32)
        kb.op("dve", lambda v: v.tensor_copy(out=cwT[:, :], in_=pj[0][:, 0:128]), reads=[pj[0]], writes=[cwT])
        cb_raw = kb.sb("cb_raw", [32, 128], F32)
        kb.dma("sp", cb_raw[:, :], P.inp("ssm_conv_b")[inst].rearrange("(cc p) -> cc p", p=128), writes=[cb_raw])
        kb.tr(pj[1][:, 0:32], cb_raw[:, :], C["ident_f"][:32, :32], reads=[cb_raw, C["ident_f"]], writes=[pj[1]])
        cbT = kb.sb("cbT", [128, 32], F32)
        kb.op("dve", lambda v: v.tensor_copy(out=cbT[:, :], in_=pj[1][:, 0:32]), reads=[pj[1]], writes=[cbT])
        diag = kb.sb("diag", [128, 128, 128], BF16)
        for idx in range(128):
            e = "dve" if idx % 2 == 0 else "pool"
            kb.op(e, lambda v: v.tensor_scalar(out=diag[:, idx, :], in0=C["ident_f"][:, :], scalar1=cwT[:, idx:idx + 1],
                                               scalar2=None, op0=ALU.mult), reads=[C["ident_f"], cwT], writes=[diag])
        dtb = load_bcast(kb, "dtb", P.inp("ssm_dt_bias")[inst], 32)
        aneg = load_bcast(kb, "aneg", P.inp("ssm_a_log")[inst], 32)
        kb.op("act", lambda a: a.activation(out=aneg[:, :], in_=aneg[:, :], func=AF.Exp), reads=[aneg], writes=[aneg])
        kb.op("dve", lambda v: v.tensor_scalar(out=aneg[:, :], in0=aneg[:, :], scalar1=-1.0, scalar2=None, op0=ALU.mult),
              reads=[aneg], writes=[aneg])
        dsk = load_bcast(kb, "dsk", P.inp("ssm_d")[inst], 32)
        one = kb.sb("one", [128, 1], F32)
        kb.op("dve", lambda v: v.memset(one[:, :], 1.0), writes=[one])

        hb = [kb.sb("h", [128, 8, 128], F32) for _ in range(2)]
        xn = kb.sb("xn", [128, 8, 128], BF16)
        raw = kb.sb("raw", [128, 32, 131], BF16)
        kb.op("dve", lambda v: v.memset(raw[:, :, :], 0.0), writes=[raw])
        xbcT = kb.sb("xbcT", [128, 32, 128], BF16)
        xs_tm = kb.sb("xs_tm", [128, 32, 64], BF16)
        B_tm = kb.sb("B_tm", [128, 8, 128], BF16)
        dtt = kb.sb("dtt", [128, 32], F32)
        ad = kb.sb("ad", [128, 32], F32)
        acum = kb.sb("acum", [128, 32], F32)
        eacum = kb.sb("eacum", [128, 32], F32)
        dst = kb.sb("dst", [128, 32], F32)
        cdec = kb.sb("cdec", [128, 32], F32)
        xd = kb.sb("xd", [128, 32, 64], BF16)
        xdd = kb.sb("xdd", [128, 32, 64], BF16)
        cbm2 = [kb.sb("cbm", [128, 128], BF16) for _ in range(2)]
        Rm = kb.sb("Rm", [128, 16, 128], F32)
        decT2 = [kb.sb("decT", [128, 4, 128], BF16) for _ in range(2)]
        MT2 = [kb.sb("MT", [128, 4, 128], BF16) for _ in range(2)]
        t1 = kb.sb("t1", [128, 4, 64], F32)
        t2 = kb.sb("t2", [128, 4, 64], F32)
        ypre = [kb.sb("ypre", [128, 32, 64], F32) for _ in range(1)]
        prev = kb.sb("prev", [128, 32, 64], F32)
        prevb = kb.sb("prevb", [128, 32, 64], BF16)
        kb.op("dve", lambda v: v.memset(prev[:, :, :], 0.0), writes=[prev])
        kb.op("dve", lambda v: v.memset(prevb[:, :, :], 0.0), writes=[prevb])

        def load_h(i):
            kb.dma("sp", hb[i % 2][:, :, :], hview(hT, i * 128, 128), writes=[hb[i % 2]])
        def proj_slice(cc4):
            ps = pj[cc4 % 2]
            for c in range(4):
                cc = cc4 * 4 + c
                for k in range(8):
                    kb.mm(ps[:, c * 128:(c + 1) * 128], wx[:, k, cc * 128:(cc + 1) * 128], xn[:, k, :],
                          start=(k == 0), stop=(k == 7), reads=[wx, xn], writes=[ps], sig=(k == 7 and c == 3))
            kb.op("act", lambda a: a.copy(out=raw[:, cc4 * 4:(cc4 + 1) * 4, 3:131],
                                          in_=ps[:, :].rearrange("p (c t) -> p c t", c=4)), reads=[ps], writes=[raw])

        load_h(0)
        for i in range(NCH):
            if i + 1 < NCH:
                load_h(i + 1)
            h = hb[i % 2]
            emit_norm(kb, C, nb, h, lnw, xn, 128)
            for k in range(8):
                kb.mm(sm[:, 0:32], xn[:, k, :], wdt[:, k, :], start=(k == 0), stop=(k == 7), reads=[xn, wdt], writes=[sm])
            kb.op("dve", lambda v: v.tensor_tensor(out=dtt[:, :], in0=sm[:, 0:32], in1=dtb[:, :], op=ALU.add),
                  reads=[sm, dtb], writes=[dtt])
            kb.op("act", lambda a: a.activation(out=dtt[:, :], in_=dtt[:, :], func=AF.Exp), reads=[dtt], writes=[dtt])
            kb.op("act", lambda a: a.activation(out=dtt[:, :], in_=dtt[:, :], func=AF.Ln, bias=one[:, :]),
                  reads=[dtt, one], writes=[dtt])
            kb.op("dve", lambda v: v.tensor_tensor(out=ad[:, :], in0=dtt[:, :], in1=aneg[:, :], op=ALU.mult),
                  reads=[dtt, aneg], writes=[ad])
            kb.mm(sm[:, 32:64], triU[:, :], ad[:, :], start=True, stop=True, reads=[triU, ad], writes=[sm])
            kb.mm(sm[:, 64:96], C["ones_f"][:, :], ad[:, :], start=True, stop=True, reads=[C["ones_f"], ad], writes=[sm])
            kb.op("dve", lambda v: v.tensor_copy(out=acum[:, :], in_=sm[:, 32:64]), reads=[sm], writes=[acum])
            kb.op("act", lambda a: a.activation(out=eacum[:, :], in_=sm[:, 32:64], func=AF.Exp), reads=[sm], writes=[eacum])
            kb.op("act", lambda a: a.activation(out=cdec[:, :], in_=sm[:, 64:96], func=AF.Exp), reads=[sm], writes=[cdec])
            kb.op("dve", lambda v: v.tensor_tensor(out=dst[:, :], in0=sm[:, 64:96], in1=acum[:, :], op=ALU.subtract),
                  reads=[sm, acum], writes=[dst])
            kb.op("act", lambda a: a.activation(out=dst[:, :], in_=dst[:, :], func=AF.Exp), reads=[dst], writes=[dst])
            kb.op("dve", lambda v: v.tensor_tensor(out=dst[:, :], in0=dst[:, :], in1=dtt[:, :], op=ALU.mult),
                  reads=[dst, dtt], writes=[dst])
            for cc4 in range(8):
                proj_slice(cc4)
            for cc4 in range(8):
                ps = pj[cc4 % 2]
                for c in range(4):
                    cc = cc4 * 4 + c
                    for k in range(4):
                        kb.mm(ps[:, c * 128:(c + 1) * 128], diag[:, k * 32 + cc, :], raw[:, cc, k:k + 128],
                              start=(k == 0), stop=(k == 3), reads=[diag, raw], writes=[ps], sig=(k == 3 and c == 3))
                for c in range(4):
                    cc = cc4 * 4 + c
                    kb.op("act", lambda a: a.activation(out=xbcT[:, cc, :], in_=ps[:, c * 128:(c + 1) * 128], func=AF.Silu,
                                                        bias=cbT[:, cc:cc + 1]), reads=[ps, cbT], writes=[xbcT])
            kb.op("pool", lambda g_: g_.tensor_copy(out=raw[:, :, 0:3], in_=raw[:, :, 128:131]), reads=[raw], writes=[raw])
            for rd in range(3):
                for j in range(8):
                    cc = rd * 8 + j
                    kb.tr(tp[:, j * 128:(j + 1) * 128], xbcT[:, cc, :], C["ident_b"][:, :], reads=[xbcT, C["ident_b"]],
                          writes=[tp], sig=(j == 7))
                if rd < 2:
                    kb.op("act", lambda a: a.copy(out=xs_tm[:, rd * 16:(rd + 1) * 16, :].rearrange("p h d -> p (h d)"),
                                                  in_=tp[:, :]), reads=[tp], writes=[xs_tm])
                else:
                    kb.op("act", lambda a: a.copy(out=B_tm[:, :, :].rearrange("p g n -> p (g n)"), in_=tp[:, :]),
                          reads=[tp], writes=[B_tm])
            dtb_ = dtt[:, :].unsqueeze(2).broadcast_to([128, 32, 64])
            dst_ = dst[:, :].unsqueeze(2).broadcast_to([128, 32, 64])
            kb.op("dve", lambda v: v.tensor_tensor(out=xd[:, :, :], in0=xs_tm[:, :, :], in1=dtb_, op=ALU.mult),
                  reads=[xs_tm, dtt], writes=[xd])
            kb.op("dve", lambda v: v.tensor_tensor(out=xdd[:, 0:16, :], in0=xs_tm[:, 0:16, :],
                                                   in1=dst[:, 0:16].unsqueeze(2).broadcast_to([128, 16, 64]), op=ALU.mult),
                  reads=[xs_tm, dst], writes=[xdd])
            kb.op("pool", lambda g_: g_.tensor_tensor(out=xdd[:, 16:32, :], in0=xs_tm[:, 16:32, :],
                                                      in1=dst[:, 16:32].unsqueeze(2).broadcast_to([128, 16, 64]), op=ALU.mult),
                  reads=[xs_tm, dst], writes=[xdd])
            def fill_rm(hh):
                hsl = slice(hh * 16, (hh + 1) * 16)
                kb.op("dve", lambda v: v.tensor_tensor(out=Rm[:, :, :], in0=triU[:, :].unsqueeze(1).broadcast_to([128, 16, 128]),
                                                       in1=ad[:, hsl].unsqueeze(2).broadcast_to([128, 16, 128]), op=ALU.mult),
                      reads=[triU, ad], writes=[Rm])
            fill_rm(0)
            yp = ypre[0]
            def S1a(g):
                hs = slice(4 * g, 4 * g + 4)
                cbm, decT = cbm2[g % 2], decT2[g % 2]
                kb.mm(cbp[:, 0:128], xbcT[:, 16 + g, :], xbcT[:, 24 + g, :], start=True, stop=True, reads=[xbcT], writes=[cbp])
                kb.op("dve", lambda v: v.tensor_tensor(out=cbm[:, :], in0=cbp[:, 0:128], in1=triU[:, :], op=ALU.mult),
                      reads=[cbp, triU], writes=[cbm])
                kb.mm(df[:, :], strictL[:, :], Rm[:, (4 * g) % 16:(4 * g) % 16 + 4, :].rearrange("p r l -> p (r l)"), start=True, stop=True,
                      reads=[strictL, Rm], writes=[df])
                kb.op("act", lambda a: a.activation(out=decT[:, :, :].rearrange("p r l -> p (r l)"), in_=df[:, :], func=AF.Exp),
                      reads=[df], writes=[decT])

            def S1b(g):
                cbm, decT, MT = cbm2[g % 2], decT2[g % 2], MT2[g % 2]
                kb.op("dve", lambda v: v.tensor_tensor(out=MT[:, :, :], in0=decT[:, :, :],
                                                       in1=cbm[:, :].unsqueeze(1).broadcast_to([128, 4, 128]), op=ALU.mult),
                      reads=[decT, cbm], writes=[MT])

            def S2(g):
                hs = slice(4 * g, 4 * g + 4)
                MT = MT2[g % 2]
                for r in range(4):
                    kb.mm(yy[:, r * 64:(r + 1) * 64], MT[:, r, :], xd[:, 4 * g + r, :], start=True, stop=True,
                          reads=[MT, xd], writes=[yy], sig=False)
                kb.mm(yy[:, 256:512], xbcT[:, 24 + g, :], prevb[:, hs, :].rearrange("p h d -> p (h d)"), start=True, stop=True,
                      reads=[xbcT, prevb], writes=[yy])
                ea_ = eacum[:, hs].unsqueeze(2).broadcast_to([128, 4, 64])
                kb.op("dve", lambda v: v.tensor_tensor(out=t1[:, :, :], in0=yy[:, 256:512].rearrange("p (h d) -> p h d", d=64),
                                                       in1=ea_, op=ALU.mult), reads=[yy, eacum], writes=[t1])
                kb.op("dve", lambda v: v.tensor_tensor(out=t1[:, :, :], in0=t1[:, :, :],
                                                       in1=yy[:, 0:256].rearrange("p (h d) -> p h d", d=64), op=ALU.add),
                      reads=[yy, t1], writes=[t1])
                kb.op("dve", lambda v: v.tensor_tensor(out=t2[:, :, :], in0=xs_tm[:, hs, :],
                                                       in1=dsk[:, hs].unsqueeze(2).broadcast_to([128, 4, 64]), op=ALU.mult),
                      reads=[xs_tm, dsk], writes=[t2])
                kb.op("dve", lambda v: v.tensor_tensor(out=yp[:, hs, :], in0=t1[:, :, :], in1=t2[:, :, :], op=ALU.add),
                      reads=[t1, t2], writes=[yp])
                kb.mm(stp_[:, 0:256], B_tm[:, g, :], xdd[:, hs, :].rearrange("p h d -> p (h d)"), start=True, stop=True,
                      reads=[B_tm, xdd], writes=[stp_])
                kb.op("dve", lambda v: v.tensor_tensor(out=prev[:, hs, :], in0=prev[:, hs, :],
                                                       in1=cdec[:, hs].unsqueeze(2).broadcast_to([128, 4, 64]), op=ALU.mult),
                      reads=[prev, cdec], writes=[prev])
                kb.op("dve", lambda v: v.tensor_tensor(out=prev[:, hs, :], in0=prev[:, hs, :],
                                                       in1=stp_[:, 0:256].rearrange("p (h d) -> p h d", d=64), op=ALU.add),
                      reads=[prev, stp_], writes=[prev])
                kb.op("act", lambda a: a.copy(out=prevb[:, hs, :], in_=prev[:, hs, :]), reads=[prev], writes=[prevb])

            S1a(0)
            S1b(0)
            for g in range(8):
                if g == 3:
                    fill_rm(1)
                if g + 1 < 8:
                    S1a(g + 1)
                S2(g)
                if g + 1 < 8:
                    S1b(g + 1)
            kb.dma("sp", ydram[i * 128:(i + 1) * 128, :], yp[:, :, :].rearrange("p h d -> p (h d)"), reads=[yp])
    with kb.phase("mamba_b"):
        lnw = load_vec_pk(kb, "lnw", P.inp("ln_mix")[layer], 8)
        wz = kb.sb("wz", [128, 8, 2048], BF16)
        for k in range(8):
            kb.dma("pool", wz[:, k, :], w_in[k * 128:(k + 1) * 128, 0:2048], writes=[wz])
        wout = load_w_bf16(kb, "wout", P.inp("ssm_w_out")[inst], 16, D)
        nw = load_bcast(kb, "nw", P.inp("ssm_norm_w")[inst], 2048)
        pj = [kb.ps("pj", [128, 512], F32) for _ in range(2)]
        tp = kb.ps("tp", [128, 1024], BF16)
        nb = NormBufs(kb, 128)
        hb = [kb.sb("h", [128, 8, 128], F32) for _ in range(3)]
        yb = [kb.sb("yb", [128, 2048], F32) for _ in range(2)]
        xn = kb.sb("xn", [128, 8, 128], BF16)
        zs = kb.sb("zs", [128, 2048], F32)
        sq = kb.sb("sq2", [128, 2048], F32)
        gss = kb.sb("gss", [128, 8], F32)
        gnb2 = [kb.sb("gnb", [128, 2048], BF16) for _ in range(2)]
        oT = kb.sb("oT", [128, 16, 128], BF16)
        hout = [kb.sb("hout", [128, 8, 128], F32) for _ in range(2)]

        xn2 = [xn, kb.sb("xnb", [128, 8, 128], BF16)]
        zs2 = [zs, kb.sb("zsb", [128, 2048], F32)]

        def load_hh(i):
            kb.dma("sp", hb[i % 3][:, :, :], hview(hT, i * 128, 128), writes=[hb[i % 3]])

        def load_y(i):
            kb.dma("sp", yb[i % 2][:, :], ydram[i * 128:(i + 1) * 128, :], writes=[yb[i % 2]])

        def front(i):
            h = hb[i % 3]
            xn_ = xn2[i % 2]
            zs_ = zs2[i % 2]
            emit_norm(kb, C, nb, h, lnw, xn_, 128)
            for ct in range(4):
                ps = pj[ct % 2]
                for k in range(8):
                    kb.mm(ps[:, :], xn_[:, k, :], wz[:, k, ct * 512:(ct + 1) * 512], start=(k == 0), stop=(k == 7),
                          reads=[xn_, wz], writes=[ps])
                kb.op("act", lambda a: a.activation(out=zs_[:, ct * 512:(ct + 1) * 512], in_=ps[:, :], func=AF.Silu),
                      reads=[ps], writes=[zs_])

        def mid(i):
            y = yb[i % 2]
            zs_ = zs2[i % 2]
            gnb = gnb2[i % 2]
            kb.op("dve", lambda v: v.tensor_tensor(out=zs_[:, :], in0=zs_[:, :], in1=y[:, :], op=ALU.mult),
                  reads=[zs_, y], writes=[zs_])
            kb.op("act", lambda a: a.activation(out=sq[:, :], in_=zs_[:, :], func=AF.Square), reads=[zs_], writes=[sq])
            kb.op("dve", lambda v: v.tensor_reduce(out=gss[:, :], in_=sq[:, :].rearrange("p (g c) -> p g c", g=8),
                                                   axis=AX.X, op=ALU.add), reads=[sq], writes=[gss])
            kb.op("act", lambda a: a.activation(out=gss[:, :], in_=gss[:, :], func=AF.Sqrt, scale=1.0 / 256,
                                                bias=C["eps"][:, :]), reads=[gss, C["eps"]], writes=[gss])
            kb.op("dve", lambda v: v.reciprocal(out=gss[:, :], in_=gss[:, :]), reads=[gss], writes=[gss])
            kb.op("dve", lambda v: v.tensor_tensor(out=sq[:, :].rearrange("p (g c) -> p g c", g=8),
                                                   in0=zs_[:, :].rearrange("p (g c) -> p g c", g=8),
                                                   in1=gss[:, :].unsqueeze(2).broadcast_to([128, 8, 256]), op=ALU.mult),
                  reads=[zs_, gss], writes=[sq])
            kb.op("dve", lambda v: v.tensor_tensor(out=gnb[:, 0:1024], in0=sq[:, 0:1024], in1=nw[:, 0:1024], op=ALU.mult),
                  reads=[sq, nw], writes=[gnb])
            kb.op("pool", lambda g_: g_.tensor_tensor(out=gnb[:, 1024:2048], in0=sq[:, 1024:2048], in1=nw[:, 1024:2048],
                                                      op=ALU.mult), reads=[sq, nw], writes=[gnb])

        def tail(i):
            h = hb[i % 3]
            gnb = gnb2[i % 2]
            for rd in range(2):
                for j in range(8):
                    k = rd * 8 + j
                    kb.tr(tp[:, j * 128:(j + 1) * 128], gnb[:, k * 128:(k + 1) * 128], C["ident_b"][:, :],
                          reads=[gnb, C["ident_b"]], writes=[tp], sig=(j == 7))
                kb.op("act", lambda a: a.copy(out=oT[:, rd * 8:(rd + 1) * 8, :], in_=tp[:, :].rearrange("p (k t) -> p k t", k=8)),
                      reads=[tp], writes=[oT])
            ho = hout[i % 2]
            for half in range(2):
                ps = pj[half]
                for dcl in range(4):
                    dc = half * 4 + dcl
                    for k in range(16):
                        kb.mm(ps[:, dcl * 128:(dcl + 1) * 128], wout[:, k, dc * 128:(dc + 1) * 128], oT[:, k, :],
                              start=(k == 0), stop=(k == 15), reads=[wout, oT], writes=[ps], sig=(k == 15 and dcl == 3))
                kb.op("dve", lambda v: v.tensor_tensor(out=ho[:, half * 4:(half + 1) * 4, :],
                                                       in0=ps[:, :].rearrange("p (c t) -> p c t", c=4),
                                                       in1=h[:, half * 4:(half + 1) * 4, :], op=ALU.add),
                      reads=[ps, h], writes=[ho])
            kb.dma("sp", hview(hT, i * 128, 128), ho[:, :, :], reads=[ho])

        load_hh(0)
        load_hh(1)
        load_y(0)
        front(0)
        for i in range(NCH + 1):
            if i + 1 < NCH:
                front(i + 1)
            if i < NCH:
                if i + 1 < NCH:
                    load_y(i + 1)
                mid(i)
            if i >= 1:
                tail(i - 1)
            if i + 2 < NCH:
                load_hh(i + 2)
DEPTH = 4
IN_SHAPES = {
    "x": ([S, D], "f"), "positions": ([S, 1], "i"),
    "ln_ffn1": ([4, D], "f"), "ffn1_w_in": ([4, D, 2 * DFF], "f"), "ffn1_w_out": ([4, DFF, D], "f"),
    "ln_mix": ([4, D], "f"), "ln_ffn2": ([4, D], "f"), "ffn2_w_in": ([4, D, 2 * DFF], "f"),
    "ffn2_w_out": ([4, DFF, D], "f"),
    "ssm_w_in": ([2, D, 6176], "f"), "ssm_conv_w": ([2, 4, 4096], "f"), "ssm_conv_b": ([2, 4096], "f"),
    "ssm_dt_bias": ([2, 32], "f"), "ssm_a_log": ([2, 32], "f"), "ssm_d": ([2, 32], "f"),
    "ssm_norm_w": ([2, 2048], "f"), "ssm_w_out": ([2, 2048, D], "f"),
    "swa_w_qkv": ([1, D, 1536], "f"), "swa_b_qkv": ([1, 1536], "f"), "swa_sinks": ([1, 16], "f"),
    "swa_w_o": ([1, D, D], "f"), "swa_b_o": ([1, D], "f"),
    "nsa_w_in": ([1, D, 2608], "f"), "nsa_pe_k": ([1, 32, 64], "f"), "nsa_k_w1": ([1, 2048, 256], "f"),
    "nsa_k_w2": ([1, 256, 64], "f"), "nsa_pe_v": ([1, 32, 64], "f"), "nsa_v_w1": ([1, 2048, 256], "f"),
    "nsa_v_w2": ([1, 256, 64], "f"), "nsa_w_o": ([1, D, D], "f"), "final_norm": ([D], "f"),
}


def host_consts():
    c = {}
    c["c_ident"] = np.eye(128, dtype=np.float32)
    attn_host_consts(c)
    mamba_host_consts(c)
    return c


class Prog:
    def __init__(self):
        self.nc = bass.Bass("TRN2", target_bir_lowering=False)
        self.ins = {}
        self.consts = host_consts()

    def inp(self, name):
        if name not in self.ins:
            if name in IN_SHAPES:
                shp, k = IN_SHAPES[name]
                dt = F32 if k == "f" else I32
            else:
                shp = list(self.consts[name].shape)
                dt = F32
            self.ins[name] = self.nc.dram_tensor(name, shp, dt, kind="ExternalInput").ap()
        return self.ins[name]


def build_program(plan=None):
    P = Prog()
    nc = P.nc
    if plan is None:
        plan = ["in"]
        for i in range(DEPTH):
            plan += [("ffn1", i), ("mix", i), ("ffn2", i)]
        plan += ["final"]
    out = nc.dram_tensor("out", [S, D], F32, kind="ExternalOutput").ap()
    hT = nc.dram_tensor("hT", [D, S], F32, kind="Internal").ap()
    kb = KB(nc)
    cin = {"c_ident": P.inp("c_ident")}
    C = setup_consts(kb, cin)
    for st in plan:
        if st == "in":
            ph_in_transpose(kb, C, P.inp("x"), hT)
        elif st == "final":
            ph_final(kb, C, hT, P.inp("final_norm"), out)
        elif st[0] in ("ffn1", "ffn2"):
            i = st[1]
            ph_ffn(kb, C, hT, P.inp("ln_" + st[0])[i], P.inp(st[0] + "_w_in")[i], P.inp(st[0] + "_w_out")[i])
        elif st[0] == "mix":
            i = st[1]
            kind, inst = i % 3, i // 3
            if kind == 0:
                ph_mamba(kb, C, P, hT, i, inst)
            elif kind == 1:
                ph_swa(kb, C, P, hT, i, inst)
            else:
                ph_nsa(kb, C, P, hT, i, inst)
    kb.barrier()
    kb.close()
    P.kb = kb
    return P


def make_in_maps(P, inputs, ncores=8):
    maps = []
    for c in range(ncores):
        m = {}
        for name in P.ins:
            if name in P.consts:
                m[name] = P.consts[name]
            elif name == "x":
                m[name] = np.ascontiguousarray(np.asarray(inputs["x"])[c])
            elif name == "positions":
                m[name] = np.ascontiguousarray(np.asarray(inputs["positions"])[c].reshape(S, 1)).astype(np.int32)
            else:
                m[name] = np.ascontiguousarray(np.asarray(inputs[name]))
        maps.append(m)
    return maps


_CACHE = {}


def kernel(**inputs):
    if "P" not in _CACHE:
        _CACHE["P"] = build_program()
    P = _CACHE["P"]
    maps = make_in_maps(P, inputs)
    res = run_bass_kernel_spmd(P.nc, maps, core_ids=list(range(8)))
    return np.stack([np.asarray(res.results[c]["out"]) for c in range(8)], axis=0).astype(np.float32)
```
